# Optimizing a Trainium2 kernel written in Bass

```python
import math
import jax, jax.numpy as jnp
from jax import lax
import numpy as np


D_MODEL = 1024
BATCH = 16
SEQ = 4096
DEPTH = 1

HEAD_DIM = 64
A_HEADS = 4
A_VDIM = 2 * HEAD_DIM
B_HEADS = 8
B_KV_HEADS = 2
B_GROUP = B_HEADS // B_KV_HEADS
IDX_HEADS = 8
IDX_DIM = 64
TOPK_MAX = 256
N_BUCKETS = 32
MAX_DISTANCE = 128
D_FF = 2816
CONV_W = 3
Q_BLOCK = 128
EPS = 1e-6
MIX_WIDTH = A_HEADS * A_VDIM + B_HEADS * HEAD_DIM
IN_SIZES = (A_HEADS * 2 * HEAD_DIM, A_HEADS * 2 * HEAD_DIM, A_HEADS * A_VDIM,
            B_HEADS * HEAD_DIM, B_KV_HEADS * HEAD_DIM, B_KV_HEADS * HEAD_DIM,
            IDX_HEADS * IDX_DIM, IDX_DIM, IDX_HEADS)
IN_COLS = sum(IN_SIZES)

kernel_name = 'hybrid_diffattn_dsa_convffn_block'


def rms_norm(x, g):
    xf = x.astype(jnp.float32)
    y = xf * lax.rsqrt(jnp.mean(xf * xf, axis=-1, keepdims=True) + EPS)
    return (y * g.astype(jnp.float32)).astype(x.dtype)


def t5_bucket(rel):
    n = jnp.maximum(rel, 0)
    max_exact = N_BUCKETS // 2
    nf = jnp.maximum(n, 1).astype(jnp.float32)
    large = max_exact + (jnp.log(nf / max_exact) / math.log(MAX_DISTANCE / max_exact)
                         * (N_BUCKETS - max_exact)).astype(jnp.int32)
    large = jnp.minimum(large, N_BUCKETS - 1)
    return jnp.where(n < max_exact, n, large)


def diff_attention(q, k, v, lam, lam_init, subln_g, bias_a):
    B_, S_, H, _, Dh = q.shape
    E = v.shape[-1]
    nb = S_ // Q_BLOCK
    kpos = jnp.arange(S_)
    scale = Dh ** -0.5

    def block(i):
        start = i * Q_BLOCK
        qb = lax.dynamic_slice_in_dim(q, start, Q_BLOCK, axis=1)
        qpos = start + jnp.arange(Q_BLOCK)
        rel = qpos[:, None] - kpos[None, :]
        bias = jnp.transpose(bias_a[t5_bucket(rel)], (2, 0, 1))
        logits = jnp.einsum('bthcd,bshcd->bhcts', qb, k).astype(jnp.float32) * scale
        logits = logits + bias[None, :, None].astype(jnp.float32)
        logits = jnp.where((rel >= 0)[None, None, None], logits, -jnp.inf)
        p = jax.nn.softmax(logits, axis=-1)
        attn = p[:, :, 0] - lam * p[:, :, 1]
        return jnp.einsum('bhts,bshe->bthe', attn.astype(v.dtype), v)

    out = lax.map(block, jnp.arange(nb))
    out = jnp.moveaxis(out, 0, 1).reshape(B_, S_, H, E)
    out = rms_norm(out, subln_g) * (1.0 - lam_init)
    return out.reshape(B_, S_, H * E)


def dsa_attention(q, k, v, iq, ik, iw, bias_b):
    B_, S_, HB, Dh = q.shape
    G = k.shape[2]
    R = HB // G
    topk = min(TOPK_MAX, S_ // 4)
    nb = S_ // Q_BLOCK
    kpos = jnp.arange(S_)
    scale = Dh ** -0.5

    def block(i):
        start = i * Q_BLOCK
        qb = lax.dynamic_slice_in_dim(q, start, Q_BLOCK, axis=1)
        iqb = lax.dynamic_slice_in_dim(iq, start, Q_BLOCK, axis=1)
        iwb = lax.dynamic_slice_in_dim(iw, start, Q_BLOCK, axis=1)
        qpos = start + jnp.arange(Q_BLOCK)
        causal = qpos[:, None] >= kpos[None, :]
        s_h = jax.nn.relu(jnp.einsum('bthd,bsd->bths', iqb, ik).astype(jnp.float32))
        score = jnp.einsum('bth,bths->bts', iwb.astype(jnp.float32), s_h)
        score = jnp.where(causal[None], score, -jnp.inf)
        _, idx = lax.top_k(score, topk)
        valid = idx <= qpos[None, :, None]
        kg = jax.vmap(lambda a, ix: a[ix])(k, idx)
        vg = jax.vmap(lambda a, ix: a[ix])(v, idx)
        qg = qb.reshape(B_, Q_BLOCK, G, R, Dh)
        logits = jnp.einsum('btgrd,btkgd->btgrk', qg, kg).astype(jnp.float32) * scale
        bias = bias_b[t5_bucket(qpos[None, :, None] - idx)]
        bias = jnp.transpose(bias.reshape(B_, Q_BLOCK, topk, G, R), (0, 1, 3, 4, 2))
        logits = jnp.where(valid[:, :, None, None, :], logits + bias.astype(jnp.float32), -jnp.inf)
        p = jax.nn.softmax(logits, axis=-1)
        o = jnp.einsum('btgrk,btkgd->btgrd', p.astype(vg.dtype), vg)
        return o.reshape(B_, Q_BLOCK, HB * Dh)

    out = lax.map(block, jnp.arange(nb))
    return jnp.moveaxis(out, 0, 1).reshape(B_, S_, HB * Dh)


def causal_dwconv(u, w, b):
    S_ = u.shape[1]
    up = jnp.pad(u, ((0, 0), (CONV_W - 1, 0), (0, 0)))
    y = b
    for j in range(CONV_W):
        y = y + up[:, j:j + S_] * w[j]
    return y


def setup_inputs(seed: int = 0) -> dict:
    key = jax.random.key(seed)
    ks = jax.random.split(key, 24)
    f32 = jnp.float32

    def nrm(k, shape, s):
        return jax.random.normal(k, shape, f32) * s

    L = DEPTH
    return {
        'x': nrm(ks[0], (BATCH, SEQ, D_MODEL), 1.0),
        'c': nrm(ks[1], (BATCH, D_MODEL), 1.0),
        'w_ada': nrm(ks[2], (L, D_MODEL, 6 * D_MODEL), 0.5 * D_MODEL ** -0.5),
        'b_ada': nrm(ks[3], (L, 6 * D_MODEL), 0.01),
        'g_attn': 1.0 + nrm(ks[4], (L, D_MODEL), 0.01),
        'w_in': nrm(ks[5], (L, D_MODEL, IN_COLS), D_MODEL ** -0.5),
        'q_norm_a': 1.0 + nrm(ks[6], (L, HEAD_DIM), 0.01),
        'k_norm_a': 1.0 + nrm(ks[7], (L, HEAD_DIM), 0.01),
        'q_norm_b': 1.0 + nrm(ks[8], (L, HEAD_DIM), 0.01),
        'k_norm_b': 1.0 + nrm(ks[9], (L, HEAD_DIM), 0.01),
        'lam_vecs': nrm(ks[10], (L, 4, HEAD_DIM), 0.1),
        'subln_a': 1.0 + nrm(ks[11], (L, A_VDIM), 0.01),
        'w_out': nrm(ks[12], (L, MIX_WIDTH, D_MODEL), MIX_WIDTH ** -0.5),
        'g_ffn': 1.0 + nrm(ks[13], (L, D_MODEL), 0.01),
        'w_up': nrm(ks[14], (L, D_MODEL, 2 * D_FF), D_MODEL ** -0.5),
        'conv_w': nrm(ks[15], (L, CONV_W, 2 * D_FF), CONV_W ** -0.5),
        'conv_b': nrm(ks[16], (L, 2 * D_FF), 0.01),
        'w_down': nrm(ks[17], (L, D_FF, D_MODEL), D_FF ** -0.5),
        'rel_bias': nrm(ks[18], (N_BUCKETS, A_HEADS + B_HEADS), 0.5),
    }


def reference(x, c, w_ada, b_ada, g_attn, w_in, q_norm_a, k_norm_a, q_norm_b, k_norm_b,
              lam_vecs, subln_a, w_out, g_ffn, w_up, conv_w, conv_b, w_down, rel_bias):
    B_, S_, _ = x.shape
    split_points = [int(v) for v in np.cumsum(IN_SIZES)[:-1]]
    bias_a = rel_bias[:, :A_HEADS]
    bias_b = rel_bias[:, A_HEADS:]
    for l in range(DEPTH):
        lam_init = 0.8 - 0.6 * math.exp(-0.3 * l)
        mod = jax.nn.silu(c) @ w_ada[l] + b_ada[l]
        sh_a, sc_a, gt_a, sh_f, sc_f, gt_f = jnp.split(mod, 6, axis=-1)

        h = rms_norm(x, g_attn[l]) * (1.0 + sc_a[:, None]) + sh_a[:, None]
        proj = h @ w_in[l]
        aq, ak, av, bq, bk, bv, iq, ik, iw = jnp.split(proj, split_points, axis=-1)
        aq = rms_norm(aq.reshape(B_, S_, A_HEADS, 2, HEAD_DIM), q_norm_a[l])
        ak = rms_norm(ak.reshape(B_, S_, A_HEADS, 2, HEAD_DIM), k_norm_a[l])
        av = av.reshape(B_, S_, A_HEADS, A_VDIM)
        bq = rms_norm(bq.reshape(B_, S_, B_HEADS, HEAD_DIM), q_norm_b[l])
        bk = rms_norm(bk.reshape(B_, S_, B_KV_HEADS, HEAD_DIM), k_norm_b[l])
        bv = bv.reshape(B_, S_, B_KV_HEADS, HEAD_DIM)
        iq = iq.reshape(B_, S_, IDX_HEADS, IDX_DIM)
        iw = iw * (IDX_HEADS * IDX_DIM) ** -0.5

        lv = lam_vecs[l].astype(jnp.float32)
        lam = jnp.exp(jnp.sum(lv[0] * lv[1])) - jnp.exp(jnp.sum(lv[2] * lv[3])) + lam_init

        o_a = diff_attention(aq, ak, av, lam, lam_init, subln_a[l], bias_a)
        o_b = dsa_attention(bq, bk, bv, iq, ik, iw, bias_b)
        mixed = jnp.concatenate([o_a, o_b], axis=-1) @ w_out[l]
        x = x + gt_a[:, None] * mixed

        h = rms_norm(x, g_ffn[l]) * (1.0 + sc_f[:, None]) + sh_f[:, None]
        u = causal_dwconv(h @ w_up[l], conv_w[l], conv_b[l])
        u_gate, u_val = jnp.split(u, 2, axis=-1)
        y = (jax.nn.silu(u_gate) * u_val) @ w_down[l]
        x = x + gt_f[:, None] * y
    return x
```

```python
from contextlib import ExitStack
import math
import numpy as np
import concourse.bass as bass
import concourse.mybir as mybir
from concourse.bass_utils import run_bass_kernel_spmd

F32 = mybir.dt.float32
BF16 = mybir.dt.bfloat16
AF = mybir.ActivationFunctionType
ALU = mybir.AluOpType

D = 1024
DFF = 2816
INC = 2888
NEG = -30000.0
EPS = 1e-6
LAM_INIT = 0.8 - 0.6
TOPK = 256
NIT = 13


class Sched:
    def __init__(self, nc):
        self.nc = nc
        self.engs = {'pe': nc.tensor, 'act': nc.scalar, 'dve': nc.vector, 'pool': nc.gpsimd, 'sp': nc.sync}
        self.sems, self.cnt, self.mult = {}, {}, {}
        self.seen = {e: {} for e in self.engs}
        self.lastw, self.readers = {}, {}
        for e in ['pe', 'act', 'dve', 'pool']:
            self._chan(e, 1)

    def _chan(self, name, mult):
        if name not in self.sems:
            self.sems[name] = self.nc.alloc_semaphore("s%d" % len(self.sems))
            self.cnt[name] = 0
            self.mult[name] = mult

    def op(self, eng, fn, reads=(), writes=(), chan=None):
        deps = {}

        def add(c, n):
            if deps.get(c, 0) < n:
                deps[c] = n
        for r in reads:
            for c, n in self.lastw.get(r, {}).items():
                add(c, n)
        for w in writes:
            for c, n in self.lastw.get(w, {}).items():
                add(c, n)
            for c, n in self.readers.get(w, {}).items():
                add(c, n)
        E = self.engs[eng]
        seen = self.seen[eng]
        need = []
        for c, n in deps.items():
            if eng == 'pe' and c == 'pe':
                continue
            if seen.get(c, 0) < n:
                need.append((c, n))
                seen[c] = n
        for c, n in need[1:]:
            E.wait_ge(self.sems[c], n * self.mult[c])
        r = fn(E)
        if isinstance(r, (tuple, list)):
            first, last = r[0], r[-1]
        else:
            first = last = r
        if need:
            c, n = need[0]
            first._wait_ge(self.sems[c], n * self.mult[c])
        ch = chan or eng
        if ch not in self.sems:
            self._chan(ch, 16)
        self.cnt[ch] += 1
        last.then_inc(self.sems[ch], self.mult[ch])
        me = (ch, self.cnt[ch])
        for w in writes:
            self.lastw[w] = {me[0]: me[1]}
            self.readers[w] = {}
        for r_ in reads:
            self.readers.setdefault(r_, {})[me[0]] = me[1]

    def dma(self, q, out, in_, reads=(), writes=(), chan=None, **kw):
        assert chan is not None
        self.op(q, lambda E: E.dma_start(out=out, in_=in_, **kw), reads, writes, chan=chan)

    def barrier(self, engines=None):
        for e in (engines or self.engs):
            E = self.engs[e]
            for c, n in self.cnt.items():
                if n > 0 and self.seen[e].get(c, 0) < n:
                    E.wait_ge(self.sems[c], n * self.mult[c])
                    self.seen[e][c] = n
        if engines is None:
            self.lastw, self.readers = {}, {}


class Alloc:
    def __init__(self, nc):
        self.nc = nc
        self.n = 0

    def sb(self, es, shape, dt, name="t"):
        self.n += 1
        return es.enter_context(self.nc.sbuf_tensor("%s_%d" % (name, self.n), list(shape), dt))

    def ps(self, es, shape, dt=F32, name="p"):
        self.n += 1
        return es.enter_context(self.nc.psum_tensor("%s_%d" % (name, self.n), list(shape), dt))


class Blk:
    __slots__ = ('pre', 's0', 's1', 's2', 'post')

    def __init__(self, s0, s1, s2, pre=None, post=None):
        self.pre, self.s0, self.s1, self.s2, self.post = pre, s0, s1, s2, post


def run_pipe(blocks, depth=1):
    n = len(blocks)
    for i in range(min(depth, n)):
        if blocks[i].pre:
            blocks[i].pre()
        blocks[i].s0()
    for i in range(n):
        if i + depth < n:
            if blocks[i + depth].pre:
                blocks[i + depth].pre()
            blocks[i + depth].s0()
        blocks[i].s1()
        blocks[i].s2()
        if blocks[i].post:
            blocks[i].post()


def build(S, NB, debug=False):
    NT = S // 128
    NCH = S // 512
    nc = bass.Bass("TRN2", target_bir_lowering=False)
    sch = Sched(nc)
    al = Alloc(nc)

    def din(name, shape, dt=F32):
        return nc.dram_tensor(name, list(shape), dt, kind="ExternalInput").ap()

    def dscr(name, shape, dt):
        return nc.dram_tensor(name, list(shape), dt, kind="ExternalOutput" if debug else "Internal").ap()

    x = din("x", [NB, S, D])
    cT = din("cT", [128, 8, NB])
    w_ada = din("w_ada", [D, 6 * D])
    b_ada = din("b_ada", [1, 6 * D])
    g_attn_c = din("g_attn_c", [128, 8])
    g_ffn_c = din("g_ffn_c", [128, 8])
    w_in = din("w_in", [D, INC])
    qkg = din("qkg", [128, 4])
    lamv = din("lamv", [1, 256])
    subln = din("subln", [1, 128])
    w_out = din("w_out", [D, D])
    w_up = din("w_up", [D, 2 * DFF])
    conv_wc = din("conv_wc", [128, 44, 3])
    conv_bc = din("conv_bc", [128, 44])
    w_down = din("w_down", [DFF, D])
    biasT = din("biasT", [128, 12, 256])
    b31 = din("b31", [1, 12])
    cmask = din("cmask", [128, 256])
    identf = din("identf", [128, 128])
    blk1 = din("blk1", [128, 128])
    out = nc.dram_tensor("out", [NB, S, D], F32, kind="ExternalOutput").ap()

    modrow = dscr("modrow", [NB, 6 * D], F32)
    QaT = dscr("QaT", [4, 128, S], BF16)
    KaT = dscr("KaT", [4, 128, S], BF16)
    Va = dscr("Va", [S, 512], BF16)
    QbT = dscr("QbT", [4, 128, S], BF16)
    KbT = dscr("KbT", [128, S], BF16)
    Vb = dscr("Vb", [S, 128], BF16)
    iqT = dscr("iqT", [4, 128, S], BF16)
    ikT = dscr("ikT", [128, S], BF16)
    iwS = dscr("iwS", [S, 8], F32)
    oS = dscr("oS", [S, D], BF16)
    x1nT = dscr("x1nT", [8, 128, S], BF16)

    with ExitStack() as g:
        identb = al.sb(g, [128, 128], BF16, "identb")
        identf_sb = al.sb(g, [128, 128], F32, "identf")
        blk1b = al.sb(g, [128, 128], BF16, "blk1b")
        cmask_sb = al.sb(g, [128, 256], F32, "cmask")
        negtri = al.sb(g, [128, 128], F32, "negtri")
        b31c = al.sb(g, [128, 12], F32, "b31c")
        DThi = al.sb(g, [128, 12, 256], BF16, "DThi")
        qkg_sb = al.sb(g, [128, 4], F32, "qkg")
        neg_lam = al.sb(g, [128, 1], F32, "neglam")
        subln_row = al.sb(g, [128, 128], F32, "sublnrow")
        gattn_sb = al.sb(g, [128, 8], F32, "gattn")
        gffn_sb = al.sb(g, [128, 8], F32, "gffn")
        cw_sb = al.sb(g, [128, 44, 3], F32, "cw")
        cb_sb = al.sb(g, [128, 44], F32, "cb")
        thr_const = al.sb(g, [128, 1], F32, "thrc")
        eps_c = al.sb(g, [128, 1], F32, "epsc")
        fvec = al.sb(g, [128, NIT], F32, "fvec")

        with ExitStack() as es:
            tmpf = al.sb(es, [128, 128], F32, "tmpf")
            bT = al.sb(es, [128, 12, 256], F32, "bT")
            lv = al.sb(es, [128, 256], F32, "lv")
            lsum = al.sb(es, [128, 2], F32, "lsum")
            junk = al.sb(es, [128, 64], F32, "junk")
            sch.dma('sp', identf_sb[:], identf[:, :], writes=['identf'], chan='c0')
            sch.dma('sp', tmpf[:], blk1[:, :], writes=['tmpf'], chan='c1')
            sch.dma('sp', cmask_sb[:], cmask[:, :], writes=['cmask'], chan='c2')
            sch.dma('sp', b31c[:], b31[0:1, :].partition_broadcast(128), writes=['b31c'], chan='c3')
            sch.dma('sp', bT[:], biasT[:, :, :], writes=['bT'], chan='c4')
            sch.dma('sp', qkg_sb[:], qkg[:, :], writes=['qkg'], chan='c5')
            sch.dma('sp', lv[:], lamv[0:1, :].partition_broadcast(128), writes=['lv'], chan='c6')
            sch.dma('sp', subln_row[:], subln[0:1, :].partition_broadcast(128), writes=['subln'], chan='c7')
            sch.dma('sp', gattn_sb[:], g_attn_c[:, :], writes=['gattn'], chan='c8')
            sch.dma('sp', gffn_sb[:], g_ffn_c[:, :], writes=['gffn'], chan='c9')
            sch.dma('sp', cw_sb[:], conv_wc[:, :, :], writes=['cw'], chan='c10')
            sch.dma('sp', cb_sb[:], conv_bc[:, :], writes=['cb'], chan='c11')
            sch.op('dve', lambda E: E.tensor_copy(out=identb[:], in_=identf_sb[:]), ['identf'], ['identb'])
            sch.op('dve', lambda E: E.tensor_copy(out=blk1b[:], in_=tmpf[:]), ['tmpf'], ['blk1b'])
            sch.op('dve', lambda E: E.tensor_scalar(out=qkg_sb[:, 0:1], in0=qkg_sb[:, 0:1], scalar1=0.125, scalar2=None,
                                                    op0=ALU.mult), ['qkg'], ['qkg'])
            sch.op('dve', lambda E: E.tensor_scalar(out=qkg_sb[:, 2:3], in0=qkg_sb[:, 2:3], scalar1=0.125, scalar2=None,
                                                    op0=ALU.mult), ['qkg'], ['qkg'])
            sch.op('dve', lambda E: E.memset(thr_const[:], -1e29), [], ['thrc'])
            sch.op('dve', lambda E: E.memset(eps_c[:], EPS), [], ['epsc'])
            for k in range(NIT):
                sch.op('dve', lambda E, k=k: E.memset(fvec[:, k:k + 1], 0.5 ** (k + 1)), [], ['fvec'])
            for h in range(12):
                sch.op('dve', lambda E, h=h: E.scalar_tensor_tensor(
                    out=bT[:, h, :], in0=bT[:, h, :], scalar=b31c[:, h:h + 1], in1=cmask_sb[:],
                    op0=ALU.subtract, op1=ALU.add), ['bT', 'b31c', 'cmask'], ['bT'])
            sch.op('dve', lambda E: E.tensor_copy(out=DThi[:], in_=bT[:]), ['bT'], ['DThi'])
            for i in range(2):
                sch.op('dve', lambda E, i=i: E.scalar_tensor_tensor(
                    out=junk[:], in0=lv[:, 128 * i:128 * i + 64], scalar=1.0, in1=lv[:, 128 * i + 64:128 * i + 128],
                    op0=ALU.mult, op1=ALU.mult, accum_out=lsum[:, i:i + 1]), ['lv'], ['junk', 'lsum'])
            sch.op('act', lambda E: E.activation(out=lsum[:], in_=lsum[:], func=AF.Exp), ['lsum'], ['lsum'])
            sch.op('dve', lambda E: E.tensor_tensor(out=neg_lam[:], in0=lsum[:, 1:2], in1=lsum[:, 0:1], op=ALU.subtract),
                   ['lsum'], ['neglam'])
            sch.op('dve', lambda E: E.tensor_scalar(out=neg_lam[:], in0=neg_lam[:], scalar1=-LAM_INIT, scalar2=None,
                                                    op0=ALU.add), ['neglam'], ['neglam'])
            sch.op('dve', lambda E: E.tensor_scalar(out=subln_row[:], in0=subln_row[:], scalar1=1.0 - LAM_INIT,
                                                    scalar2=None, op0=ALU.mult), ['subln'], ['subln'])
            with ExitStack() as e2:
                pt = al.ps(e2, [128, 128], F32, "ptri")
                sch.op('pe', lambda E: E.transpose(out=pt[:], in_=cmask_sb[:, 0:128], identity=identf_sb[:]),
                       ['cmask', 'identf'], ['ptri'])
                sch.op('dve', lambda E: E.tensor_scalar(out=negtri[:], in0=pt[:], scalar1=1e30 / 30000.0, scalar2=None,
                                                        op0=ALU.mult), ['ptri'], ['negtri'])
                sch.barrier()

        with ExitStack() as es:
            sc = al.sb(es, [128, 8, NB], F32, "sc")
            wb = [al.sb(es, [128, 8, 512], F32, "wada%d" % i) for i in range(2)]
            mrow = al.sb(es, [NB, 6 * D], F32, "mrow")
            brow = al.sb(es, [NB, 6 * D], F32, "brow")
            pm = [al.ps(es, [128, 512], F32, "pm%d" % i) for i in range(2)]
            sch.dma('sp', sc[:], cT[:, :, :], writes=['sc'], chan='c0')
            sch.dma('sp', brow[:], b_ada[0:1, :].partition_broadcast(NB), writes=['brow'], chan='c1')
            sch.op('act', lambda E: E.activation(out=sc[:], in_=sc[:], func=AF.Silu), ['sc'], ['sc'])
            for cc in range(12):
                wt = wb[cc % 2]
                sch.dma('sp', wt[:], w_ada[:, cc * 512:(cc + 1) * 512].rearrange("(k p) n -> p k n", p=128),
                        writes=['wada%d' % (cc % 2)], chan='wada%d' % (cc % 2))

                def mm(E, wt=wt, cc=cc):
                    r = []
                    for k in range(8):
                        r.append(E.matmul(pm[cc % 2][0:NB, :], lhsT=sc[:, k, :], rhs=wt[:, k, :],
                                          start=(k == 0), stop=(k == 7)))
                    return r
                sch.op('pe', mm, ['sc', 'wada%d' % (cc % 2)], ['pm%d' % (cc % 2)])
                sch.op('dve', lambda E, cc=cc: E.tensor_tensor(
                    out=mrow[:, cc * 512:(cc + 1) * 512], in0=pm[cc % 2][0:NB, :], in1=brow[:, cc * 512:(cc + 1) * 512],
                    op=ALU.add), ['pm%d' % (cc % 2), 'brow'], ['mrow'])
            sch.dma('sp', modrow[:, :], mrow[:], reads=['mrow'], writes=['modrow'], chan='c2')
            sch.barrier()

        for b in range(NB):
            with ExitStack() as eb:
                modc = al.sb(eb, [128, 48], F32, "modc")
                a_attn = al.sb(eb, [128, 8], F32, "aattn")
                a_ffn = al.sb(eb, [128, 8], F32, "affn")
                for i in range(6):
                    sch.dma('sp', modc[:, 8 * i:8 * i + 8],
                            modrow[b, 1024 * i:1024 * (i + 1)].rearrange("(t p) -> p t", p=128),
                            writes=['modc'], chan='modc', allow_slow_non_contiguous=True)
                sch.op('dve', lambda E: E.scalar_tensor_tensor(out=a_attn[:], in0=modc[:, 8:16], scalar=1.0,
                                                               in1=gattn_sb[:], op0=ALU.add, op1=ALU.mult),
                       ['modc'], ['aattn'])
                sch.op('dve', lambda E: E.scalar_tensor_tensor(out=a_ffn[:], in0=modc[:, 32:40], scalar=1.0,
                                                               in1=gffn_sb[:], op0=ALU.add, op1=ALU.mult),
                       ['modc'], ['affn'])
                sch.barrier()
                phase_proj(nc, sch, al, locals())
                phase_attn_a(nc, sch, al, locals())
                phase_attn_b(nc, sch, al, locals())
                phase_f(nc, sch, al, locals())
        sch.barrier()
    return nc


def phase_proj(nc, sch, al, L):
    S, NT, NCH, b = L['S'], L['NT'], L['NCH'], L['b']
    x, w_in = L['x'], L['w_in']
    identb, blk1b, qkg_sb = L['identb'], L['blk1b'], L['qkg_sb']
    a_attn, modc = L['a_attn'], L['modc']
    with ExitStack() as es:
        win = al.sb(es, [128, 8, INC], BF16, "win")
        with ExitStack() as e2:
            stg = [al.sb(e2, [128, INC], F32, "wstg%d" % i) for i in range(2)]
            for k in range(8):
                sch.dma('sp', stg[k % 2][:], w_in[k * 128:(k + 1) * 128, :], writes=['wstg%d' % (k % 2)],
                        chan='wstg%d' % (k % 2))
                eng = 'dve' if k % 2 == 0 else 'pool'
                sch.op(eng, lambda E, k=k: E.tensor_copy(out=win[:, k, :], in_=stg[k % 2][:]),
                       ['wstg%d' % (k % 2)], ['win'])
            sch.barrier()
        xt = [al.sb(es, [128, D], F32, "xt%d" % i) for i in range(2)]
        xn = [al.sb(es, [128, D], BF16, "xn%d" % i) for i in range(2)]
        junk = al.sb(es, [128, D], BF16, "junk")
        ss = al.sb(es, [128, 2], F32, "ss")
        hT = [al.sb(es, [128, 8, 512], BF16, "hT%d" % i) for i in range(2)]
        qsb = [al.sb(es, [128, 512], F32, "qsb%d" % i) for i in range(3)]
        sq = [al.sb(es, [128, 512], BF16, "sq%d" % i) for i in range(3)]
        lr = [al.sb(es, [128, 512], F32, "lr%d" % i) for i in range(3)]
        stF = [al.sb(es, [128, 512], BF16, "stF%d" % i) for i in range(3)]
        stV = [al.sb(es, [128, 512], BF16, "stV%d" % i) for i in range(2)]
        stB = [al.sb(es, [128, 128], BF16, "stB%d" % i) for i in range(2)]
        stW = [al.sb(es, [128, 8], F32, "stW%d" % i) for i in range(2)]
        tp = [al.ps(es, [128, 8, 128], BF16, "tp%d" % i) for i in range(2)]
        pp = [al.ps(es, [128, 512], F32, "pp%d" % i) for i in range(3)]
        pss = [al.ps(es, [128, 512], F32, "pss%d" % i) for i in range(2)]
        pB = al.ps(es, [128, 136], F32, "pB")

        fm = []
        for t in range(4):
            fm.append((L['QaT'], t, 128 * t, 128, 0))
        for t in range(4):
            fm.append((L['KaT'], t, 512 + 128 * t, 128, 1))
        for t in range(4):
            fm.append((L['QbT'], t, 1536 + 128 * t, 128, 2))
        fm.append((L['KbT'], None, 2048, 128, 3))
        for t in range(4):
            fm.append((L['iqT'], t, 2304 + 128 * t, 128, None))
        fm.append((L['ikT'], None, 2816, 64, None))

        nfm = 0
        ntok = 0
        pend = [None]
        npss = [0]

        def prep_tile(c, tl):
            h = hT[c % 2]
            hn = 'hT%d' % (c % 2)
            tt = c * 4 + tl
            r = tt % 2
            sch.dma('sp', xt[r][:], x[b, tt * 128:(tt + 1) * 128, :], writes=['xt%d' % r], chan='xt%d' % r)
            sch.op('dve', lambda E, r=r: E.scalar_tensor_tensor(
                out=junk[:], in0=xt[r][:], scalar=1.0, in1=xt[r][:], op0=ALU.mult, op1=ALU.mult,
                accum_out=ss[:, 0:1]), ['xt%d' % r], ['junk', 'ss'])
            sch.op('act', lambda E: E.activation(out=ss[:, 1:2], in_=ss[:, 0:1], func=AF.Ln, scale=1.0 / D,
                                                 bias=L['eps_c'][:]), ['ss'], ['ss1'])
            sch.op('act', lambda E: E.activation(out=ss[:, 1:2], in_=ss[:, 1:2], func=AF.Exp, scale=-0.5),
                   ['ss1'], ['ss1'])
            sch.op('dve', lambda E, r=r: E.tensor_scalar(out=xn[r][:], in0=xt[r][:], scalar1=ss[:, 1:2],
                                                         scalar2=None, op0=ALU.mult),
                   ['xt%d' % r, 'ss1'], ['xn%d' % r])

        def prep_tile_b(c, tl):
            h = hT[c % 2]
            hn = 'hT%d' % (c % 2)
            tt = c * 4 + tl
            r = tt % 2

            def tr(E, r=r):
                res = []
                for k in range(8):
                    res.append(E.transpose(out=tp[r][:, k, :], in_=xn[r][:, k * 128:(k + 1) * 128],
                                           identity=identb[:]))
                return res
            sch.op('pe', tr, ['xn%d' % r, 'identb'], ['tp%d' % r])

            def ev(E, r=r, tl=tl, h=h):
                res = []
                for k in range(8):
                    res.append(E.activation(out=h[:, k, tl * 128:(tl + 1) * 128], in_=tp[r][:, k, :],
                                            func=AF.Identity, scale=a_attn[:, k:k + 1], bias=modc[:, k:k + 1]))
                return res
            sch.op('act', ev, ['tp%d' % r, 'aattn', 'modc'], [hn])

        for tl in range(4):
            prep_tile(0, tl)
            prep_tile_b(0, tl)
        for c in range(NCH):
            h = hT[c % 2]
            hn = 'hT%d' % (c % 2)
            nfm_c = 0
            for (dst, t, c0, nr, gi) in fm:
                if c + 1 < NCH and nfm_c in (0, 4, 8, 12):
                    prep_tile(c + 1, nfm_c // 4)
                if c + 1 < NCH and nfm_c in (3, 7, 11, 15):
                    prep_tile_b(c + 1, (nfm_c - 3) // 4)
                nfm_c += 1
                pr = nfm % 3
                nfm += 1

                def mm(E, c0=c0, nr=nr, pr=pr, h=h):
                    res = []
                    for k in range(8):
                        res.append(E.matmul(pp[pr][0:nr, :], lhsT=win[:, k, c0:c0 + nr], rhs=h[:, k, :],
                                            start=(k == 0), stop=(k == 7)))
                    return res
                sch.op('pe', mm, ['win', hn], ['pp%d' % pr])
                st = stF[pr]
                sn = 'stF%d' % pr
                if t is None:
                    dap = dst[0:nr, c * 512:(c + 1) * 512]
                else:
                    dap = dst[t, 0:nr, c * 512:(c + 1) * 512]
                if gi is None:
                    sch.op('act', lambda E, nr=nr, pr=pr, st=st: E.activation(out=st[0:nr, :], in_=pp[pr][0:nr, :],
                                                                              func=AF.Copy),
                           ['pp%d' % pr], [sn])

                    def e2(dap=dap, st=st, sn=sn, nr=nr):
                        sch.dma('pool', dap, st[0:nr, :], reads=[sn], writes=[], chan='st_' + sn)
                else:
                    q = nfm % 3
                    sch.op('act', lambda E, pr=pr, q=q: E.activation(out=qsb[q][:], in_=pp[pr][:], func=AF.Copy),
                           ['pp%d' % pr], ['qsb%d' % q])
                    sch.op('dve', lambda E, q=q: E.tensor_tensor(out=sq[q][:], in0=qsb[q][:], in1=qsb[q][:],
                                                                  op=ALU.mult), ['qsb%d' % q], ['sq%d' % q])
                    ps_ = npss[0] % 2
                    npss[0] += 1
                    sch.op('pe', lambda E, q=q, ps_=ps_: E.matmul(pss[ps_][:], lhsT=blk1b[:], rhs=sq[q][:], start=True,
                                                                  stop=True),
                           ['sq%d' % q, 'blk1b'], ['pss%d' % ps_])

                    def e2(dap=dap, st=st, sn=sn, nr=nr, q=q, gi=gi, ps_=ps_):
                        sch.op('act', lambda E: E.activation(out=lr[q][:], in_=pss[ps_][:], func=AF.Ln, scale=1.0 / 64,
                                                             bias=L['eps_c'][:]), ['pss%d' % ps_], ['lr%d' % q])
                        sch.op('act', lambda E: E.activation(out=lr[q][:], in_=lr[q][:], func=AF.Exp, scale=-0.5),
                               ['lr%d' % q], ['lr%d' % q])
                        sch.op('dve', lambda E: E.scalar_tensor_tensor(
                            out=st[:], in0=qsb[q][:], scalar=qkg_sb[:, gi:gi + 1], in1=lr[q][:], op0=ALU.mult,
                            op1=ALU.mult), ['qsb%d' % q, 'lr%d' % q, 'qkg'], [sn])
                        sch.dma('pool', dap, st[0:nr, :], reads=[sn], writes=[], chan='st_' + sn)
                if pend[0] is not None:
                    pend[0]()
                pend[0] = e2
            if pend[0] is not None:
                pend[0]()
                pend[0] = None

            for tl in range(4):
                tt = c * 4 + tl
                pr = nfm % 3
                nfm += 1
                v = ntok % 2
                ntok += 1

                def mmv(E, pr=pr, tl=tl, h=h):
                    res = []
                    for k in range(8):
                        res.append(E.matmul(pp[pr][:, :], lhsT=h[:, k, tl * 128:(tl + 1) * 128],
                                            rhs=win[:, k, 1024:1536], start=(k == 0), stop=(k == 7)))
                    return res
                sch.op('pe', mmv, ['win', hn], ['pp%d' % pr])
                sch.op('act', lambda E, pr=pr, v=v: E.activation(out=stV[v][:], in_=pp[pr][:], func=AF.Copy),
                       ['pp%d' % pr], ['stV%d' % v])
                sch.dma('pool', L['Va'][tt * 128:(tt + 1) * 128, :], stV[v][:], reads=['stV%d' % v], writes=[],
                        chan='st_stV%d' % v)

                def mmb(E, tl=tl, h=h):
                    res = []
                    for k in range(8):
                        res.append(E.matmul(pB[:, 0:128], lhsT=h[:, k, tl * 128:(tl + 1) * 128],
                                            rhs=win[:, k, 2176:2304], start=(k == 0), stop=(k == 7)))
                    for k in range(8):
                        res.append(E.matmul(pB[:, 128:136], lhsT=h[:, k, tl * 128:(tl + 1) * 128],
                                            rhs=win[:, k, 2880:2888], start=(k == 0), stop=(k == 7)))
                    return res
                sch.op('pe', mmb, ['win', hn], ['pB'])
                sch.op('dve', lambda E, v=v: E.tensor_copy(out=stB[v][:], in_=pB[:, 0:128]), ['pB'], ['stB%d' % v])
                sch.op('dve', lambda E, v=v: E.tensor_scalar(out=stW[v][:], in0=pB[:, 128:136], scalar1=512.0 ** -0.5,
                                                             scalar2=None, op0=ALU.mult), ['pB'], ['stW%d' % v])
                sch.dma('pool', L['Vb'][tt * 128:(tt + 1) * 128, :], stB[v][:], reads=['stB%d' % v], writes=[],
                        chan='st_stB%d' % v)
                sch.dma('pool', L['iwS'][tt * 128:(tt + 1) * 128, :], stW[v][:], reads=['stW%d' % v], writes=[],
                        chan='st_stW%d' % v)
        sch.barrier()


def phase_attn_a(nc, sch, al, L):
    S, NT, NCH, b = L['S'], L['NT'], L['NCH'], L['b']
    identb, DThi, b31c = L['identb'], L['DThi'], L['b31c']
    neg_lam, subln_row, eps_c = L['neg_lam'], L['subln_row'], L['eps_c']
    QaT, KaT, Va, oS = L['QaT'], L['KaT'], L['Va'], L['oS']
    with ExitStack() as es:
        Ka = al.sb(es, [128, 4, S], BF16, "Ka")
        Vs = al.sb(es, [128, NT, 4, 130], BF16, "Vs")
        sch.op('pool', lambda E: E.memset(Vs[:], 1.0), [], ['Vs'])
        for t in range(4):
            sch.dma('sp', Ka[:, t, :], KaT[t, :, :], writes=['Ka%d' % t], chan='ldK')
        for j0 in range(NT):
            sch.dma('sp', Vs[:, j0, :, 0:128],
                    Va[j0 * 128:(j0 + 1) * 128, :].rearrange("p (h e) -> p h e", h=4),
                    reads=['Vs'], writes=['Vs%d' % j0], chan='ldV')
        sch.barrier()
        Qc = [al.sb(es, [128, 512], BF16, "Qc%d" % i) for i in range(2)]
        pt = [al.sb(es, [128, 2, 512], BF16, "pt%d" % i) for i in range(3)]
        ost = [al.sb(es, [128, 4, 128], BF16, "ost%d" % i) for i in range(2)]
        rr = [al.sb(es, [128, 4], F32, "rr%d" % i) for i in range(2)]
        t0 = [al.sb(es, [128, 128], F32, "t0%d" % i) for i in range(2)]
        dd = [al.sb(es, [128, 128], F32, "dd%d" % i) for i in range(2)]
        junk = al.sb(es, [128, 128], F32, "junkA")
        accs = [al.sb(es, [128, 2, 2, 2, 129], F32, "accs%d" % i) for i in range(2)]
        st = [al.ps(es, [128, 2, 512], F32, "st%d" % i) for i in range(2)]
        acc = [[al.ps(es, [128, 2, 256], F32, "acc%d%d" % (m, q)) for q in range(2)] for m in range(2)]
        nst = 0
        nq = 0
        nep = [0]
        blocks = []
        for h in range(4):
            for c in range(NCH):
                cq = nq % 2
                nq += 1
                nj = 4 * c + 4
                oq = (h * NCH + c) % 2

                def pre(cq=cq, h=h, c=c):
                    sch.dma('sp', Qc[cq][:], QaT[h, :, c * 512:(c + 1) * 512], writes=['Qc%d' % cq], chan='Qc%d' % cq)

                def epi(h=h, c=c, oq=oq):
                    aq = nep[0] % 2
                    nep[0] += 1
                    A_ = accs[aq]
                    an = 'accs%d' % aq
                    for m in range(2):
                        for q_ in range(2):
                            sch.op('dve', lambda E, m=m, q_=q_: E.tensor_copy(out=A_[:, m, q_, :, :],
                                                                              in_=acc[m][q_][:, :, 0:129]),
                                   [('acc', m, q_)], [an])
                    for li in range(4):
                        e = li % 2
                        a0 = A_[:, 0, li // 2, li % 2, :]
                        a1 = A_[:, 1, li // 2, li % 2, :]
                        sch.op('dve', lambda E, e=e, a0=a0: E.reciprocal(out=rr[e][:, 0:1], in_=a0[:, 128:129]),
                               [an], ['rr%d' % e])
                        sch.op('dve', lambda E, e=e, a1=a1: E.reciprocal(out=rr[e][:, 1:2], in_=a1[:, 128:129]),
                               [an], ['rr%d' % e])
                        sch.op('dve', lambda E, e=e: E.tensor_tensor(out=rr[e][:, 1:2], in0=rr[e][:, 1:2], in1=neg_lam[:],
                                                                     op=ALU.mult), ['rr%d' % e], ['rr%d' % e])
                        sch.op('dve', lambda E, e=e, a0=a0: E.tensor_scalar(out=t0[e][:], in0=a0[:, 0:128],
                                                                            scalar1=rr[e][:, 0:1], scalar2=None,
                                                                            op0=ALU.mult),
                               [an, 'rr%d' % e], ['t0%d' % e])
                        sch.op('dve', lambda E, e=e, a1=a1: E.scalar_tensor_tensor(
                            out=dd[e][:], in0=a1[:, 0:128], scalar=rr[e][:, 1:2], in1=t0[e][:], op0=ALU.mult,
                            op1=ALU.add), [an, 'rr%d' % e, 't0%d' % e], ['dd%d' % e])
                        sch.op('dve', lambda E, e=e: E.scalar_tensor_tensor(
                            out=junk[:], in0=dd[e][:], scalar=1.0, in1=dd[e][:], op0=ALU.mult, op1=ALU.mult,
                            accum_out=rr[e][:, 2:3]), ['dd%d' % e], ['junkA', 'rs%d' % e])
                        sch.op('act', lambda E, e=e: E.activation(out=rr[e][:, 3:4], in_=rr[e][:, 2:3], func=AF.Ln,
                                                                  scale=1.0 / 128, bias=eps_c[:]), ['rs%d' % e], ['rt%d' % e])
                        sch.op('act', lambda E, e=e: E.activation(out=rr[e][:, 3:4], in_=rr[e][:, 3:4], func=AF.Exp,
                                                                  scale=-0.5), ['rt%d' % e], ['rt%d' % e])
                        sch.op('dve', lambda E, e=e, li=li, oq=oq: E.scalar_tensor_tensor(
                            out=ost[oq][:, li, :], in0=dd[e][:], scalar=rr[e][:, 3:4], in1=subln_row[:], op0=ALU.mult,
                            op1=ALU.mult), ['dd%d' % e, 'rt%d' % e], ['ost%d' % oq])
                    sch.dma('pool', oS[c * 512:(c + 1) * 512, h * 128:(h + 1) * 128].rearrange("(li p) e -> p li e", p=128),
                            ost[oq][:], reads=['ost%d' % oq], writes=[], chan='st_ost%d' % oq)

                for j in range(nj):
                    lo = max(0, j - 4 * c)
                    off = lo * 128
                    segs = [(j + dl - 4 * c, dl) for dl in (0, 1) if 0 <= j + dl - 4 * c <= 3]
                    r = nst % 2
                    p_ = nst % 3
                    nst += 1

                    def qk(E, r=r, j=j, off=off, segs=segs, cq=cq, h=h):
                        res = [E.matmul(st[r][:, m, off:512], lhsT=Ka[64 * m:64 * m + 64, h, j * 128:(j + 1) * 128],
                                        rhs=Qc[cq][64 * m:64 * m + 64, off:512], start=True, stop=(not segs))
                               for m in range(2)]
                        if segs:
                            c0 = segs[0][0] * 128
                            n = 128 * len(segs)
                            d0 = segs[0][1] * 128
                            for m in range(2):
                                for T in (DThi,):
                                    res.append(E.matmul(st[r][:, m, c0:c0 + n], lhsT=identb[:], rhs=T[:, h, d0:d0 + n],
                                                        start=False, stop=True))
                        return res

                    def pv(E, p_=p_, j=j, lo=lo, c=c, h=h):
                        res = []
                        for m in range(2):
                            for li in range(lo, 4):
                                res.append(E.matmul(acc[m][li // 2][:, li % 2, 0:129],
                                                    lhsT=pt[p_][:, m, li * 128:(li + 1) * 128],
                                                    rhs=Vs[:, j, h, 0:129], start=(j == 0 and li % 2 == 0),
                                                    stop=(j == 4 * c + li), skip_group_check=True))
                        return res
                    s0 = lambda qk=qk, cq=cq, r=r: sch.op('pe', qk, ['Qc%d' % cq], ['st%d' % r])
                    s1 = lambda r=r, p_=p_, off=off, h=h: sch.op('act', lambda E: E.activation(
                        out=pt[p_][:, :, off:512], in_=st[r][:, :, off:512], func=AF.Exp, bias=b31c[:, h:h + 1]),
                        ['st%d' % r], ['pt%d' % p_])
                    s2 = lambda pv=pv, p_=p_, lo=lo: sch.op(
                        'pe', pv, ['pt%d' % p_], sorted(set(('acc', m, li // 2) for m in range(2) for li in range(lo, 4))))
                    blocks.append(Blk(s0, s1, s2, pre=(pre if j == 0 else None), post=(epi if j == nj - 1 else None)))
        run_pipe(blocks, 1)
        sch.barrier()


def phase_attn_b(nc, sch, al, L):
    S, NT, NCH, b = L['S'], L['NT'], L['NCH'], L['b']
    TK = min(TOPK, S // 4)
    identb, identf_sb, DThi, b31c = L['identb'], L['identf_sb'], L['DThi'], L['b31c']
    negtri, thr_const, fvec = L['negtri'], L['thr_const'], L['fvec']
    QbT, KbT, Vb, iqT, ikT, iwS, oS = L['QbT'], L['KbT'], L['Vb'], L['iqT'], L['ikT'], L['iwS'], L['oS']
    with ExitStack() as es:
        Kb = al.sb(es, [128, 2, S], BF16, "Kb")
        Vs = al.sb(es, [128, NT, 2, 66], BF16, "VsB")
        ik2 = al.sb(es, [128, S], BF16, "ik2")
        iw = al.sb(es, [128, NT, 8], F32, "iw")
        sch.op('pool', lambda E: E.memset(Vs[:], 1.0), [], ['VsB'])
        for half in range(2):
            for g in range(2):
                sch.dma('sp', Kb[64 * half:64 * half + 64, g, :], KbT[64 * g:64 * g + 64, :],
                        writes=['Kb%d%d' % (half, g)], chan='ldK')
            sch.dma('sp', ik2[64 * half:64 * half + 64, :], ikT[0:64, :], writes=['ik2%d' % half], chan='ldK')
        for j0 in range(NT):
            sch.dma('sp', iw[:, j0, :], iwS[j0 * 128:(j0 + 1) * 128, :], writes=['iw%d' % j0], chan='ldK')
        for j0 in range(NT):
            sch.dma('sp', Vs[:, j0, :, 0:64],
                    Vb[j0 * 128:(j0 + 1) * 128, :].rearrange("p (g e) -> p g e", g=2),
                    reads=['VsB'], writes=['VsB%d' % j0], chan='ldV')
        sch.barrier()
        NI = 4
        Ib = [al.sb(es, [128, S], F32, "I%d" % i) for i in range(NI)]
        ma = [al.sb(es, [128, S], BF16, "ma%d" % i) for i in range(NI)]
        maT = al.sb(es, [128, NT, 512], BF16, "maT")
        Rb = [al.sb(es, [128, 2, 512], BF16, "R%d" % i) for i in range(2)]
        dg = [al.sb(es, [128, 8, 128], BF16, "dg%d" % i) for i in range(2)]
        iqc = [al.sb(es, [128, 4, 128], BF16, "iqc%d" % i) for i in range(2)]
        Qc = [al.sb(es, [128, 4, 512], BF16, "QcB%d" % i) for i in range(2)]
        pt = [al.sb(es, [128, 2, 512], BF16, "ptB%d" % i) for i in range(3)]
        ost = [al.sb(es, [128, 4, 512], BF16, "ostB%d" % i) for i in range(1)]
        bs = [al.sb(es, [128, 8], F32, "bs%d" % i) for i in range(NI)]
        rcp = al.sb(es, [128, 2, 4], F32, "rcp")
        bw = [al.sb(es, [128, NIT], F32, "bw%d" % i) for i in range(NI)]
        PP = [al.ps(es, [128, 2, 512], F32, "PP%d" % i) for i in range(2)]
        pacc = al.ps(es, [128, 512], F32, "pacc")
        tpb = al.ps(es, [128, 8, 128], BF16, "tpb")
        accb = al.ps(es, [128, 2, 4, 128], F32, "accb")
        cnt = {'x': 0, 'R': 0, 'pt': 0, 'sb': 0, 'acc': 0}

        def idx_chunk(c):
            blocks = []
            pending = []
            for li in range(4):
                i = 4 * c + li
                ib = i % NI
                q = i % 2
                Li = 128 * (i + 1)

                def pre(i=i, q=q):
                    sch.dma('sp', iqc[q][:], iqT[:, :, i * 128:(i + 1) * 128].rearrange("t p n -> p t n"),
                            writes=['iqc%d' % q], chan='iqc%d' % q)
                    for hh in range(8):
                        sch.op('pool', lambda E, hh=hh: E.tensor_scalar(
                            out=dg[q][:, hh, :], in0=identf_sb[:], scalar1=iw[:, i, hh:hh + 1], scalar2=None,
                            op0=ALU.mult), [], ['dg%d' % q])

                def gen_tile(i=i, ib=ib, Li=Li):
                    sch.op('dve', lambda E: E.tensor_tensor(
                        out=Ib[ib][:, i * 128:(i + 1) * 128], in0=Ib[ib][:, i * 128:(i + 1) * 128], in1=negtri[:],
                        op=ALU.add), ['I%d' % ib], ['I%d' % ib])
                    B = bs[ib]
                    bn = 'bs%d' % ib
                    if i >= TK // 128:
                        W = bw[ib]
                        sch.op('dve', lambda E: E.tensor_reduce(
                            out=B[:, 0:1], in_=Ib[ib][:, 0:i * 128], axis=mybir.AxisListType.X, op=ALU.min),
                            ['I%d' % ib], [bn])
                        sch.op('dve', lambda E: E.tensor_reduce(
                            out=B[:, 1:2], in_=Ib[ib][:, 0:Li], axis=mybir.AxisListType.X, op=ALU.max),
                            ['I%d' % ib], [bn])
                        yield
                        sch.op('dve', lambda E: E.tensor_tensor(out=B[:, 1:2], in0=B[:, 1:2], in1=B[:, 0:1],
                                                                op=ALU.subtract), [bn], [bn])
                        sch.op('dve', lambda E: E.tensor_scalar(out=W[:], in0=fvec[:], scalar1=B[:, 1:2], scalar2=None,
                                                                op0=ALU.mult), [bn], [bn + 'w'])
                        sch.op('dve', lambda E: E.tensor_tensor(out=B[:, 2:3], in0=B[:, 0:1], in1=W[:, 0:1], op=ALU.add),
                               [bn, bn + 'w'], [bn])
                        yield
                        for k in range(NIT):
                            sch.op('dve', lambda E: E.tensor_scalar(
                                out=ma[ib][:, 0:Li], in0=Ib[ib][:, 0:Li], scalar1=B[:, 2:3], scalar2=None, op0=ALU.is_ge,
                                op1=ALU.add, accum_out=B[:, 3:4]), [bn, 'I%d' % ib], [bn, 'ma%d' % ib])
                            yield
                            last = (k == NIT - 1)
                            sch.op('dve', lambda E, last=last: E.tensor_scalar(
                                out=B[:, 4:5], in0=B[:, 3:4], scalar1=float(TK), scalar2=(-1.0 if last else -0.5),
                                op0=ALU.is_ge, op1=ALU.add), [bn], [bn])
                            yield
                            sch.op('dve', lambda E, k=k, last=last: E.scalar_tensor_tensor(
                                out=(B[:, 5:6] if last else B[:, 2:3]), in0=B[:, 4:5], scalar=W[:, k:k + 1], in1=B[:, 2:3],
                                op0=ALU.mult, op1=ALU.add), [bn, bn + 'w'], [bn])
                            yield
                        thr = B[:, 5:6]
                    else:
                        thr = thr_const[:, 0:1]
                    sch.op('dve', lambda E: E.tensor_scalar(
                        out=ma[ib][:, 0:Li], in0=Ib[ib][:, 0:Li], scalar1=thr, scalar2=NEG, op0=ALU.is_lt, op1=ALU.mult),
                        [bn, 'I%d' % ib], ['ma%d' % ib])

                def post_tile(li=li, gen_tile=gen_tile):
                    pending.append(gen_tile())
                    if li % 2 == 1:
                        gens = list(pending)
                        del pending[:]
                        while gens:
                            for g_ in list(gens):
                                try:
                                    next(g_)
                                except StopIteration:
                                    gens.remove(g_)

                nsc = (Li + 511) // 512
                for si, sc0 in enumerate(range(0, Li, 512)):
                    w = min(512, Li - sc0)
                    for t in range(4):
                        xr = cnt['x'] % 2
                        cnt['x'] += 1
                        rq = cnt['R'] % 2
                        cnt['R'] += 1
                        s0 = lambda xr=xr, q=q, t=t, sc0=sc0, w=w: sch.op('pe', lambda E: [E.matmul(
                            PP[xr][:, e, 0:w], lhsT=iqc[q][64 * e:64 * e + 64, t, :], rhs=ik2[64 * e:64 * e + 64, sc0:sc0 + w],
                            start=True, stop=True) for e in range(2)], ['iqc%d' % q], ['PP%d' % xr])
                        s1 = lambda xr=xr, rq=rq, w=w: sch.op('act', lambda E: E.activation(
                            out=Rb[rq][:, :, 0:w], in_=PP[xr][:, :, 0:w], func=AF.Relu), ['PP%d' % xr], ['R%d' % rq])
                        s2 = lambda rq=rq, q=q, t=t, w=w: sch.op('pe', lambda E: [E.matmul(
                            pacc[:, 0:w], lhsT=dg[q][:, 2 * t + e, :], rhs=Rb[rq][:, e, 0:w], start=(t == 0 and e == 0),
                            stop=(t == 3 and e == 1)) for e in range(2)], ['R%d' % rq, 'dg%d' % q], ['pacc'])
                        post = None
                        if t == 3:
                            last = (si == nsc - 1)

                            def post(ib=ib, sc0=sc0, w=w, last=last, post_tile=post_tile):
                                sch.op('act', lambda E: E.activation(out=Ib[ib][:, sc0:sc0 + w], in_=pacc[:, 0:w],
                                                                     func=AF.Copy), ['pacc'], ['I%d' % ib])
                                if last:
                                    post_tile()
                        blocks.append(Blk(s0, s1, s2, pre=(pre if (si == 0 and t == 0) else None), post=post))
            run_pipe(blocks, 1)

        def tr_chunk(c):
            for li in range(4):
                i = 4 * c + li
                ib = i % NI
                for j0 in range(0, i + 1, 8):
                    n = min(8, i + 1 - j0)

                    def tr(E, ib=ib, j0=j0, n=n):
                        return [E.transpose(out=tpb[:, jj, :], in_=ma[ib][:, (j0 + jj) * 128:(j0 + jj + 1) * 128],
                                            identity=identb[:]) for jj in range(n)]
                    sch.op('pe', tr, ['ma%d' % ib], ['tpb'])
                    sch.op('dve', lambda E, j0=j0, n=n, li=li: E.tensor_copy(
                        out=maT[:, j0:j0 + n, li * 128:(li + 1) * 128], in_=tpb[:, 0:n, :]), ['tpb'], ['maT'])

        def attn_chunk(c):
            cq = c % 2
            blocks = []
            nj = 4 * c + 4

            def pre():
                sch.dma('sp', Qc[cq][:], QbT[:, :, c * 512:(c + 1) * 512].rearrange("t p n -> p t n"),
                        writes=['QcB%d' % cq], chan='QcB%d' % cq)
            for t in range(4):
                g = t // 2

                def epi(t=t):
                    sch.op('act', lambda E: E.activation(out=rcp[:], in_=accb[:, :, :, 64], func=AF.Ln),
                           ['accb'], ['rcp'])
                    sch.op('act', lambda E: E.activation(out=rcp[:], in_=rcp[:], func=AF.Exp, scale=-1.0),
                           ['rcp'], ['rcp'])
                    sch.op('act', lambda E: [E.activation(
                        out=ost[0][:, li, (2 * t + e) * 64:(2 * t + e + 1) * 64], in_=accb[:, e, li, 0:64], func=AF.Copy,
                        scale=rcp[:, e, li:li + 1]) for e in range(2) for li in range(4)], ['accb', 'rcp'], ['ostB0'])
                    if t == 3:
                        sch.dma('pool', oS[c * 512:(c + 1) * 512, 512:1024].rearrange("(li p) e -> p li e", p=128),
                                ost[0][:], reads=['ostB0'], writes=[], chan='st_ostB0')
                for j in range(nj):
                    lo = max(0, j - 4 * c)
                    off = lo * 128
                    segs = [(j + dl - 4 * c, dl) for dl in (0, 1) if 0 <= j + dl - 4 * c <= 3]
                    r = cnt['x'] % 2
                    cnt['x'] += 1
                    p_ = cnt['pt'] % 3
                    cnt['pt'] += 1

                    def qk(E, r=r, j=j, off=off, segs=segs, t=t, g=g):
                        res = [E.matmul(PP[r][:, e, off:512], lhsT=Kb[64 * e:64 * e + 64, g, j * 128:(j + 1) * 128],
                                        rhs=Qc[cq][64 * e:64 * e + 64, t, off:512], start=True, stop=False)
                               for e in range(2)]
                        for e in range(2):
                            res.append(E.matmul(PP[r][:, e, off:512], lhsT=identb[:], rhs=maT[:, j, off:512], start=False,
                                                stop=(not segs)))
                        if segs:
                            c0 = segs[0][0] * 128
                            n = 128 * len(segs)
                            d0 = segs[0][1] * 128
                            for e in range(2):
                                for T in (DThi,):
                                    res.append(E.matmul(PP[r][:, e, c0:c0 + n], lhsT=identb[:],
                                                        rhs=T[:, 4 + 2 * t + e, d0:d0 + n], start=False, stop=True))
                        return res

                    def pv(E, p_=p_, j=j, lo=lo, g=g):
                        return [E.matmul(accb[:, e, li, 0:65], lhsT=pt[p_][:, e, li * 128:(li + 1) * 128],
                                         rhs=Vs[:, j, g, 0:65], start=(j == 0 and li == 0), stop=(j == 4 * c + li),
                                         skip_group_check=True) for e in range(2) for li in range(lo, 4)]
                    s0 = lambda qk=qk, r=r: sch.op('pe', qk, ['QcB%d' % cq, 'maT'], ['PP%d' % r])
                    s1 = lambda r=r, p_=p_, off=off, t=t: sch.op('act', lambda E: [E.activation(
                        out=pt[p_][:, e, off:512], in_=PP[r][:, e, off:512], func=AF.Exp,
                        bias=b31c[:, 4 + 2 * t + e:5 + 2 * t + e]) for e in range(2)], ['PP%d' % r], ['ptB%d' % p_])
                    s2 = lambda pv=pv, p_=p_: sch.op('pe', pv, ['ptB%d' % p_], ['accb'])
                    blocks.append(Blk(s0, s1, s2, pre=(pre if (t == 0 and j == 0) else None),
                                      post=(epi if j == nj - 1 else None)))
            run_pipe(blocks, 1)

        idx_chunk(0)
        tr_chunk(0)
        for c in range(NCH):
            if c + 1 < NCH:
                idx_chunk(c + 1)
            attn_chunk(c)
            if c + 1 < NCH:
                tr_chunk(c + 1)
        sch.barrier()


def phase_f(nc, sch, al, L):
    with ExitStack() as es:
        wu = al.sb(es, [128, 8, 2 * DFF], BF16, "wu")
        wd = al.sb(es, [128, 22, D], BF16, "wd")
        phase_f1(nc, sch, al, L, wu, wd)
        phase_f2(nc, sch, al, L, wu, wd)


def phase_f1(nc, sch, al, L, wu, wd):
    S, NT, NCH, b = L['S'], L['NT'], L['NCH'], L['b']
    w_up, w_down = L['w_up'], L['w_down']
    identb, eps_c, a_ffn, modc = L['identb'], L['eps_c'], L['a_ffn'], L['modc']
    x, out, oS, x1nT, w_out, modrow = L['x'], L['out'], L['oS'], L['x1nT'], L['w_out'], L['modrow']
    with ExitStack() as es:
        wo = al.sb(es, [128, 8, D], BF16, "wo")
        with ExitStack() as e2:
            grow = al.sb(e2, [128, D], F32, "grow")
            stg = [al.sb(e2, [128, D], F32, "wos%d" % i) for i in range(2)]
            sch.dma('sp', grow[:], modrow[b:b + 1, 2048:3072].partition_broadcast(128), writes=['grow'], chan='grow')
            for k in range(8):
                sch.dma('sp', stg[k % 2][:], w_out[k * 128:(k + 1) * 128, :], writes=['wos%d' % (k % 2)],
                        chan='wos%d' % (k % 2))
                sch.op('dve', lambda E, k=k: E.tensor_tensor(out=wo[:, k, :], in0=stg[k % 2][:], in1=grow[:], op=ALU.mult),
                       ['wos%d' % (k % 2), 'grow'], ['wo'])
            sch.barrier()
        ot = [al.sb(es, [128, D], BF16, "ot%d" % i) for i in range(2)]
        oT = [al.sb(es, [128, 8, 128], BF16, "oT%d" % i) for i in range(2)]
        xt = [al.sb(es, [128, D], F32, "xtf%d" % i) for i in range(2)]
        x1 = [al.sb(es, [128, D], F32, "x1%d" % i) for i in range(2)]
        x1n = [al.sb(es, [128, D], BF16, "x1n%d" % i) for i in range(2)]
        x1s = [al.sb(es, [128, 8, 128], BF16, "x1s%d" % i) for i in range(2)]
        junk = al.sb(es, [128, D], BF16, "junkF")
        ss = [al.sb(es, [128, 2], F32, "ssf%d" % i) for i in range(2)]
        tpo = [al.ps(es, [128, 8, 128], BF16, "tpo%d" % i) for i in range(2)]
        tpn = [al.ps(es, [128, 8, 128], BF16, "tpn%d" % i) for i in range(2)]
        pso = [[al.ps(es, [128, 512], F32, "pso%d%d" % (i, hf)) for hf in range(2)] for i in range(2)]
        growf = al.sb(es, [128, D], F32, "growf")
        pstg = [al.sb(es, [128, 1408], F32, "pstg%d" % i) for i in range(2)]
        sch.dma('sp', growf[:], modrow[b:b + 1, 5120:6144].partition_broadcast(128), writes=['growf'], chan='growf')
        steps = []
        for k in range(8):
            for q4 in range(4):
                steps.append(('cast', w_up[k * 128:(k + 1) * 128, q4 * 1408:(q4 + 1) * 1408],
                              wu[:, k, q4 * 1408:(q4 + 1) * 1408]))
        for ct in range(22):
            steps.append(('mul', w_down[ct * 128:(ct + 1) * 128, :], wd[:, ct, :]))
        nstep = [0]

        def prep_step():
            if nstep[0] >= len(steps):
                return
            kind, src, dst = steps[nstep[0]]
            q = nstep[0] % 2
            n_ = nstep[0]
            nstep[0] += 1
            if kind == 'cast':
                sch.dma('sp', pstg[q][:], src, writes=['pstg%d' % q], chan='pstg%d' % q)
                sch.op('act', lambda E: E.activation(out=dst, in_=pstg[q][:], func=AF.Copy), ['pstg%d' % q],
                       ['wprep%d' % n_])
            else:
                sch.dma('sp', pstg[q][:, 0:D], src, writes=['pstg%d' % q], chan='pstg%d' % q)
                sch.op('dve', lambda E: E.tensor_tensor(out=dst, in0=pstg[q][:, 0:D], in1=growf[:], op=ALU.mult),
                       ['pstg%d' % q, 'growf'], ['wprep%d' % n_])
        def part_a(tt):
            r = tt % 2
            rows = slice(tt * 128, (tt + 1) * 128)
            sch.dma('sp', ot[r][:], oS[rows, :], writes=['ot%d' % r], chan='ot%d' % r)
            sch.dma('sp', xt[r][:], x[b, rows, :], writes=['xtf%d' % r], chan='xtf%d' % r)
            sch.op('pe', lambda E, r=r: [E.transpose(out=tpo[r][:, k, :], in_=ot[r][:, k * 128:(k + 1) * 128],
                                                     identity=identb[:]) for k in range(8)],
                   ['ot%d' % r], ['tpo%d' % r])
            sch.op('act', lambda E, r=r: E.activation(out=oT[r][:], in_=tpo[r][:], func=AF.Copy),
                   ['tpo%d' % r], ['oT%d' % r])
            for hf in range(2):
                sch.op('pe', lambda E, r=r, hf=hf: [E.matmul(pso[r][hf][:], lhsT=oT[r][:, k, :],
                                                             rhs=wo[:, k, hf * 512:(hf + 1) * 512],
                                                             start=(k == 0), stop=(k == 7)) for k in range(8)],
                       ['oT%d' % r], ['pso%d%d' % (r, hf)])

        def part_a2(tt):
            r = tt % 2
            rows = slice(tt * 128, (tt + 1) * 128)
            for hf in range(2):
                sch.op('dve', lambda E, r=r, hf=hf: E.tensor_tensor(
                    out=x1[r][:, hf * 512:(hf + 1) * 512], in0=pso[r][hf][:], in1=xt[r][:, hf * 512:(hf + 1) * 512],
                    op=ALU.add), ['pso%d%d' % (r, hf), 'xtf%d' % r], ['x1%d' % r])
            sch.dma('pool', out[b, rows, :], x1[r][:], reads=['x1%d' % r], writes=[], chan='st_x1%d' % r)
            sch.op('dve', lambda E, r=r: E.scalar_tensor_tensor(
                out=junk[:], in0=x1[r][:], scalar=1.0, in1=x1[r][:], op0=ALU.mult, op1=ALU.mult,
                accum_out=ss[r][:, 0:1]), ['x1%d' % r], ['junkF', 'ssf%d' % r])
            sch.op('act', lambda E, r=r: E.activation(out=ss[r][:, 1:2], in_=ss[r][:, 0:1], func=AF.Ln, scale=1.0 / D,
                                                      bias=eps_c[:]), ['ssf%d' % r], ['ssg%d' % r])
            sch.op('act', lambda E, r=r: E.activation(out=ss[r][:, 1:2], in_=ss[r][:, 1:2], func=AF.Exp, scale=-0.5),
                   ['ssg%d' % r], ['ssg%d' % r])
            sch.op('dve', lambda E, r=r: E.tensor_scalar(out=x1n[r][:], in0=x1[r][:], scalar1=ss[r][:, 1:2],
                                                         scalar2=None, op0=ALU.mult),
                   ['x1%d' % r, 'ssg%d' % r], ['x1n%d' % r])

        def part_b(tt):
            r = tt % 2
            rows = slice(tt * 128, (tt + 1) * 128)
            sch.op('pe', lambda E, r=r: [E.transpose(out=tpn[r][:, k, :], in_=x1n[r][:, k * 128:(k + 1) * 128],
                                                     identity=identb[:]) for k in range(8)],
                   ['x1n%d' % r], ['tpn%d' % r])
            sch.op('act', lambda E, r=r: [E.activation(out=x1s[r][:, k, :], in_=tpn[r][:, k, :], func=AF.Identity,
                                                       scale=a_ffn[:, k:k + 1], bias=modc[:, 24 + k:25 + k])
                                          for k in range(8)], ['tpn%d' % r], ['x1s%d' % r])
            sch.dma('pool', x1nT[:, :, rows].rearrange("k p n -> p k n"), x1s[r][:], reads=['x1s%d' % r], writes=[],
                    chan='st_x1s%d' % r)

        for tt in range(NT + 2):
            if tt < NT:
                for _ in range((len(steps) + NT - 1) // NT):
                    prep_step()
                part_a(tt)
            if 1 <= tt <= NT:
                part_a2(tt - 1)
            if tt >= 2:
                part_b(tt - 2)
        sch.barrier()


def phase_f2(nc, sch, al, L, wu, wd):
    S, NT, NCH, b = L['S'], L['NT'], L['NCH'], L['b']
    cw, cb = L['cw_sb'], L['cb_sb']
    out, x1nT = L['out'], L['x1nT']
    with ExitStack() as es:
        gT = al.sb(es, [128, 22, 512], BF16, "gT")
        gpre = al.sb(es, [128, 2, 2, 512], BF16, "gpre")
        xc = [al.sb(es, [128, 8, 512], BF16, "xc%d" % i) for i in range(2)]
        u = [al.sb(es, [128, 514], F32, "u%d" % i) for i in range(2)]
        y = [al.sb(es, [128, 512], F32, "y%d" % i) for i in range(3)]
        sg = [al.sb(es, [128, 512], BF16, "sg%d" % i) for i in range(2)]
        x1t = [al.sb(es, [128, D], F32, "x1t%d" % i) for i in range(2)]
        hal = al.sb(es, [128, 44, 2], F32, "hal")
        pu = [al.ps(es, [128, 512], F32, "pu%d" % i) for i in range(4)]
        pd = [[al.ps(es, [128, 512], F32, "pd%d%d" % (i, hf)) for hf in range(2)] for i in range(2)]
        sch.op('pool', lambda E: E.memset(hal[:], 0.0), [], ['hal'])
        nu = 0
        ny = 0
        nt_ = 0
        def load_xc(c):
            sch.dma('sp', xc[c % 2][:], x1nT[:, :, c * 512:(c + 1) * 512].rearrange("k p n -> p k n"),
                    writes=['xc%d' % (c % 2)], chan='xc%d' % (c % 2))

        cntf = {'u': 0, 'y': 0}
        NPRE = 2

        def up_pair(c, ct):
            r = c % 2
            ys = []
            for hf in range(2):
                col = hf * 22 + ct
                p_ = cntf['u'] % 4
                uq = cntf['u'] % 2
                cntf['u'] += 1
                yq = cntf['y'] % 3
                cntf['y'] += 1
                ys.append(yq)
                sch.op('pe', lambda E, p_=p_, col=col, r=r: [E.matmul(
                    pu[p_][:], lhsT=wu[:, k, col * 128:(col + 1) * 128], rhs=xc[r][:, k, :], start=(k == 0),
                    stop=(k == 7)) for k in range(8)], ['xc%d' % r], ['pu%d' % p_])
                sch.op('act', lambda E, p_=p_, uq=uq: E.activation(out=u[uq][:, 2:514], in_=pu[p_][:], func=AF.Copy),
                       ['pu%d' % p_], ['u%d' % uq])
                sch.op('act', lambda E, p_=p_, yq=yq, col=col: E.activation(
                    out=y[yq][:], in_=pu[p_][:], func=AF.Identity, scale=cw[:, col, 2:3], bias=cb[:, col:col + 1]),
                    ['pu%d' % p_], ['y%d' % yq])
                sch.op('pool', lambda E, uq=uq, col=col: E.tensor_copy(out=u[uq][:, 0:2], in_=hal[:, col, :]),
                       ['hal'], ['u%d' % uq])
                sch.op('pool', lambda E, uq=uq, col=col: E.tensor_copy(out=hal[:, col, :], in_=u[uq][:, 512:514]),
                       ['u%d' % uq], ['hal'])
                for jj in (1, 0):
                    sch.op('dve', lambda E, uq=uq, yq=yq, col=col, jj=jj: E.scalar_tensor_tensor(
                        out=y[yq][:], in0=u[uq][:, jj:jj + 512], scalar=cw[:, col, jj:jj + 1], in1=y[yq][:],
                        op0=ALU.mult, op1=ALU.add), ['u%d' % uq, 'y%d' % yq], ['y%d' % yq])
            sq_ = ct % 2
            sch.op('act', lambda E, sq_=sq_, yg=ys[0]: E.activation(out=sg[sq_][:], in_=y[yg][:], func=AF.Silu),
                   ['y%d' % ys[0]], ['sg%d' % sq_])
            gdst = gpre[:, c % 2, ct, :] if ct < NPRE else gT[:, ct, :]
            gname = ('gpre', c % 2, ct) if ct < NPRE else ('gT', ct)
            sch.op('pool', lambda E, sq_=sq_, yv=ys[1], gdst=gdst: E.tensor_tensor(out=gdst, in0=sg[sq_][:],
                                                                                   in1=y[yv][:], op=ALU.mult),
                   ['sg%d' % sq_, 'y%d' % ys[1]], [gname])

        load_xc(0)
        for c in range(NCH):
            for ct in range(NPRE if c > 0 else 0, 22):
                up_pair(c, ct)
            if c + 1 < NCH:
                load_xc(c + 1)

            def load_x1t(li):
                rws = slice(c * 512 + li * 128, c * 512 + (li + 1) * 128)
                sch.dma('sp', x1t[li % 2][:], out[b, rws, :], writes=['x1t%d' % (li % 2)], chan='x1t%d' % (li % 2))
            load_x1t(0)
            load_x1t(1)
            if c + 1 < NCH:
                for ct in range(NPRE):
                    up_pair(c + 1, ct)
            for li in range(4):
                tq = li % 2
                rows = slice(c * 512 + li * 128, c * 512 + (li + 1) * 128)
                for hf in range(2):
                    sch.op('pe', lambda E, tq=tq, hf=hf, li=li, c=c: [E.matmul(
                        pd[tq][hf][:], lhsT=(gpre[:, c % 2, ct, li * 128:(li + 1) * 128] if ct < NPRE
                                             else gT[:, ct, li * 128:(li + 1) * 128]),
                        rhs=wd[:, ct, hf * 512:(hf + 1) * 512],
                        start=(ct == 0), stop=(ct == 21)) for ct in range(22)],
                        [(('gpre', c % 2, ct) if ct < NPRE else ('gT', ct)) for ct in range(22)],
                        ['pd%d%d' % (tq, hf)])
                    sch.op('dve', lambda E, tq=tq, hf=hf: E.tensor_tensor(
                        out=x1t[tq][:, hf * 512:(hf + 1) * 512], in0=pd[tq][hf][:], in1=x1t[tq][:, hf * 512:(hf + 1) * 512],
                        op=ALU.add), ['pd%d%d' % (tq, hf), 'x1t%d' % tq], ['x1t%d' % tq])
                sch.dma('sp', out[b, rows, :], x1t[tq][:], reads=['x1t%d' % tq], writes=[], chan='st_x1t%d' % tq)
                if li + 2 < 4:
                    load_x1t(li + 2)
        sch.barrier()


def t5_bucket_np(n):
    n = np.maximum(n, 0)
    nf = np.maximum(n, 1).astype(np.float32)
    large = 16 + (np.log(nf / np.float32(16)) / np.float32(math.log(128 / 16)) * np.float32(16)).astype(np.int32)
    large = np.minimum(large, 31)
    return np.where(n < 16, n, large)


def prep_shared(inp):
    f = lambda a: np.ascontiguousarray(a, dtype=np.float32)
    rb = np.asarray(inp['rel_bias'], np.float32)
    s_ = np.arange(128)[:, None]
    t_ = np.arange(256)[None, :]
    bk = t5_bucket_np(t_ - s_)
    biasT = rb[bk]
    cm = np.where(t_ >= s_, 0.0, NEG).astype(np.float32)
    pq = np.arange(128)
    sh = {
        'w_ada': f(inp['w_ada'][0]), 'b_ada': f(inp['b_ada']),
        'g_attn_c': f(np.asarray(inp['g_attn'][0]).reshape(8, 128).T),
        'g_ffn_c': f(np.asarray(inp['g_ffn'][0]).reshape(8, 128).T),
        'w_in': f(inp['w_in'][0]),
        'qkg': f(np.stack([np.tile(np.asarray(inp[k][0]), 2) for k in
                           ('q_norm_a', 'k_norm_a', 'q_norm_b', 'k_norm_b')], 1)),
        'lamv': f(np.asarray(inp['lam_vecs'][0]).reshape(1, 256)),
        'subln': f(np.asarray(inp['subln_a'])),
        'w_out': f(inp['w_out'][0]), 'w_up': f(inp['w_up'][0]),
        'conv_wc': f(np.asarray(inp['conv_w'][0]).T.reshape(44, 128, 3).transpose(1, 0, 2)),
        'conv_bc': f(np.asarray(inp['conv_b'][0]).reshape(44, 128).T),
        'w_down': f(inp['w_down'][0]),
        'biasT': f(biasT.transpose(0, 2, 1)),
        'b31': f(rb[31:32, :]),
        'cmask': cm,
        'identf': np.eye(128, dtype=np.float32),
        'blk1': f((pq[:, None] // 64) == (pq[None, :] // 64)),
    }
    return sh


def core_inputs(inp, sh, rows):
    m = dict(sh)
    xs = np.ascontiguousarray(np.asarray(inp['x'])[rows], dtype=np.float32)
    cs = np.asarray(inp['c'], np.float32)[rows]
    m['x'] = xs
    m['cT'] = np.ascontiguousarray(cs.reshape(len(rows), 8, 128).transpose(2, 1, 0))
    return m


_NC_CACHE = {}


def kernel(**inputs):
    B, S, _ = inputs['x'].shape
    ncores = 8
    NB = B // ncores
    key = (S, NB)
    if key not in _NC_CACHE:
        _NC_CACHE[key] = build(S, NB)
    nc = _NC_CACHE[key]
    sh = prep_shared(inputs)
    in_maps = [core_inputs(inputs, sh, list(range(i * NB, (i + 1) * NB))) for i in range(ncores)]
    res = run_bass_kernel_spmd(nc, in_maps, core_ids=list(range(ncores)))
    return np.concatenate([np.asarray(r['out']) for r in res.results], axis=0).astype(np.float32)
```

```python
from contextlib import ExitStack
import math
import numpy as np
import concourse.bass as bass
import concourse.mybir as mybir
from concourse.bass_utils import run_bass_kernel_spmd

F32 = mybir.dt.float32
BF16 = mybir.dt.bfloat16
AF = mybir.ActivationFunctionType
ALU = mybir.AluOpType

D = 1024
DFF = 2816
INC = 2888
NEG = -30000.0
EPS = 1e-6
LAM_INIT = 0.8 - 0.6
TOPK = 256
NIT = 13


class Sched:
    def __init__(self, nc):
        self.nc = nc
        self.engs = {'pe': nc.tensor, 'act': nc.scalar, 'dve': nc.vector, 'pool': nc.gpsimd, 'sp': nc.sync}
        self.sems, self.cnt, self.mult = {}, {}, {}
        self.seen = {e: {} for e in self.engs}
        self.lastw, self.readers = {}, {}
        for e in ['pe', 'act', 'dve', 'pool']:
            self._chan(e, 1)

    def _chan(self, name, mult):
        if name not in self.sems:
            self.sems[name] = self.nc.alloc_semaphore("s%d" % len(self.sems))
            self.cnt[name] = 0
            self.mult[name] = mult

    def op(self, eng, fn, reads=(), writes=(), chan=None):
        deps = {}

        def add(c, n):
            if deps.get(c, 0) < n:
                deps[c] = n
        for r in reads:
            for c, n in self.lastw.get(r, {}).items():
                add(c, n)
        for w in writes:
            for c, n in self.lastw.get(w, {}).items():
                add(c, n)
            for c, n in self.readers.get(w, {}).items():
                add(c, n)
        E = self.engs[eng]
        seen = self.seen[eng]
        need = []
        for c, n in deps.items():
            if eng == 'pe' and c == 'pe':
                continue
            if seen.get(c, 0) < n:
                need.append((c, n))
                seen[c] = n
        for c, n in need[1:]:
            E.wait_ge(self.sems[c], n * self.mult[c])
        r = fn(E)
        if isinstance(r, (tuple, list)):
            first, last = r[0], r[-1]
        else:
            first = last = r
        if need:
            c, n = need[0]
            first._wait_ge(self.sems[c], n * self.mult[c])
        ch = chan or eng
        if ch not in self.sems:
            self._chan(ch, 16)
        self.cnt[ch] += 1
        last.then_inc(self.sems[ch], self.mult[ch])
        me = (ch, self.cnt[ch])
        for w in writes:
            self.lastw[w] = {me[0]: me[1]}
            self.readers[w] = {}
        for r_ in reads:
            self.readers.setdefault(r_, {})[me[0]] = me[1]

    def dma(self, q, out, in_, reads=(), writes=(), chan=None, **kw):
        assert chan is not None
        self.op(q, lambda E: E.dma_start(out=out, in_=in_, **kw), reads, writes, chan=chan)

    def barrier(self, engines=None):
        for e in (engines or self.engs):
            E = self.engs[e]
            for c, n in self.cnt.items():
                if n > 0 and self.seen[e].get(c, 0) < n:
                    E.wait_ge(self.sems[c], n * self.mult[c])
                    self.seen[e][c] = n
        if engines is None:
            self.lastw, self.readers = {}, {}


class Alloc:
    def __init__(self, nc):
        self.nc = nc
        self.n = 0

    def sb(self, es, shape, dt, name="t"):
        self.n += 1
        return es.enter_context(self.nc.sbuf_tensor("%s_%d" % (name, self.n), list(shape), dt))

    def ps(self, es, shape, dt=F32, name="p"):
        self.n += 1
        return es.enter_context(self.nc.psum_tensor("%s_%d" % (name, self.n), list(shape), dt))


class Blk:
    __slots__ = ('pre', 's0', 's1', 's2', 'post', 'post2')

    def __init__(self, s0, s1, s2, pre=None, post=None, post2=None):
        self.pre, self.s0, self.s1, self.s2, self.post, self.post2 = pre, s0, s1, s2, post, post2


def run_pipe(blocks, depth=1):
    n = len(blocks)
    for i in range(min(depth, n)):
        if blocks[i].pre:
            blocks[i].pre()
        blocks[i].s0()
    for i in range(n):
        if i + depth < n:
            if blocks[i + depth].pre:
                blocks[i + depth].pre()
            blocks[i + depth].s0()
        blocks[i].s1()
        blocks[i].s2()
        if blocks[i].post:
            blocks[i].post()
        if i >= 3 and blocks[i - 3].post2:
            blocks[i - 3].post2()
    for i in range(max(0, n - 3), n):
        if blocks[i].post2:
            blocks[i].post2()


def build(S, NB, debug=False):
    NT = S // 128
    NCH = S // 512
    nc = bass.Bass("TRN2", target_bir_lowering=False)
    sch = Sched(nc)
    al = Alloc(nc)

    def din(name, shape, dt=F32):
        return nc.dram_tensor(name, list(shape), dt, kind="ExternalInput").ap()

    def dscr(name, shape, dt):
        return nc.dram_tensor(name, list(shape), dt, kind="ExternalOutput" if debug else "Internal").ap()

    x = din("x", [NB, S, D])
    cT = din("cT", [128, 8, NB])
    w_ada = din("w_ada", [D, 6 * D])
    b_ada = din("b_ada", [1, 6 * D])
    g_attn_c = din("g_attn_c", [128, 8])
    g_ffn_c = din("g_ffn_c", [128, 8])
    w_in = din("w_in", [D, INC])
    qkg = din("qkg", [128, 4])
    lamv = din("lamv", [1, 256])
    subln = din("subln", [1, 128])
    w_out = din("w_out", [D, D])
    w_up = din("w_up", [D, 2 * DFF])
    conv_wc = din("conv_wc", [128, 44, 3])
    conv_bc = din("conv_bc", [128, 44])
    w_down = din("w_down", [DFF, D])
    biasT = din("biasT", [128, 12, 256])
    b31 = din("b31", [1, 12])
    cmask = din("cmask", [128, 256])
    identf = din("identf", [128, 128])
    blk1 = din("blk1", [128, 128])
    out = nc.dram_tensor("out", [NB, S, D], F32, kind="ExternalOutput").ap()

    modrow = dscr("modrow", [NB, 6 * D], F32)
    QaT = dscr("QaT", [4, 128, S], BF16)
    KaT = dscr("KaT", [4, 128, S], BF16)
    Va = dscr("Va", [S, 512], BF16)
    QbT = dscr("QbT", [4, 128, S], BF16)
    KbT = dscr("KbT", [128, S], BF16)
    Vb = dscr("Vb", [S, 128], BF16)
    iqT = dscr("iqT", [4, 128, S], BF16)
    ikT = dscr("ikT", [128, S], BF16)
    iwS = dscr("iwS", [S, 8], F32)
    oS = dscr("oS", [S, D], BF16)
    x1nT = dscr("x1nT", [8, 128, S], BF16)

    with ExitStack() as g:
        identb = al.sb(g, [128, 128], BF16, "identb")
        identf_sb = al.sb(g, [128, 128], F32, "identf")
        blk1b = al.sb(g, [128, 128], BF16, "blk1b")
        cmask_sb = al.sb(g, [128, 256], F32, "cmask")
        negtri = al.sb(g, [128, 128], F32, "negtri")
        b31c = al.sb(g, [128, 12], F32, "b31c")
        DThi = al.sb(g, [128, 12, 256], BF16, "DThi")
        qkg_sb = al.sb(g, [128, 4], F32, "qkg")
        neg_lam = al.sb(g, [128, 1], F32, "neglam")
        subln_row = al.sb(g, [128, 128], F32, "sublnrow")
        gattn_sb = al.sb(g, [128, 8], F32, "gattn")
        gffn_sb = al.sb(g, [128, 8], F32, "gffn")
        cw_sb = al.sb(g, [128, 44, 3], F32, "cw")
        cb_sb = al.sb(g, [128, 44], F32, "cb")
        thr_const = al.sb(g, [128, 1], F32, "thrc")
        eps_c = al.sb(g, [128, 1], F32, "epsc")
        fvec = al.sb(g, [128, NIT], F32, "fvec")

        with ExitStack() as es:
            tmpf = al.sb(es, [128, 128], F32, "tmpf")
            bT = al.sb(es, [128, 12, 256], F32, "bT")
            lv = al.sb(es, [128, 256], F32, "lv")
            lsum = al.sb(es, [128, 2], F32, "lsum")
            junk = al.sb(es, [128, 64], F32, "junk")
            sch.dma('sp', identf_sb[:], identf[:, :], writes=['identf'], chan='c0')
            sch.dma('sp', tmpf[:], blk1[:, :], writes=['tmpf'], chan='c1')
            sch.dma('sp', cmask_sb[:], cmask[:, :], writes=['cmask'], chan='c2')
            sch.dma('sp', b31c[:], b31[0:1, :].partition_broadcast(128), writes=['b31c'], chan='c3')
            sch.dma('sp', bT[:], biasT[:, :, :], writes=['bT'], chan='c4')
            sch.dma('sp', qkg_sb[:], qkg[:, :], writes=['qkg'], chan='c5')
            sch.dma('sp', lv[:], lamv[0:1, :].partition_broadcast(128), writes=['lv'], chan='c6')
            sch.dma('sp', subln_row[:], subln[0:1, :].partition_broadcast(128), writes=['subln'], chan='c7')
            sch.dma('sp', gattn_sb[:], g_attn_c[:, :], writes=['gattn'], chan='c8')
            sch.dma('sp', gffn_sb[:], g_ffn_c[:, :], writes=['gffn'], chan='c9')
            sch.dma('sp', cw_sb[:], conv_wc[:, :, :], writes=['cw'], chan='c10')
            sch.dma('sp', cb_sb[:], conv_bc[:, :], writes=['cb'], chan='c11')
            sch.op('dve', lambda E: E.tensor_copy(out=identb[:], in_=identf_sb[:]), ['identf'], ['identb'])
            sch.op('dve', lambda E: E.tensor_copy(out=blk1b[:], in_=tmpf[:]), ['tmpf'], ['blk1b'])
            sch.op('dve', lambda E: E.tensor_scalar(out=qkg_sb[:, 0:1], in0=qkg_sb[:, 0:1], scalar1=0.125, scalar2=None,
                                                    op0=ALU.mult), ['qkg'], ['qkg'])
            sch.op('dve', lambda E: E.tensor_scalar(out=qkg_sb[:, 2:3], in0=qkg_sb[:, 2:3], scalar1=0.125, scalar2=None,
                                                    op0=ALU.mult), ['qkg'], ['qkg'])
            sch.op('dve', lambda E: E.memset(thr_const[:], -1e29), [], ['thrc'])
            sch.op('dve', lambda E: E.memset(eps_c[:], EPS), [], ['epsc'])
            for k in range(NIT):
                sch.op('dve', lambda E, k=k: E.memset(fvec[:, k:k + 1], 0.5 ** (k + 1)), [], ['fvec'])
            for h in range(12):
                sch.op('dve', lambda E, h=h: E.scalar_tensor_tensor(
                    out=bT[:, h, :], in0=bT[:, h, :], scalar=b31c[:, h:h + 1], in1=cmask_sb[:],
                    op0=ALU.subtract, op1=ALU.add), ['bT', 'b31c', 'cmask'], ['bT'])
            sch.op('dve', lambda E: E.tensor_copy(out=DThi[:], in_=bT[:]), ['bT'], ['DThi'])
            for i in range(2):
                sch.op('dve', lambda E, i=i: E.scalar_tensor_tensor(
                    out=junk[:], in0=lv[:, 128 * i:128 * i + 64], scalar=1.0, in1=lv[:, 128 * i + 64:128 * i + 128],
                    op0=ALU.mult, op1=ALU.mult, accum_out=lsum[:, i:i + 1]), ['lv'], ['junk', 'lsum'])
            sch.op('act', lambda E: E.activation(out=lsum[:], in_=lsum[:], func=AF.Exp), ['lsum'], ['lsum'])
            sch.op('dve', lambda E: E.tensor_tensor(out=neg_lam[:], in0=lsum[:, 1:2], in1=lsum[:, 0:1], op=ALU.subtract),
                   ['lsum'], ['neglam'])
            sch.op('dve', lambda E: E.tensor_scalar(out=neg_lam[:], in0=neg_lam[:], scalar1=-LAM_INIT, scalar2=None,
                                                    op0=ALU.add), ['neglam'], ['neglam'])
            sch.op('dve', lambda E: E.tensor_scalar(out=subln_row[:], in0=subln_row[:], scalar1=1.0 - LAM_INIT,
                                                    scalar2=None, op0=ALU.mult), ['subln'], ['subln'])
            with ExitStack() as e2:
                pt = al.ps(e2, [128, 128], F32, "ptri")
                sch.op('pe', lambda E: E.transpose(out=pt[:], in_=cmask_sb[:, 0:128], identity=identf_sb[:]),
                       ['cmask', 'identf'], ['ptri'])
                sch.op('dve', lambda E: E.tensor_scalar(out=negtri[:], in0=pt[:], scalar1=1e30 / 30000.0, scalar2=None,
                                                        op0=ALU.mult), ['ptri'], ['negtri'])
                sch.barrier()

        with ExitStack() as es:
            sc = al.sb(es, [128, 8, NB], F32, "sc")
            wb = [al.sb(es, [128, 8, 512], F32, "wada%d" % i) for i in range(2)]
            mrow = al.sb(es, [NB, 6 * D], F32, "mrow")
            brow = al.sb(es, [NB, 6 * D], F32, "brow")
            pm = [al.ps(es, [128, 512], F32, "pm%d" % i) for i in range(2)]
            sch.dma('sp', sc[:], cT[:, :, :], writes=['sc'], chan='c0')
            sch.dma('sp', brow[:], b_ada[0:1, :].partition_broadcast(NB), writes=['brow'], chan='c1')
            sch.op('act', lambda E: E.activation(out=sc[:], in_=sc[:], func=AF.Silu), ['sc'], ['sc'])
            for cc in range(12):
                wt = wb[cc % 2]
                sch.dma('sp', wt[:], w_ada[:, cc * 512:(cc + 1) * 512].rearrange("(k p) n -> p k n", p=128),
                        writes=['wada%d' % (cc % 2)], chan='wada%d' % (cc % 2))

                def mm(E, wt=wt, cc=cc):
                    r = []
                    for k in range(8):
                        r.append(E.matmul(pm[cc % 2][0:NB, :], lhsT=sc[:, k, :], rhs=wt[:, k, :],
                                          start=(k == 0), stop=(k == 7)))
                    return r
                sch.op('pe', mm, ['sc', 'wada%d' % (cc % 2)], ['pm%d' % (cc % 2)])
                sch.op('dve', lambda E, cc=cc: E.tensor_tensor(
                    out=mrow[:, cc * 512:(cc + 1) * 512], in0=pm[cc % 2][0:NB, :], in1=brow[:, cc * 512:(cc + 1) * 512],
                    op=ALU.add), ['pm%d' % (cc % 2), 'brow'], ['mrow'])
            sch.dma('sp', modrow[:, :], mrow[:], reads=['mrow'], writes=['modrow'], chan='c2')
            sch.barrier()

        for b in range(NB):
            with ExitStack() as eb:
                modc = al.sb(eb, [128, 48], F32, "modc")
                a_attn = al.sb(eb, [128, 8], F32, "aattn")
                a_ffn = al.sb(eb, [128, 8], F32, "affn")
                with ExitStack() as em:
                    mt = al.sb(em, [48, 128], F32, "mt")
                    pmt = al.ps(em, [128, 48], F32, "pmt")
                    sch.dma('sp', mt[:], modrow[b, :].rearrange("(t p) -> t p", p=128), writes=['mt'], chan='modc')
                    sch.op('pe', lambda E: E.transpose(out=pmt[:], in_=mt[:], identity=identf_sb[0:48, 0:48]),
                           ['mt'], ['pmt'])
                    sch.op('dve', lambda E: E.tensor_copy(out=modc[:], in_=pmt[:]), ['pmt'], ['modc'])
                    sch.barrier()
                sch.op('dve', lambda E: E.scalar_tensor_tensor(out=a_attn[:], in0=modc[:, 8:16], scalar=1.0,
                                                               in1=gattn_sb[:], op0=ALU.add, op1=ALU.mult),
                       ['modc'], ['aattn'])
                sch.op('dve', lambda E: E.scalar_tensor_tensor(out=a_ffn[:], in0=modc[:, 32:40], scalar=1.0,
                                                               in1=gffn_sb[:], op0=ALU.add, op1=ALU.mult),
                       ['modc'], ['affn'])
                sch.barrier()
                phase_proj(nc, sch, al, locals())
                phase_attn_a(nc, sch, al, locals())
                phase_attn_b(nc, sch, al, locals())
                phase_f(nc, sch, al, locals())
        sch.barrier()
    return nc


def phase_proj(nc, sch, al, L):
    S, NT, NCH, b = L['S'], L['NT'], L['NCH'], L['b']
    x, w_in = L['x'], L['w_in']
    identb, blk1b, qkg_sb = L['identb'], L['blk1b'], L['qkg_sb']
    a_attn, modc = L['a_attn'], L['modc']
    with ExitStack() as es:
        win = al.sb(es, [128, 8, INC], BF16, "win")
        with ExitStack() as e2:
            stg = [al.sb(e2, [128, INC], F32, "wstg%d" % i) for i in range(2)]
            for k in range(8):
                sch.dma('sp', stg[k % 2][:], w_in[k * 128:(k + 1) * 128, :], writes=['wstg%d' % (k % 2)],
                        chan='wstg%d' % (k % 2))
                eng = 'dve' if k % 2 == 0 else 'pool'
                sch.op(eng, lambda E, k=k: E.tensor_copy(out=win[:, k, :], in_=stg[k % 2][:]),
                       ['wstg%d' % (k % 2)], ['win'])
            sch.barrier()
        xt = [al.sb(es, [128, D], F32, "xt%d" % i) for i in range(2)]
        xn = [al.sb(es, [128, D], BF16, "xn%d" % i) for i in range(2)]
        junk = al.sb(es, [128, D], BF16, "junk")
        ss = al.sb(es, [128, 2], F32, "ss")
        hT = [al.sb(es, [128, 8, 512], BF16, "hT%d" % i) for i in range(2)]
        qsb = [al.sb(es, [128, 512], F32, "qsb%d" % i) for i in range(3)]
        sq = [al.sb(es, [128, 512], BF16, "sq%d" % i) for i in range(3)]
        lr = [al.sb(es, [128, 512], F32, "lr%d" % i) for i in range(3)]
        stF = [al.sb(es, [128, 512], BF16, "stF%d" % i) for i in range(3)]
        stV = [al.sb(es, [128, 512], BF16, "stV%d" % i) for i in range(2)]
        stB = [al.sb(es, [128, 128], BF16, "stB%d" % i) for i in range(2)]
        stW = [al.sb(es, [128, 8], F32, "stW%d" % i) for i in range(2)]
        tp = [al.ps(es, [128, 8, 128], BF16, "tp%d" % i) for i in range(2)]
        pp = [al.ps(es, [128, 512], F32, "pp%d" % i) for i in range(3)]
        pss = [al.ps(es, [128, 512], F32, "pss%d" % i) for i in range(2)]
        pB = al.ps(es, [128, 136], F32, "pB")

        fm = []
        for t in range(4):
            fm.append((L['QaT'], t, 128 * t, 128, 0))
        for t in range(4):
            fm.append((L['KaT'], t, 512 + 128 * t, 128, 1))
        for t in range(4):
            fm.append((L['QbT'], t, 1536 + 128 * t, 128, 2))
        fm.append((L['KbT'], None, 2048, 128, 3))
        for t in range(4):
            fm.append((L['iqT'], t, 2304 + 128 * t, 128, None))
        fm.append((L['ikT'], None, 2816, 64, None))

        nfm = 0
        ntok = 0
        pend = [None]
        npss = [0]

        def prep_tile(c, tl):
            h = hT[c % 2]
            hn = 'hT%d' % (c % 2)
            tt = c * 4 + tl
            r = tt % 2
            sch.dma('sp', xt[r][:], x[b, tt * 128:(tt + 1) * 128, :], writes=['xt%d' % r], chan='xt%d' % r)
            sch.op('dve', lambda E, r=r: E.scalar_tensor_tensor(
                out=junk[:], in0=xt[r][:], scalar=1.0, in1=xt[r][:], op0=ALU.mult, op1=ALU.mult,
                accum_out=ss[:, 0:1]), ['xt%d' % r], ['junk', 'ss'])
            sch.op('act', lambda E: E.activation(out=ss[:, 1:2], in_=ss[:, 0:1], func=AF.Ln, scale=1.0 / D,
                                                 bias=L['eps_c'][:]), ['ss'], ['ss1'])
            sch.op('act', lambda E: E.activation(out=ss[:, 1:2], in_=ss[:, 1:2], func=AF.Exp, scale=-0.5),
                   ['ss1'], ['ss1'])
            sch.op('dve', lambda E, r=r: E.tensor_scalar(out=xn[r][:], in0=xt[r][:], scalar1=ss[:, 1:2],
                                                         scalar2=None, op0=ALU.mult),
                   ['xt%d' % r, 'ss1'], ['xn%d' % r])

        def prep_tile_b(c, tl):
            h = hT[c % 2]
            hn = 'hT%d' % (c % 2)
            tt = c * 4 + tl
            r = tt % 2

            def tr(E, r=r):
                res = []
                for k in range(8):
                    res.append(E.transpose(out=tp[r][:, k, :], in_=xn[r][:, k * 128:(k + 1) * 128],
                                           identity=identb[:]))
                return res
            sch.op('pe', tr, ['xn%d' % r, 'identb'], ['tp%d' % r])

            def ev(E, r=r, tl=tl, h=h):
                res = []
                for k in range(8):
                    res.append(E.activation(out=h[:, k, tl * 128:(tl + 1) * 128], in_=tp[r][:, k, :],
                                            func=AF.Identity, scale=a_attn[:, k:k + 1], bias=modc[:, k:k + 1]))
                return res
            sch.op('act', ev, ['tp%d' % r, 'aattn', 'modc'], [hn])

        for tl in range(4):
            prep_tile(0, tl)
            prep_tile_b(0, tl)
        for c in range(NCH):
            h = hT[c % 2]
            hn = 'hT%d' % (c % 2)
            nfm_c = 0
            for (dst, t, c0, nr, gi) in fm:
                if c + 1 < NCH and nfm_c in (0, 4, 8, 12):
                    prep_tile(c + 1, nfm_c // 4)
                if c + 1 < NCH and nfm_c in (3, 7, 11, 15):
                    prep_tile_b(c + 1, (nfm_c - 3) // 4)
                nfm_c += 1
                pr = nfm % 3
                nfm += 1

                def mm(E, c0=c0, nr=nr, pr=pr, h=h):
                    res = []
                    for k in range(8):
                        res.append(E.matmul(pp[pr][0:nr, :], lhsT=win[:, k, c0:c0 + nr], rhs=h[:, k, :],
                                            start=(k == 0), stop=(k == 7)))
                    return res
                sch.op('pe', mm, ['win', hn], ['pp%d' % pr])
                st = stF[pr]
                sn = 'stF%d' % pr
                if t is None:
                    dap = dst[0:nr, c * 512:(c + 1) * 512]
                else:
                    dap = dst[t, 0:nr, c * 512:(c + 1) * 512]
                if gi is None:
                    sch.op('act', lambda E, nr=nr, pr=pr, st=st: E.activation(out=st[0:nr, :], in_=pp[pr][0:nr, :],
                                                                              func=AF.Copy),
                           ['pp%d' % pr], [sn])

                    def e2(dap=dap, st=st, sn=sn, nr=nr):
                        sch.dma('pool', dap, st[0:nr, :], reads=[sn], writes=[], chan='st_' + sn)
                else:
                    q = nfm % 3
                    sch.op('act', lambda E, pr=pr, q=q: E.activation(out=qsb[q][:], in_=pp[pr][:], func=AF.Copy),
                           ['pp%d' % pr], ['qsb%d' % q])
                    sch.op('dve', lambda E, q=q: E.tensor_tensor(out=sq[q][:], in0=qsb[q][:], in1=qsb[q][:],
                                                                  op=ALU.mult), ['qsb%d' % q], ['sq%d' % q])
                    ps_ = npss[0] % 2
                    npss[0] += 1
                    sch.op('pe', lambda E, q=q, ps_=ps_: E.matmul(pss[ps_][:], lhsT=blk1b[:], rhs=sq[q][:], start=True,
                                                                  stop=True),
                           ['sq%d' % q, 'blk1b'], ['pss%d' % ps_])

                    def e2(dap=dap, st=st, sn=sn, nr=nr, q=q, gi=gi, ps_=ps_):
                        sch.op('act', lambda E: E.activation(out=lr[q][:], in_=pss[ps_][:], func=AF.Ln, scale=1.0 / 64,
                                                             bias=L['eps_c'][:]), ['pss%d' % ps_], ['lr%d' % q])
                        sch.op('act', lambda E: E.activation(out=lr[q][:], in_=lr[q][:], func=AF.Exp, scale=-0.5),
                               ['lr%d' % q], ['lr%d' % q])
                        sch.op('dve', lambda E: E.scalar_tensor_tensor(
                            out=st[:], in0=qsb[q][:], scalar=qkg_sb[:, gi:gi + 1], in1=lr[q][:], op0=ALU.mult,
                            op1=ALU.mult), ['qsb%d' % q, 'lr%d' % q, 'qkg'], [sn])
                        sch.dma('pool', dap, st[0:nr, :], reads=[sn], writes=[], chan='st_' + sn)
                if pend[0] is not None:
                    pend[0]()
                pend[0] = e2
            if pend[0] is not None:
                pend[0]()
                pend[0] = None

            for tl in range(4):
                tt = c * 4 + tl
                pr = nfm % 3
                nfm += 1
                v = ntok % 2
                ntok += 1

                def mmv(E, pr=pr, tl=tl, h=h):
                    res = []
                    for k in range(8):
                        res.append(E.matmul(pp[pr][:, :], lhsT=h[:, k, tl * 128:(tl + 1) * 128],
                                            rhs=win[:, k, 1024:1536], start=(k == 0), stop=(k == 7)))
                    return res
                sch.op('pe', mmv, ['win', hn], ['pp%d' % pr])
                sch.op('act', lambda E, pr=pr, v=v: E.activation(out=stV[v][:], in_=pp[pr][:], func=AF.Copy),
                       ['pp%d' % pr], ['stV%d' % v])
                sch.dma('pool', L['Va'][tt * 128:(tt + 1) * 128, :], stV[v][:], reads=['stV%d' % v], writes=[],
                        chan='st_stV%d' % v)

                def mmb(E, tl=tl, h=h):
                    res = []
                    for k in range(8):
                        res.append(E.matmul(pB[:, 0:128], lhsT=h[:, k, tl * 128:(tl + 1) * 128],
                                            rhs=win[:, k, 2176:2304], start=(k == 0), stop=(k == 7)))
                    for k in range(8):
                        res.append(E.matmul(pB[:, 128:136], lhsT=h[:, k, tl * 128:(tl + 1) * 128],
                                            rhs=win[:, k, 2880:2888], start=(k == 0), stop=(k == 7)))
                    return res
                sch.op('pe', mmb, ['win', hn], ['pB'])
                sch.op('dve', lambda E, v=v: E.tensor_copy(out=stB[v][:], in_=pB[:, 0:128]), ['pB'], ['stB%d' % v])
                sch.op('dve', lambda E, v=v: E.tensor_scalar(out=stW[v][:], in0=pB[:, 128:136], scalar1=512.0 ** -0.5,
                                                             scalar2=None, op0=ALU.mult), ['pB'], ['stW%d' % v])
                sch.dma('pool', L['Vb'][tt * 128:(tt + 1) * 128, :], stB[v][:], reads=['stB%d' % v], writes=[],
                        chan='st_stB%d' % v)
                sch.dma('pool', L['iwS'][tt * 128:(tt + 1) * 128, :], stW[v][:], reads=['stW%d' % v], writes=[],
                        chan='st_stW%d' % v)
        sch.barrier()


def phase_attn_a(nc, sch, al, L):
    S, NT, NCH, b = L['S'], L['NT'], L['NCH'], L['b']
    identb, DThi, b31c = L['identb'], L['DThi'], L['b31c']
    neg_lam, subln_row, eps_c = L['neg_lam'], L['subln_row'], L['eps_c']
    QaT, KaT, Va, oS = L['QaT'], L['KaT'], L['Va'], L['oS']
    with ExitStack() as es:
        Ka = al.sb(es, [128, 4, S], BF16, "Ka")
        Vs = al.sb(es, [128, NT, 4, 130], BF16, "Vs")
        sch.op('pool', lambda E: E.memset(Vs[:], 1.0), [], ['Vs'])
        for t in range(4):
            sch.dma('sp', Ka[:, t, :], KaT[t, :, :], writes=['Ka%d' % t], chan='ldK')
        for j0 in range(NT):
            sch.dma('sp', Vs[:, j0, :, 0:128],
                    Va[j0 * 128:(j0 + 1) * 128, :].rearrange("p (h e) -> p h e", h=4),
                    reads=['Vs'], writes=['Vs%d' % j0], chan='ldV')
        sch.barrier()
        Qc = [al.sb(es, [128, 512], BF16, "Qc%d" % i) for i in range(2)]
        pt = [al.sb(es, [128, 2, 512], BF16, "pt%d" % i) for i in range(3)]
        ost = [al.sb(es, [128, 4, 128], BF16, "ost%d" % i) for i in range(2)]
        rr = [al.sb(es, [128, 4], F32, "rr%d" % i) for i in range(2)]
        t0 = [al.sb(es, [128, 128], F32, "t0%d" % i) for i in range(2)]
        dd = [al.sb(es, [128, 128], F32, "dd%d" % i) for i in range(2)]
        junk = al.sb(es, [128, 128], F32, "junkA")
        accs = [al.sb(es, [128, 2, 2, 2, 129], F32, "accs%d" % i) for i in range(2)]
        st = [al.ps(es, [128, 2, 512], F32, "st%d" % i) for i in range(2)]
        acc = [[al.ps(es, [128, 2, 256], F32, "acc%d%d" % (m, q)) for q in range(2)] for m in range(2)]
        nst = 0
        nq = 0
        nep = [0]
        blocks = []
        for h in range(4):
            for c in range(NCH):
                cq = nq % 2
                nq += 1
                nj = 4 * c + 4
                oq = (h * NCH + c) % 2

                def pre(cq=cq, h=h, c=c):
                    sch.dma('sp', Qc[cq][:], QaT[h, :, c * 512:(c + 1) * 512], writes=['Qc%d' % cq], chan='Qc%d' % cq)

                def epi(h=h, c=c, oq=oq):
                    aq = nep[0] % 2
                    nep[0] += 1
                    A_ = accs[aq]
                    an = 'accs%d' % aq
                    for m in range(2):
                        for q_ in range(2):
                            sch.op('dve', lambda E, m=m, q_=q_: E.tensor_copy(out=A_[:, m, q_, :, :],
                                                                              in_=acc[m][q_][:, :, 0:129]),
                                   [('acc', m, q_)], [an])

                    def rest(h=h, c=c, oq=oq, A_=A_, an=an):
                        epi_rest(h, c, oq, A_, an)
                    return rest

                def epi_rest(h, c, oq, A_, an):
                    for li in range(4):
                        e = li % 2
                        a0 = A_[:, 0, li // 2, li % 2, :]
                        a1 = A_[:, 1, li // 2, li % 2, :]
                        sch.op('dve', lambda E, e=e, a0=a0: E.reciprocal(out=rr[e][:, 0:1], in_=a0[:, 128:129]),
                               [an], ['rr%d' % e])
                        sch.op('dve', lambda E, e=e, a1=a1: E.reciprocal(out=rr[e][:, 1:2], in_=a1[:, 128:129]),
                               [an], ['rr%d' % e])
                        sch.op('dve', lambda E, e=e: E.tensor_tensor(out=rr[e][:, 1:2], in0=rr[e][:, 1:2], in1=neg_lam[:],
                                                                     op=ALU.mult), ['rr%d' % e], ['rr%d' % e])
                        sch.op('dve', lambda E, e=e, a0=a0: E.tensor_scalar(out=t0[e][:], in0=a0[:, 0:128],
                                                                            scalar1=rr[e][:, 0:1], scalar2=None,
                                                                            op0=ALU.mult),
                               [an, 'rr%d' % e], ['t0%d' % e])
                        sch.op('dve', lambda E, e=e, a1=a1: E.scalar_tensor_tensor(
                            out=dd[e][:], in0=a1[:, 0:128], scalar=rr[e][:, 1:2], in1=t0[e][:], op0=ALU.mult,
                            op1=ALU.add), [an, 'rr%d' % e, 't0%d' % e], ['dd%d' % e])
                        sch.op('dve', lambda E, e=e: E.scalar_tensor_tensor(
                            out=junk[:], in0=dd[e][:], scalar=1.0, in1=dd[e][:], op0=ALU.mult, op1=ALU.mult,
                            accum_out=rr[e][:, 2:3]), ['dd%d' % e], ['junkA', 'rs%d' % e])
                        sch.op('act', lambda E, e=e: E.activation(out=rr[e][:, 3:4], in_=rr[e][:, 2:3], func=AF.Ln,
                                                                  scale=1.0 / 128, bias=eps_c[:]), ['rs%d' % e], ['rt%d' % e])
                        sch.op('act', lambda E, e=e: E.activation(out=rr[e][:, 3:4], in_=rr[e][:, 3:4], func=AF.Exp,
                                                                  scale=-0.5), ['rt%d' % e], ['rt%d' % e])
                        sch.op('dve', lambda E, e=e, li=li, oq=oq: E.scalar_tensor_tensor(
                            out=ost[oq][:, li, :], in0=dd[e][:], scalar=rr[e][:, 3:4], in1=subln_row[:], op0=ALU.mult,
                            op1=ALU.mult), ['dd%d' % e, 'rt%d' % e], ['ost%d' % oq])
                    sch.dma('pool', oS[c * 512:(c + 1) * 512, h * 128:(h + 1) * 128].rearrange("(li p) e -> p li e", p=128),
                            ost[oq][:], reads=['ost%d' % oq], writes=[], chan='st_ost%d' % oq)

                for j in range(nj):
                    lo = max(0, j - 4 * c)
                    off = lo * 128
                    segs = [(j + dl - 4 * c, dl) for dl in (0, 1) if 0 <= j + dl - 4 * c <= 3]
                    r = nst % 2
                    p_ = nst % 3
                    nst += 1

                    def qk(E, r=r, j=j, off=off, segs=segs, cq=cq, h=h):
                        res = [E.matmul(st[r][:, m, off:512], lhsT=Ka[64 * m:64 * m + 64, h, j * 128:(j + 1) * 128],
                                        rhs=Qc[cq][64 * m:64 * m + 64, off:512], start=True, stop=(not segs))
                               for m in range(2)]
                        if segs:
                            c0 = segs[0][0] * 128
                            n = 128 * len(segs)
                            d0 = segs[0][1] * 128
                            for m in range(2):
                                for T in (DThi,):
                                    res.append(E.matmul(st[r][:, m, c0:c0 + n], lhsT=identb[:], rhs=T[:, h, d0:d0 + n],
                                                        start=False, stop=True))
                        return res

                    def pv(E, p_=p_, j=j, lo=lo, c=c, h=h):
                        res = []
                        for m in range(2):
                            for li in range(lo, 4):
                                res.append(E.matmul(acc[m][li // 2][:, li % 2, 0:129],
                                                    lhsT=pt[p_][:, m, li * 128:(li + 1) * 128],
                                                    rhs=Vs[:, j, h, 0:129], start=(j == 0 and li % 2 == 0),
                                                    stop=(j == 4 * c + li), skip_group_check=True))
                        return res
                    s0 = lambda qk=qk, cq=cq, r=r: sch.op('pe', qk, ['Qc%d' % cq], ['st%d' % r])
                    s1 = lambda r=r, p_=p_, off=off, h=h: sch.op('act', lambda E: E.activation(
                        out=pt[p_][:, :, off:512], in_=st[r][:, :, off:512], func=AF.Exp, bias=b31c[:, h:h + 1]),
                        ['st%d' % r], ['pt%d' % p_])
                    s2 = lambda pv=pv, p_=p_, lo=lo: sch.op(
                        'pe', pv, ['pt%d' % p_], sorted(set(('acc', m, li // 2) for m in range(2) for li in range(lo, 4))))
                    blk = Blk(s0, s1, s2, pre=(pre if j == 0 else None))
                    if j == nj - 1:
                        def post(blk=blk, epi=epi):
                            blk.post2 = epi()
                        blk.post = post
                    blocks.append(blk)
        run_pipe(blocks, 1)
        sch.barrier()


def phase_attn_b(nc, sch, al, L):
    S, NT, NCH, b = L['S'], L['NT'], L['NCH'], L['b']
    TK = min(TOPK, S // 4)
    identb, identf_sb, DThi, b31c = L['identb'], L['identf_sb'], L['DThi'], L['b31c']
    negtri, thr_const, fvec = L['negtri'], L['thr_const'], L['fvec']
    QbT, KbT, Vb, iqT, ikT, iwS, oS = L['QbT'], L['KbT'], L['Vb'], L['iqT'], L['ikT'], L['iwS'], L['oS']
    with ExitStack() as es:
        Kb = al.sb(es, [128, 2, S], BF16, "Kb")
        Vs = al.sb(es, [128, NT, 2, 66], BF16, "VsB")
        ik2 = al.sb(es, [128, S], BF16, "ik2")
        iw = al.sb(es, [128, NT, 8], F32, "iw")
        sch.op('pool', lambda E: E.memset(Vs[:], 1.0), [], ['VsB'])
        for half in range(2):
            for g in range(2):
                sch.dma('sp', Kb[64 * half:64 * half + 64, g, :], KbT[64 * g:64 * g + 64, :],
                        writes=['Kb%d%d' % (half, g)], chan='ldK')
            sch.dma('sp', ik2[64 * half:64 * half + 64, :], ikT[0:64, :], writes=['ik2%d' % half], chan='ldK')
        for j0 in range(NT):
            sch.dma('sp', iw[:, j0, :], iwS[j0 * 128:(j0 + 1) * 128, :], writes=['iw%d' % j0], chan='ldK')
        for j0 in range(NT):
            sch.dma('sp', Vs[:, j0, :, 0:64],
                    Vb[j0 * 128:(j0 + 1) * 128, :].rearrange("p (g e) -> p g e", g=2),
                    reads=['VsB'], writes=['VsB%d' % j0], chan='ldV')
        sch.barrier()
        NI = 4
        Ib = [al.sb(es, [128, S], F32, "I%d" % i) for i in range(NI)]
        ma = [al.sb(es, [128, S], BF16, "ma%d" % i) for i in range(NI)]
        maT = al.sb(es, [128, NT, 512], BF16, "maT")
        Rb = [al.sb(es, [128, 2, 512], BF16, "R%d" % i) for i in range(2)]
        dg = [al.sb(es, [128, 8, 128], BF16, "dg%d" % i) for i in range(2)]
        iqc = [al.sb(es, [128, 4, 128], BF16, "iqc%d" % i) for i in range(2)]
        Qc = [al.sb(es, [128, 4, 512], BF16, "QcB%d" % i) for i in range(2)]
        pt = [al.sb(es, [128, 2, 512], BF16, "ptB%d" % i) for i in range(3)]
        ost = [al.sb(es, [128, 4, 512], BF16, "ostB%d" % i) for i in range(1)]
        bs = [al.sb(es, [128, 8], F32, "bs%d" % i) for i in range(NI)]
        rcp = al.sb(es, [128, 2, 4], F32, "rcp")
        bw = [al.sb(es, [128, NIT], F32, "bw%d" % i) for i in range(NI)]
        PP = [al.ps(es, [128, 2, 512], F32, "PP%d" % i) for i in range(2)]
        pacc = al.ps(es, [128, 512], F32, "pacc")
        tpb = al.ps(es, [128, 8, 128], BF16, "tpb")
        accb = al.ps(es, [128, 2, 4, 128], F32, "accb")
        cnt = {'x': 0, 'R': 0, 'pt': 0, 'sb': 0, 'acc': 0}

        def idx_chunk(c):
            blocks = []
            pending = []
            for li in range(4):
                i = 4 * c + li
                ib = i % NI
                q = i % 2
                Li = 128 * (i + 1)

                def pre(i=i, q=q):
                    sch.dma('sp', iqc[q][:], iqT[:, :, i * 128:(i + 1) * 128].rearrange("t p n -> p t n"),
                            writes=['iqc%d' % q], chan='iqc%d' % q)
                    for hh in range(8):
                        sch.op('pool', lambda E, hh=hh: E.tensor_scalar(
                            out=dg[q][:, hh, :], in0=identf_sb[:], scalar1=iw[:, i, hh:hh + 1], scalar2=None,
                            op0=ALU.mult), [], ['dg%d' % q])

                def gen_tile(i=i, ib=ib, Li=Li):
                    sch.op('dve', lambda E: E.tensor_tensor(
                        out=Ib[ib][:, i * 128:(i + 1) * 128], in0=Ib[ib][:, i * 128:(i + 1) * 128], in1=negtri[:],
                        op=ALU.add), ['I%d' % ib], ['I%d' % ib])
                    B = bs[ib]
                    bn = 'bs%d' % ib
                    if i >= TK // 128:
                        W = bw[ib]
                        sch.op('dve', lambda E: E.tensor_reduce(
                            out=B[:, 0:1], in_=Ib[ib][:, 0:i * 128], axis=mybir.AxisListType.X, op=ALU.min),
                            ['I%d' % ib], [bn])
                        sch.op('dve', lambda E: E.tensor_reduce(
                            out=B[:, 1:2], in_=Ib[ib][:, 0:Li], axis=mybir.AxisListType.X, op=ALU.max),
                            ['I%d' % ib], [bn])
                        yield
                        sch.op('dve', lambda E: E.tensor_tensor(out=B[:, 1:2], in0=B[:, 1:2], in1=B[:, 0:1],
                                                                op=ALU.subtract), [bn], [bn])
                        sch.op('dve', lambda E: E.tensor_scalar(out=W[:], in0=fvec[:], scalar1=B[:, 1:2], scalar2=None,
                                                                op0=ALU.mult), [bn], [bn + 'w'])
                        sch.op('dve', lambda E: E.tensor_tensor(out=B[:, 2:3], in0=B[:, 0:1], in1=W[:, 0:1], op=ALU.add),
                               [bn, bn + 'w'], [bn])
                        yield
                        for k in range(NIT):
                            sch.op('dve', lambda E: E.tensor_scalar(
                                out=ma[ib][:, 0:Li], in0=Ib[ib][:, 0:Li], scalar1=B[:, 2:3], scalar2=None, op0=ALU.is_ge,
                                op1=ALU.add, accum_out=B[:, 3:4]), [bn, 'I%d' % ib], [bn, 'ma%d' % ib])
                            yield
                            last = (k == NIT - 1)
                            sch.op('dve', lambda E, last=last: E.tensor_scalar(
                                out=B[:, 4:5], in0=B[:, 3:4], scalar1=float(TK), scalar2=(-1.0 if last else -0.5),
                                op0=ALU.is_ge, op1=ALU.add), [bn], [bn])
                            yield
                            sch.op('dve', lambda E, k=k, last=last: E.scalar_tensor_tensor(
                                out=(B[:, 5:6] if last else B[:, 2:3]), in0=B[:, 4:5], scalar=W[:, k:k + 1], in1=B[:, 2:3],
                                op0=ALU.mult, op1=ALU.add), [bn, bn + 'w'], [bn])
                            yield
                        thr = B[:, 5:6]
                    else:
                        thr = thr_const[:, 0:1]
                    sch.op('dve', lambda E: E.tensor_scalar(
                        out=ma[ib][:, 0:Li], in0=Ib[ib][:, 0:Li], scalar1=thr, scalar2=NEG, op0=ALU.is_lt, op1=ALU.mult),
                        [bn, 'I%d' % ib], ['ma%d' % ib])

                def post_tile(li=li, gen_tile=gen_tile):
                    pending.append(gen_tile())
                    if li % 2 == 1:
                        gens = list(pending)
                        del pending[:]
                        while gens:
                            for g_ in list(gens):
                                try:
                                    next(g_)
                                except StopIteration:
                                    gens.remove(g_)

                nsc = (Li + 511) // 512
                for si, sc0 in enumerate(range(0, Li, 512)):
                    w = min(512, Li - sc0)
                    for t in range(4):
                        xr = cnt['x'] % 2
                        cnt['x'] += 1
                        rq = cnt['R'] % 2
                        cnt['R'] += 1
                        s0 = lambda xr=xr, q=q, t=t, sc0=sc0, w=w: sch.op('pe', lambda E: [E.matmul(
                            PP[xr][:, e, 0:w], lhsT=iqc[q][64 * e:64 * e + 64, t, :], rhs=ik2[64 * e:64 * e + 64, sc0:sc0 + w],
                            start=True, stop=True) for e in range(2)], ['iqc%d' % q], ['PP%d' % xr])
                        s1 = lambda xr=xr, rq=rq, w=w: sch.op('act', lambda E: E.activation(
                            out=Rb[rq][:, :, 0:w], in_=PP[xr][:, :, 0:w], func=AF.Relu), ['PP%d' % xr], ['R%d' % rq])
                        s2 = lambda rq=rq, q=q, t=t, w=w: sch.op('pe', lambda E: [E.matmul(
                            pacc[:, 0:w], lhsT=dg[q][:, 2 * t + e, :], rhs=Rb[rq][:, e, 0:w], start=(t == 0 and e == 0),
                            stop=(t == 3 and e == 1)) for e in range(2)], ['R%d' % rq, 'dg%d' % q], ['pacc'])
                        post = None
                        if t == 3:
                            last = (si == nsc - 1)

                            def post(ib=ib, sc0=sc0, w=w, last=last, post_tile=post_tile):
                                sch.op('act', lambda E: E.activation(out=Ib[ib][:, sc0:sc0 + w], in_=pacc[:, 0:w],
                                                                     func=AF.Copy), ['pacc'], ['I%d' % ib])
                                if last:
                                    post_tile()
                        blocks.append(Blk(s0, s1, s2, pre=(pre if (si == 0 and t == 0) else None), post=post))
            run_pipe(blocks, 1)

        def tr_chunk(c):
            for li in range(4):
                i = 4 * c + li
                ib = i % NI
                for j0 in range(0, i + 1, 8):
                    n = min(8, i + 1 - j0)

                    def tr(E, ib=ib, j0=j0, n=n):
                        return [E.transpose(out=tpb[:, jj, :], in_=ma[ib][:, (j0 + jj) * 128:(j0 + jj + 1) * 128],
                                            identity=identb[:]) for jj in range(n)]
                    sch.op('pe', tr, ['ma%d' % ib], ['tpb'])
                    sch.op('dve', lambda E, j0=j0, n=n, li=li: E.tensor_copy(
                        out=maT[:, j0:j0 + n, li * 128:(li + 1) * 128], in_=tpb[:, 0:n, :]), ['tpb'], ['maT'])

        def attn_chunk(c):
            cq = c % 2
            blocks = []
            nj = 4 * c + 4

            def pre():
                sch.dma('sp', Qc[cq][:], QbT[:, :, c * 512:(c + 1) * 512].rearrange("t p n -> p t n"),
                        writes=['QcB%d' % cq], chan='QcB%d' % cq)
            for t in range(4):
                g = t // 2

                def epi(t=t):
                    sch.op('act', lambda E: E.activation(out=rcp[:], in_=accb[:, :, :, 64], func=AF.Ln),
                           ['accb'], ['rcp'])
                    sch.op('act', lambda E: E.activation(out=rcp[:], in_=rcp[:], func=AF.Exp, scale=-1.0),
                           ['rcp'], ['rcp'])
                    sch.op('act', lambda E: [E.activation(
                        out=ost[0][:, li, (2 * t + e) * 64:(2 * t + e + 1) * 64], in_=accb[:, e, li, 0:64], func=AF.Copy,
                        scale=rcp[:, e, li:li + 1]) for e in range(2) for li in range(4)], ['accb', 'rcp'], ['ostB0'])
                    if t == 3:
                        sch.dma('pool', oS[c * 512:(c + 1) * 512, 512:1024].rearrange("(li p) e -> p li e", p=128),
                                ost[0][:], reads=['ostB0'], writes=[], chan='st_ostB0')
                for j in range(nj):
                    lo = max(0, j - 4 * c)
                    off = lo * 128
                    segs = [(j + dl - 4 * c, dl) for dl in (0, 1) if 0 <= j + dl - 4 * c <= 3]
                    r = cnt['x'] % 2
                    cnt['x'] += 1
                    p_ = cnt['pt'] % 3
                    cnt['pt'] += 1

                    def qk(E, r=r, j=j, off=off, segs=segs, t=t, g=g):
                        res = [E.matmul(PP[r][:, e, off:512], lhsT=Kb[64 * e:64 * e + 64, g, j * 128:(j + 1) * 128],
                                        rhs=Qc[cq][64 * e:64 * e + 64, t, off:512], start=True, stop=False)
                               for e in range(2)]
                        for e in range(2):
                            res.append(E.matmul(PP[r][:, e, off:512], lhsT=identb[:], rhs=maT[:, j, off:512], start=False,
                                                stop=(not segs)))
                        if segs:
                            c0 = segs[0][0] * 128
                            n = 128 * len(segs)
                            d0 = segs[0][1] * 128
                            for e in range(2):
                                for T in (DThi,):
                                    res.append(E.matmul(PP[r][:, e, c0:c0 + n], lhsT=identb[:],
                                                        rhs=T[:, 4 + 2 * t + e, d0:d0 + n], start=False, stop=True))
                        return res

                    def pv(E, p_=p_, j=j, lo=lo, g=g):
                        return [E.matmul(accb[:, e, li, 0:65], lhsT=pt[p_][:, e, li * 128:(li + 1) * 128],
                                         rhs=Vs[:, j, g, 0:65], start=(j == 0 and li == 0), stop=(j == 4 * c + li),
                                         skip_group_check=True) for e in range(2) for li in range(lo, 4)]
                    s0 = lambda qk=qk, r=r: sch.op('pe', qk, ['QcB%d' % cq, 'maT'], ['PP%d' % r])
                    s1 = lambda r=r, p_=p_, off=off, t=t: sch.op('act', lambda E: [E.activation(
                        out=pt[p_][:, e, off:512], in_=PP[r][:, e, off:512], func=AF.Exp,
                        bias=b31c[:, 4 + 2 * t + e:5 + 2 * t + e]) for e in range(2)], ['PP%d' % r], ['ptB%d' % p_])
                    s2 = lambda pv=pv, p_=p_: sch.op('pe', pv, ['ptB%d' % p_], ['accb'])
                    blocks.append(Blk(s0, s1, s2, pre=(pre if (t == 0 and j == 0) else None),
                                      post=(epi if j == nj - 1 else None)))
            run_pipe(blocks, 1)

        idx_chunk(0)
        tr_chunk(0)
        for c in range(NCH):
            if c + 1 < NCH:
                idx_chunk(c + 1)
            attn_chunk(c)
            if c + 1 < NCH:
                tr_chunk(c + 1)
        sch.barrier()


def phase_f(nc, sch, al, L):
    with ExitStack() as es:
        wu = al.sb(es, [128, 8, 2 * DFF], BF16, "wu")
        wd = al.sb(es, [128, 22, D], BF16, "wd")
        phase_f1(nc, sch, al, L, wu, wd)
        phase_f2(nc, sch, al, L, wu, wd)


def phase_f1(nc, sch, al, L, wu, wd):
    S, NT, NCH, b = L['S'], L['NT'], L['NCH'], L['b']
    w_up, w_down = L['w_up'], L['w_down']
    identb, eps_c, a_ffn, modc = L['identb'], L['eps_c'], L['a_ffn'], L['modc']
    x, out, oS, x1nT, w_out, modrow = L['x'], L['out'], L['oS'], L['x1nT'], L['w_out'], L['modrow']
    with ExitStack() as es:
        wo = al.sb(es, [128, 8, D], BF16, "wo")
        with ExitStack() as e2:
            grow = al.sb(e2, [128, D], F32, "grow")
            stg = [al.sb(e2, [128, D], F32, "wos%d" % i) for i in range(2)]
            sch.dma('sp', grow[:], modrow[b:b + 1, 2048:3072].partition_broadcast(128), writes=['grow'], chan='grow')
            for k in range(8):
                sch.dma('sp', stg[k % 2][:], w_out[k * 128:(k + 1) * 128, :], writes=['wos%d' % (k % 2)],
                        chan='wos%d' % (k % 2))
                sch.op('dve', lambda E, k=k: E.tensor_tensor(out=wo[:, k, :], in0=stg[k % 2][:], in1=grow[:], op=ALU.mult),
                       ['wos%d' % (k % 2), 'grow'], ['wo'])
            sch.barrier()
        ot = [al.sb(es, [128, D], BF16, "ot%d" % i) for i in range(2)]
        oT = [al.sb(es, [128, 8, 128], BF16, "oT%d" % i) for i in range(2)]
        xt = [al.sb(es, [128, D], F32, "xtf%d" % i) for i in range(2)]
        x1 = [al.sb(es, [128, D], F32, "x1%d" % i) for i in range(2)]
        x1n = [al.sb(es, [128, D], BF16, "x1n%d" % i) for i in range(2)]
        x1s = [al.sb(es, [128, 8, 128], BF16, "x1s%d" % i) for i in range(2)]
        junk = al.sb(es, [128, D], BF16, "junkF")
        ss = [al.sb(es, [128, 2], F32, "ssf%d" % i) for i in range(2)]
        tpo = [al.ps(es, [128, 8, 128], BF16, "tpo%d" % i) for i in range(2)]
        tpn = [al.ps(es, [128, 8, 128], BF16, "tpn%d" % i) for i in range(2)]
        pso = [[al.ps(es, [128, 512], F32, "pso%d%d" % (i, hf)) for hf in range(2)] for i in range(2)]
        growf = al.sb(es, [128, D], F32, "growf")
        pstg = [al.sb(es, [128, 1408], F32, "pstg%d" % i) for i in range(2)]
        sch.dma('sp', growf[:], modrow[b:b + 1, 5120:6144].partition_broadcast(128), writes=['growf'], chan='growf')
        steps = []
        for k in range(8):
            for q4 in range(4):
                steps.append(('cast', w_up[k * 128:(k + 1) * 128, q4 * 1408:(q4 + 1) * 1408],
                              wu[:, k, q4 * 1408:(q4 + 1) * 1408]))
        for ct in range(22):
            steps.append(('mul', w_down[ct * 128:(ct + 1) * 128, :], wd[:, ct, :]))
        nstep = [0]

        def prep_step():
            if nstep[0] >= len(steps):
                return
            kind, src, dst = steps[nstep[0]]
            q = nstep[0] % 2
            n_ = nstep[0]
            nstep[0] += 1
            if kind == 'cast':
                sch.dma('sp', pstg[q][:], src, writes=['pstg%d' % q], chan='pstg%d' % q)
                sch.op('act', lambda E: E.activation(out=dst, in_=pstg[q][:], func=AF.Copy), ['pstg%d' % q],
                       ['wprep%d' % n_])
            else:
                sch.dma('sp', pstg[q][:, 0:D], src, writes=['pstg%d' % q], chan='pstg%d' % q)
                sch.op('dve', lambda E: E.tensor_tensor(out=dst, in0=pstg[q][:, 0:D], in1=growf[:], op=ALU.mult),
                       ['pstg%d' % q, 'growf'], ['wprep%d' % n_])
        def part_a(tt):
            r = tt % 2
            rows = slice(tt * 128, (tt + 1) * 128)
            sch.dma('sp', ot[r][:], oS[rows, :], writes=['ot%d' % r], chan='ot%d' % r)
            sch.dma('sp', xt[r][:], x[b, rows, :], writes=['xtf%d' % r], chan='xtf%d' % r)
            sch.op('pe', lambda E, r=r: [E.transpose(out=tpo[r][:, k, :], in_=ot[r][:, k * 128:(k + 1) * 128],
                                                     identity=identb[:]) for k in range(8)],
                   ['ot%d' % r], ['tpo%d' % r])
            sch.op('act', lambda E, r=r: E.activation(out=oT[r][:], in_=tpo[r][:], func=AF.Copy),
                   ['tpo%d' % r], ['oT%d' % r])
            for hf in range(2):
                sch.op('pe', lambda E, r=r, hf=hf: [E.matmul(pso[r][hf][:], lhsT=oT[r][:, k, :],
                                                             rhs=wo[:, k, hf * 512:(hf + 1) * 512],
                                                             start=(k == 0), stop=(k == 7)) for k in range(8)],
                       ['oT%d' % r], ['pso%d%d' % (r, hf)])

        def part_a2(tt):
            r = tt % 2
            rows = slice(tt * 128, (tt + 1) * 128)
            for hf in range(2):
                sch.op('dve', lambda E, r=r, hf=hf: E.tensor_tensor(
                    out=x1[r][:, hf * 512:(hf + 1) * 512], in0=pso[r][hf][:], in1=xt[r][:, hf * 512:(hf + 1) * 512],
                    op=ALU.add), ['pso%d%d' % (r, hf), 'xtf%d' % r], ['x1%d' % r])
            sch.dma('pool', out[b, rows, :], x1[r][:], reads=['x1%d' % r], writes=[], chan='st_x1%d' % r)
            sch.op('dve', lambda E, r=r: E.scalar_tensor_tensor(
                out=junk[:], in0=x1[r][:], scalar=1.0, in1=x1[r][:], op0=ALU.mult, op1=ALU.mult,
                accum_out=ss[r][:, 0:1]), ['x1%d' % r], ['junkF', 'ssf%d' % r])
            sch.op('act', lambda E, r=r: E.activation(out=ss[r][:, 1:2], in_=ss[r][:, 0:1], func=AF.Ln, scale=1.0 / D,
                                                      bias=eps_c[:]), ['ssf%d' % r], ['ssg%d' % r])
            sch.op('act', lambda E, r=r: E.activation(out=ss[r][:, 1:2], in_=ss[r][:, 1:2], func=AF.Exp, scale=-0.5),
                   ['ssg%d' % r], ['ssg%d' % r])
            sch.op('dve', lambda E, r=r: E.tensor_scalar(out=x1n[r][:], in0=x1[r][:], scalar1=ss[r][:, 1:2],
                                                         scalar2=None, op0=ALU.mult),
                   ['x1%d' % r, 'ssg%d' % r], ['x1n%d' % r])

        def part_b(tt):
            r = tt % 2
            rows = slice(tt * 128, (tt + 1) * 128)
            sch.op('pe', lambda E, r=r: [E.transpose(out=tpn[r][:, k, :], in_=x1n[r][:, k * 128:(k + 1) * 128],
                                                     identity=identb[:]) for k in range(8)],
                   ['x1n%d' % r], ['tpn%d' % r])
            sch.op('act', lambda E, r=r: [E.activation(out=x1s[r][:, k, :], in_=tpn[r][:, k, :], func=AF.Identity,
                                                       scale=a_ffn[:, k:k + 1], bias=modc[:, 24 + k:25 + k])
                                          for k in range(8)], ['tpn%d' % r], ['x1s%d' % r])
            sch.dma('pool', x1nT[:, :, rows].rearrange("k p n -> p k n"), x1s[r][:], reads=['x1s%d' % r], writes=[],
                    chan='st_x1s%d' % r)

        for tt in range(NT + 2):
            if tt < NT:
                for _ in range((len(steps) + NT - 1) // NT):
                    prep_step()
                part_a(tt)
            if 1 <= tt <= NT:
                part_a2(tt - 1)
            if tt >= 2:
                part_b(tt - 2)
        sch.barrier()


def phase_f2(nc, sch, al, L, wu, wd):
    S, NT, NCH, b = L['S'], L['NT'], L['NCH'], L['b']
    cw, cb = L['cw_sb'], L['cb_sb']
    out, x1nT = L['out'], L['x1nT']
    with ExitStack() as es:
        gT = al.sb(es, [128, 22, 512], BF16, "gT")
        gpre = al.sb(es, [128, 2, 2, 512], BF16, "gpre")
        xc = [al.sb(es, [128, 8, 512], BF16, "xc%d" % i) for i in range(2)]
        u = [al.sb(es, [128, 514], F32, "u%d" % i) for i in range(2)]
        y = [al.sb(es, [128, 512], F32, "y%d" % i) for i in range(3)]
        sg = [al.sb(es, [128, 512], BF16, "sg%d" % i) for i in range(2)]
        x1t = [al.sb(es, [128, D], F32, "x1t%d" % i) for i in range(2)]
        hal = al.sb(es, [128, 44, 2], F32, "hal")
        pu = [al.ps(es, [128, 512], F32, "pu%d" % i) for i in range(4)]
        pd = [[al.ps(es, [128, 512], F32, "pd%d%d" % (i, hf)) for hf in range(2)] for i in range(2)]
        sch.op('pool', lambda E: E.memset(hal[:], 0.0), [], ['hal'])
        nu = 0
        ny = 0
        nt_ = 0
        def load_xc(c):
            sch.dma('sp', xc[c % 2][:], x1nT[:, :, c * 512:(c + 1) * 512].rearrange("k p n -> p k n"),
                    writes=['xc%d' % (c % 2)], chan='xc%d' % (c % 2))

        cntf = {'u': 0, 'y': 0}
        NPRE = 2

        def up_pair(c, ct):
            r = c % 2
            ys = []
            for hf in range(2):
                col = hf * 22 + ct
                p_ = cntf['u'] % 4
                uq = cntf['u'] % 2
                cntf['u'] += 1
                yq = cntf['y'] % 3
                cntf['y'] += 1
                ys.append(yq)
                sch.op('pe', lambda E, p_=p_, col=col, r=r: [E.matmul(
                    pu[p_][:], lhsT=wu[:, k, col * 128:(col + 1) * 128], rhs=xc[r][:, k, :], start=(k == 0),
                    stop=(k == 7)) for k in range(8)], ['xc%d' % r], ['pu%d' % p_])
                sch.op('act', lambda E, p_=p_, uq=uq: E.activation(out=u[uq][:, 2:514], in_=pu[p_][:], func=AF.Copy),
                       ['pu%d' % p_], ['u%d' % uq])
                sch.op('act', lambda E, p_=p_, yq=yq, col=col: E.activation(
                    out=y[yq][:], in_=pu[p_][:], func=AF.Identity, scale=cw[:, col, 2:3], bias=cb[:, col:col + 1]),
                    ['pu%d' % p_], ['y%d' % yq])
                sch.op('pool', lambda E, uq=uq, col=col: E.tensor_copy(out=u[uq][:, 0:2], in_=hal[:, col, :]),
                       ['hal'], ['u%d' % uq])
                sch.op('pool', lambda E, uq=uq, col=col: E.tensor_copy(out=hal[:, col, :], in_=u[uq][:, 512:514]),
                       ['u%d' % uq], ['hal'])
                for jj in (1, 0):
                    sch.op('dve', lambda E, uq=uq, yq=yq, col=col, jj=jj: E.scalar_tensor_tensor(
                        out=y[yq][:], in0=u[uq][:, jj:jj + 512], scalar=cw[:, col, jj:jj + 1], in1=y[yq][:],
                        op0=ALU.mult, op1=ALU.add), ['u%d' % uq, 'y%d' % yq], ['y%d' % yq])
            sq_ = ct % 2
            sch.op('act', lambda E, sq_=sq_, yg=ys[0]: E.activation(out=sg[sq_][:], in_=y[yg][:], func=AF.Silu),
                   ['y%d' % ys[0]], ['sg%d' % sq_])
            gdst = gpre[:, c % 2, ct, :] if ct < NPRE else gT[:, ct, :]
            gname = ('gpre', c % 2, ct) if ct < NPRE else ('gT', ct)
            sch.op('pool', lambda E, sq_=sq_, yv=ys[1], gdst=gdst: E.tensor_tensor(out=gdst, in0=sg[sq_][:],
                                                                                   in1=y[yv][:], op=ALU.mult),
                   ['sg%d' % sq_, 'y%d' % ys[1]], [gname])

        load_xc(0)
        for c in range(NCH):
            for ct in range(NPRE if c > 0 else 0, 22):
                up_pair(c, ct)
            if c + 1 < NCH:
                load_xc(c + 1)

            def load_x1t(li):
                rws = slice(c * 512 + li * 128, c * 512 + (li + 1) * 128)
                sch.dma('sp', x1t[li % 2][:], out[b, rws, :], writes=['x1t%d' % (li % 2)], chan='x1t%d' % (li % 2))
            load_x1t(0)
            load_x1t(1)
            if c + 1 < NCH:
                for ct in range(NPRE):
                    up_pair(c + 1, ct)
            for li in range(4):
                tq = li % 2
                rows = slice(c * 512 + li * 128, c * 512 + (li + 1) * 128)
                for hf in range(2):
                    sch.op('pe', lambda E, tq=tq, hf=hf, li=li, c=c: [E.matmul(
                        pd[tq][hf][:], lhsT=(gpre[:, c % 2, ct, li * 128:(li + 1) * 128] if ct < NPRE
                                             else gT[:, ct, li * 128:(li + 1) * 128]),
                        rhs=wd[:, ct, hf * 512:(hf + 1) * 512],
                        start=(ct == 0), stop=(ct == 21)) for ct in range(22)],
                        [(('gpre', c % 2, ct) if ct < NPRE else ('gT', ct)) for ct in range(22)],
                        ['pd%d%d' % (tq, hf)])
                    sch.op('dve', lambda E, tq=tq, hf=hf: E.tensor_tensor(
                        out=x1t[tq][:, hf * 512:(hf + 1) * 512], in0=pd[tq][hf][:], in1=x1t[tq][:, hf * 512:(hf + 1) * 512],
                        op=ALU.add), ['pd%d%d' % (tq, hf), 'x1t%d' % tq], ['x1t%d' % tq])
                sch.dma('sp', out[b, rows, :], x1t[tq][:], reads=['x1t%d' % tq], writes=[], chan='st_x1t%d' % tq)
                if li + 2 < 4:
                    load_x1t(li + 2)
        sch.barrier()


def t5_bucket_np(n):
    n = np.maximum(n, 0)
    nf = np.maximum(n, 1).astype(np.float32)
    large = 16 + (np.log(nf / np.float32(16)) / np.float32(math.log(128 / 16)) * np.float32(16)).astype(np.int32)
    large = np.minimum(large, 31)
    return np.where(n < 16, n, large)


def prep_shared(inp):
    f = lambda a: np.ascontiguousarray(a, dtype=np.float32)
    rb = np.asarray(inp['rel_bias'], np.float32)
    s_ = np.arange(128)[:, None]
    t_ = np.arange(256)[None, :]
    bk = t5_bucket_np(t_ - s_)
    biasT = rb[bk]
    cm = np.where(t_ >= s_, 0.0, NEG).astype(np.float32)
    pq = np.arange(128)
    sh = {
        'w_ada': f(inp['w_ada'][0]), 'b_ada': f(inp['b_ada']),
        'g_attn_c': f(np.asarray(inp['g_attn'][0]).reshape(8, 128).T),
        'g_ffn_c': f(np.asarray(inp['g_ffn'][0]).reshape(8, 128).T),
        'w_in': f(inp['w_in'][0]),
        'qkg': f(np.stack([np.tile(np.asarray(inp[k][0]), 2) for k in
                           ('q_norm_a', 'k_norm_a', 'q_norm_b', 'k_norm_b')], 1)),
        'lamv': f(np.asarray(inp['lam_vecs'][0]).reshape(1, 256)),
        'subln': f(np.asarray(inp['subln_a'])),
        'w_out': f(inp['w_out'][0]), 'w_up': f(inp['w_up'][0]),
        'conv_wc': f(np.asarray(inp['conv_w'][0]).T.reshape(44, 128, 3).transpose(1, 0, 2)),
        'conv_bc': f(np.asarray(inp['conv_b'][0]).reshape(44, 128).T),
        'w_down': f(inp['w_down'][0]),
        'biasT': f(biasT.transpose(0, 2, 1)),
        'b31': f(rb[31:32, :]),
        'cmask': cm,
        'identf': np.eye(128, dtype=np.float32),
        'blk1': f((pq[:, None] // 64) == (pq[None, :] // 64)),
    }
    return sh


def core_inputs(inp, sh, rows):
    m = dict(sh)
    xs = np.ascontiguousarray(np.asarray(inp['x'])[rows], dtype=np.float32)
    cs = np.asarray(inp['c'], np.float32)[rows]
    m['x'] = xs
    m['cT'] = np.ascontiguousarray(cs.reshape(len(rows), 8, 128).transpose(2, 1, 0))
    return m


_NC_CACHE = {}


def kernel(**inputs):
    B, S, _ = inputs['x'].shape
    ncores = 8
    NB = B // ncores
    key = (S, NB)
    if key not in _NC_CACHE:
        _NC_CACHE[key] = build(S, NB)
    nc = _NC_CACHE[key]
    sh = prep_shared(inputs)
    in_maps = [core_inputs(inputs, sh, list(range(i * NB, (i + 1) * NB))) for i in range(ncores)]
    res = run_bass_kernel_spmd(nc, in_maps, core_ids=list(range(ncores)))
    return np.concatenate([np.asarray(r['out']) for r in res.results], axis=0).astype(np.float32)
```

```python
from contextlib import ExitStack
import math
import numpy as np
import concourse.bass as bass
import concourse.mybir as mybir
from concourse.bass_utils import run_bass_kernel_spmd

F32 = mybir.dt.float32
BF16 = mybir.dt.bfloat16
AF = mybir.ActivationFunctionType
ALU = mybir.AluOpType

D = 1024
DFF = 2816
INC = 2888
NEG = -30000.0
EPS = 1e-6
LAM_INIT = 0.8 - 0.6
TOPK = 256
NIT = 13


class Sched:
    def __init__(self, nc):
        self.nc = nc
        self.engs = {'pe': nc.tensor, 'act': nc.scalar, 'dve': nc.vector, 'pool': nc.gpsimd, 'sp': nc.sync}
        self.sems, self.cnt, self.mult = {}, {}, {}
        self.seen = {e: {} for e in self.engs}
        self.lastw, self.readers = {}, {}
        for e in ['pe', 'act', 'dve', 'pool']:
            self._chan(e, 1)

    def _chan(self, name, mult):
        if name not in self.sems:
            self.sems[name] = self.nc.alloc_semaphore("s%d" % len(self.sems))
            self.cnt[name] = 0
            self.mult[name] = mult

    def op(self, eng, fn, reads=(), writes=(), chan=None):
        deps = {}

        def add(c, n):
            if deps.get(c, 0) < n:
                deps[c] = n
        for r in reads:
            for c, n in self.lastw.get(r, {}).items():
                add(c, n)
        for w in writes:
            for c, n in self.lastw.get(w, {}).items():
                add(c, n)
            for c, n in self.readers.get(w, {}).items():
                add(c, n)
        E = self.engs[eng]
        seen = self.seen[eng]
        need = []
        for c, n in deps.items():
            if eng == 'pe' and c == 'pe':
                continue
            if seen.get(c, 0) < n:
                need.append((c, n))
                seen[c] = n
        for c, n in need[1:]:
            E.wait_ge(self.sems[c], n * self.mult[c])
        r = fn(E)
        if isinstance(r, (tuple, list)):
            first, last = r[0], r[-1]
        else:
            first = last = r
        if need:
            c, n = need[0]
            first._wait_ge(self.sems[c], n * self.mult[c])
        ch = chan or eng
        if ch not in self.sems:
            self._chan(ch, 16)
        self.cnt[ch] += 1
        last.then_inc(self.sems[ch], self.mult[ch])
        me = (ch, self.cnt[ch])
        for w in writes:
            self.lastw[w] = {me[0]: me[1]}
            self.readers[w] = {}
        for r_ in reads:
            self.readers.setdefault(r_, {})[me[0]] = me[1]

    def dma(self, q, out, in_, reads=(), writes=(), chan=None, **kw):
        assert chan is not None
        self.op(q, lambda E: E.dma_start(out=out, in_=in_, **kw), reads, writes, chan=chan)

    def barrier(self, engines=None):
        for e in (engines or self.engs):
            E = self.engs[e]
            for c, n in self.cnt.items():
                if n > 0 and self.seen[e].get(c, 0) < n:
                    E.wait_ge(self.sems[c], n * self.mult[c])
                    self.seen[e][c] = n
        if engines is None:
            self.lastw, self.readers = {}, {}


class Alloc:
    def __init__(self, nc):
        self.nc = nc
        self.n = 0

    def sb(self, es, shape, dt, name="t"):
        self.n += 1
        return es.enter_context(self.nc.sbuf_tensor("%s_%d" % (name, self.n), list(shape), dt))

    def ps(self, es, shape, dt=F32, name="p"):
        self.n += 1
        return es.enter_context(self.nc.psum_tensor("%s_%d" % (name, self.n), list(shape), dt))


class Blk:
    __slots__ = ('pre', 's0', 's1', 's2', 'post', 'post2')

    def __init__(self, s0, s1, s2, pre=None, post=None, post2=None):
        self.pre, self.s0, self.s1, self.s2, self.post, self.post2 = pre, s0, s1, s2, post, post2


def run_pipe(blocks, depth=1):
    n = len(blocks)
    for i in range(min(depth, n)):
        if blocks[i].pre:
            blocks[i].pre()
        blocks[i].s0()
    for i in range(n):
        if i + depth < n:
            if blocks[i + depth].pre:
                blocks[i + depth].pre()
            blocks[i + depth].s0()
        blocks[i].s1()
        blocks[i].s2()
        if blocks[i].post:
            blocks[i].post()
        if i >= 3 and blocks[i - 3].post2:
            blocks[i - 3].post2()
    for i in range(max(0, n - 3), n):
        if blocks[i].post2:
            blocks[i].post2()


def build(S, NB, debug=False):
    NT = S // 128
    NCH = S // 512
    nc = bass.Bass("TRN2", target_bir_lowering=False)
    sch = Sched(nc)
    al = Alloc(nc)

    def din(name, shape, dt=F32):
        return nc.dram_tensor(name, list(shape), dt, kind="ExternalInput").ap()

    def dscr(name, shape, dt):
        return nc.dram_tensor(name, list(shape), dt, kind="ExternalOutput" if debug else "Internal").ap()

    x = din("x", [NB, S, D])
    cT = din("cT", [128, 8, NB])
    w_ada = din("w_ada", [D, 6 * D])
    b_ada = din("b_ada", [1, 6 * D])
    g_attn_c = din("g_attn_c", [128, 8])
    g_ffn_c = din("g_ffn_c", [128, 8])
    w_in = din("w_in", [D, INC])
    qkg = din("qkg", [128, 4])
    lamv = din("lamv", [1, 256])
    subln = din("subln", [1, 128])
    w_out = din("w_out", [D, D])
    w_up = din("w_up", [D, 2 * DFF])
    conv_wc = din("conv_wc", [128, 44, 3])
    conv_bc = din("conv_bc", [128, 44])
    w_down = din("w_down", [DFF, D])
    biasT = din("biasT", [128, 12, 256])
    b31 = din("b31", [1, 12])
    cmask = din("cmask", [128, 256])
    identf = din("identf", [128, 128])
    blk1 = din("blk1", [128, 128])
    out = nc.dram_tensor("out", [NB, S, D], F32, kind="ExternalOutput").ap()

    modrow = dscr("modrow", [NB, 6 * D], F32)
    QaT = dscr("QaT", [4, 128, S], BF16)
    KaT = dscr("KaT", [4, 128, S], BF16)
    Va = dscr("Va", [S, 512], BF16)
    QbT = dscr("QbT", [4, 128, S], BF16)
    KbT = dscr("KbT", [128, S], BF16)
    Vb = dscr("Vb", [S, 128], BF16)
    iqT = dscr("iqT", [4, 128, S], BF16)
    ikT = dscr("ikT", [128, S], BF16)
    iwS = dscr("iwS", [S, 8], F32)
    oS = dscr("oS", [S, D], BF16)
    x1nT = dscr("x1nT", [8, 128, S], BF16)

    with ExitStack() as g:
        identb = al.sb(g, [128, 128], BF16, "identb")
        identf_sb = al.sb(g, [128, 128], F32, "identf")
        blk1b = al.sb(g, [128, 128], BF16, "blk1b")
        cmask_sb = al.sb(g, [128, 256], F32, "cmask")
        negtri = al.sb(g, [128, 128], F32, "negtri")
        b31c = al.sb(g, [128, 12], F32, "b31c")
        DThi = al.sb(g, [128, 12, 256], BF16, "DThi")
        qkg_sb = al.sb(g, [128, 4], F32, "qkg")
        neg_lam = al.sb(g, [128, 1], F32, "neglam")
        subln_row = al.sb(g, [128, 128], F32, "sublnrow")
        gattn_sb = al.sb(g, [128, 8], F32, "gattn")
        gffn_sb = al.sb(g, [128, 8], F32, "gffn")
        cw_sb = al.sb(g, [128, 44, 3], F32, "cw")
        cb_sb = al.sb(g, [128, 44], F32, "cb")
        thr_const = al.sb(g, [128, 1], F32, "thrc")
        eps_c = al.sb(g, [128, 1], F32, "epsc")
        fvec = al.sb(g, [128, NIT], F32, "fvec")

        with ExitStack() as es:
            tmpf = al.sb(es, [128, 128], F32, "tmpf")
            bT = al.sb(es, [128, 12, 256], F32, "bT")
            lv = al.sb(es, [128, 256], F32, "lv")
            lsum = al.sb(es, [128, 2], F32, "lsum")
            junk = al.sb(es, [128, 64], F32, "junk")
            sch.dma('sp', identf_sb[:], identf[:, :], writes=['identf'], chan='c0')
            sch.dma('sp', tmpf[:], blk1[:, :], writes=['tmpf'], chan='c1')
            sch.dma('sp', cmask_sb[:], cmask[:, :], writes=['cmask'], chan='c2')
            sch.dma('sp', b31c[:], b31[0:1, :].partition_broadcast(128), writes=['b31c'], chan='c3')
            sch.dma('sp', bT[:], biasT[:, :, :], writes=['bT'], chan='c4')
            sch.dma('sp', qkg_sb[:], qkg[:, :], writes=['qkg'], chan='c5')
            sch.dma('sp', lv[:], lamv[0:1, :].partition_broadcast(128), writes=['lv'], chan='c6')
            sch.dma('sp', subln_row[:], subln[0:1, :].partition_broadcast(128), writes=['subln'], chan='c7')
            sch.dma('sp', gattn_sb[:], g_attn_c[:, :], writes=['gattn'], chan='c8')
            sch.dma('sp', gffn_sb[:], g_ffn_c[:, :], writes=['gffn'], chan='c9')
            sch.dma('sp', cw_sb[:], conv_wc[:, :, :], writes=['cw'], chan='c10')
            sch.dma('sp', cb_sb[:], conv_bc[:, :], writes=['cb'], chan='c11')
            sch.op('dve', lambda E: E.tensor_copy(out=identb[:], in_=identf_sb[:]), ['identf'], ['identb'])
            sch.op('dve', lambda E: E.tensor_copy(out=blk1b[:], in_=tmpf[:]), ['tmpf'], ['blk1b'])
            sch.op('dve', lambda E: E.tensor_scalar(out=qkg_sb[:, 0:1], in0=qkg_sb[:, 0:1], scalar1=0.125, scalar2=None,
                                                    op0=ALU.mult), ['qkg'], ['qkg'])
            sch.op('dve', lambda E: E.tensor_scalar(out=qkg_sb[:, 2:3], in0=qkg_sb[:, 2:3], scalar1=0.125, scalar2=None,
                                                    op0=ALU.mult), ['qkg'], ['qkg'])
            sch.op('dve', lambda E: E.memset(thr_const[:], -1e29), [], ['thrc'])
            sch.op('dve', lambda E: E.memset(eps_c[:], EPS), [], ['epsc'])
            for k in range(NIT):
                sch.op('dve', lambda E, k=k: E.memset(fvec[:, k:k + 1], 0.5 ** (k + 1)), [], ['fvec'])
            for h in range(12):
                sch.op('dve', lambda E, h=h: E.scalar_tensor_tensor(
                    out=bT[:, h, :], in0=bT[:, h, :], scalar=b31c[:, h:h + 1], in1=cmask_sb[:],
                    op0=ALU.subtract, op1=ALU.add), ['bT', 'b31c', 'cmask'], ['bT'])
            sch.op('dve', lambda E: E.tensor_copy(out=DThi[:], in_=bT[:]), ['bT'], ['DThi'])
            for i in range(2):
                sch.op('dve', lambda E, i=i: E.scalar_tensor_tensor(
                    out=junk[:], in0=lv[:, 128 * i:128 * i + 64], scalar=1.0, in1=lv[:, 128 * i + 64:128 * i + 128],
                    op0=ALU.mult, op1=ALU.mult, accum_out=lsum[:, i:i + 1]), ['lv'], ['junk', 'lsum'])
            sch.op('act', lambda E: E.activation(out=lsum[:], in_=lsum[:], func=AF.Exp), ['lsum'], ['lsum'])
            sch.op('dve', lambda E: E.tensor_tensor(out=neg_lam[:], in0=lsum[:, 1:2], in1=lsum[:, 0:1], op=ALU.subtract),
                   ['lsum'], ['neglam'])
            sch.op('dve', lambda E: E.tensor_scalar(out=neg_lam[:], in0=neg_lam[:], scalar1=-LAM_INIT, scalar2=None,
                                                    op0=ALU.add), ['neglam'], ['neglam'])
            sch.op('dve', lambda E: E.tensor_scalar(out=subln_row[:], in0=subln_row[:], scalar1=1.0 - LAM_INIT,
                                                    scalar2=None, op0=ALU.mult), ['subln'], ['subln'])
            with ExitStack() as e2:
                pt = al.ps(e2, [128, 128], F32, "ptri")
                sch.op('pe', lambda E: E.transpose(out=pt[:], in_=cmask_sb[:, 0:128], identity=identf_sb[:]),
                       ['cmask', 'identf'], ['ptri'])
                sch.op('dve', lambda E: E.tensor_scalar(out=negtri[:], in0=pt[:], scalar1=1e30 / 30000.0, scalar2=None,
                                                        op0=ALU.mult), ['ptri'], ['negtri'])
                sch.barrier()

        with ExitStack() as es:
            sc = al.sb(es, [128, 8, NB], F32, "sc")
            wb = [al.sb(es, [128, 8, 512], F32, "wada%d" % i) for i in range(2)]
            mrow = al.sb(es, [NB, 6 * D], F32, "mrow")
            brow = al.sb(es, [NB, 6 * D], F32, "brow")
            pm = [al.ps(es, [128, 512], F32, "pm%d" % i) for i in range(2)]
            sch.dma('sp', sc[:], cT[:, :, :], writes=['sc'], chan='c0')
            sch.dma('sp', brow[:], b_ada[0:1, :].partition_broadcast(NB), writes=['brow'], chan='c1')
            sch.op('act', lambda E: E.activation(out=sc[:], in_=sc[:], func=AF.Silu), ['sc'], ['sc'])
            for cc in range(12):
                wt = wb[cc % 2]
                sch.dma('sp', wt[:], w_ada[:, cc * 512:(cc + 1) * 512].rearrange("(k p) n -> p k n", p=128),
                        writes=['wada%d' % (cc % 2)], chan='wada%d' % (cc % 2))

                def mm(E, wt=wt, cc=cc):
                    r = []
                    for k in range(8):
                        r.append(E.matmul(pm[cc % 2][0:NB, :], lhsT=sc[:, k, :], rhs=wt[:, k, :],
                                          start=(k == 0), stop=(k == 7)))
                    return r
                sch.op('pe', mm, ['sc', 'wada%d' % (cc % 2)], ['pm%d' % (cc % 2)])
                sch.op('dve', lambda E, cc=cc: E.tensor_tensor(
                    out=mrow[:, cc * 512:(cc + 1) * 512], in0=pm[cc % 2][0:NB, :], in1=brow[:, cc * 512:(cc + 1) * 512],
                    op=ALU.add), ['pm%d' % (cc % 2), 'brow'], ['mrow'])
            sch.dma('sp', modrow[:, :], mrow[:], reads=['mrow'], writes=['modrow'], chan='c2')
            sch.barrier()

        for b in range(NB):
            with ExitStack() as eb:
                modc = al.sb(eb, [128, 48], F32, "modc")
                a_attn = al.sb(eb, [128, 8], F32, "aattn")
                a_ffn = al.sb(eb, [128, 8], F32, "affn")
                with ExitStack() as em:
                    mt = al.sb(em, [48, 128], F32, "mt")
                    pmt = al.ps(em, [128, 48], F32, "pmt")
                    sch.dma('sp', mt[:], modrow[b, :].rearrange("(t p) -> t p", p=128), writes=['mt'], chan='modc')
                    sch.op('pe', lambda E: E.transpose(out=pmt[:], in_=mt[:], identity=identf_sb[0:48, 0:48]),
                           ['mt'], ['pmt'])
                    sch.op('dve', lambda E: E.tensor_copy(out=modc[:], in_=pmt[:]), ['pmt'], ['modc'])
                    sch.barrier()
                sch.op('dve', lambda E: E.scalar_tensor_tensor(out=a_attn[:], in0=modc[:, 8:16], scalar=1.0,
                                                               in1=gattn_sb[:], op0=ALU.add, op1=ALU.mult),
                       ['modc'], ['aattn'])
                sch.op('dve', lambda E: E.scalar_tensor_tensor(out=a_ffn[:], in0=modc[:, 32:40], scalar=1.0,
                                                               in1=gffn_sb[:], op0=ALU.add, op1=ALU.mult),
                       ['modc'], ['affn'])
                sch.barrier()
                phase_proj(nc, sch, al, locals())
                phase_attn_a(nc, sch, al, locals())
                phase_attn_b(nc, sch, al, locals())
                phase_f(nc, sch, al, locals())
        sch.barrier()
    return nc


def phase_proj(nc, sch, al, L):
    S, NT, NCH, b = L['S'], L['NT'], L['NCH'], L['b']
    x, w_in = L['x'], L['w_in']
    identb, blk1b, qkg_sb = L['identb'], L['blk1b'], L['qkg_sb']
    a_attn, modc = L['a_attn'], L['modc']
    with ExitStack() as es:
        win = al.sb(es, [128, 8, INC], BF16, "win")
        with ExitStack() as e2:
            stg = [al.sb(e2, [128, INC], F32, "wstg%d" % i) for i in range(2)]
            for k in range(8):
                sch.dma('sp', stg[k % 2][:], w_in[k * 128:(k + 1) * 128, :], writes=['wstg%d' % (k % 2)],
                        chan='wstg%d' % (k % 2))
                if k % 2 == 0:
                    sch.op('dve', lambda E, k=k: E.tensor_copy(out=win[:, k, :], in_=stg[k % 2][:]),
                           ['wstg%d' % (k % 2)], ['win%d' % k])
                else:
                    sch.op('act', lambda E, k=k: E.activation(out=win[:, k, :], in_=stg[k % 2][:], func=AF.Copy),
                           ['wstg%d' % (k % 2)], ['win%d' % k])
            sch.barrier()
        xt = [al.sb(es, [128, D], F32, "xt%d" % i) for i in range(2)]
        xn = [al.sb(es, [128, D], BF16, "xn%d" % i) for i in range(2)]
        junk = al.sb(es, [128, D], BF16, "junk")
        ss = al.sb(es, [128, 2], F32, "ss")
        hT = [al.sb(es, [128, 8, 512], BF16, "hT%d" % i) for i in range(2)]
        qsb = [al.sb(es, [128, 512], F32, "qsb%d" % i) for i in range(3)]
        sq = [al.sb(es, [128, 512], BF16, "sq%d" % i) for i in range(3)]
        lr = [al.sb(es, [128, 512], F32, "lr%d" % i) for i in range(3)]
        stF = [al.sb(es, [128, 512], BF16, "stF%d" % i) for i in range(3)]
        stV = [al.sb(es, [128, 512], BF16, "stV%d" % i) for i in range(2)]
        stB = [al.sb(es, [128, 128], BF16, "stB%d" % i) for i in range(2)]
        stW = [al.sb(es, [128, 8], F32, "stW%d" % i) for i in range(2)]
        tp = [al.ps(es, [128, 8, 128], BF16, "tp%d" % i) for i in range(2)]
        pp = [al.ps(es, [128, 512], F32, "pp%d" % i) for i in range(3)]
        pss = [al.ps(es, [128, 512], F32, "pss%d" % i) for i in range(2)]
        pB = al.ps(es, [128, 136], F32, "pB")

        fm = []
        for t in range(4):
            fm.append((L['QaT'], t, 128 * t, 128, 0))
        for t in range(4):
            fm.append((L['KaT'], t, 512 + 128 * t, 128, 1))
        for t in range(4):
            fm.append((L['QbT'], t, 1536 + 128 * t, 128, 2))
        fm.append((L['KbT'], None, 2048, 128, 3))
        for t in range(4):
            fm.append((L['iqT'], t, 2304 + 128 * t, 128, None))
        fm.append((L['ikT'], None, 2816, 64, None))

        nfm = 0
        ntok = 0
        pend = [None]
        npss = [0]

        def prep_tile(c, tl):
            h = hT[c % 2]
            hn = 'hT%d' % (c % 2)
            tt = c * 4 + tl
            r = tt % 2
            sch.dma('sp', xt[r][:], x[b, tt * 128:(tt + 1) * 128, :], writes=['xt%d' % r], chan='xt%d' % r)
            sch.op('dve', lambda E, r=r: E.scalar_tensor_tensor(
                out=junk[:], in0=xt[r][:], scalar=1.0, in1=xt[r][:], op0=ALU.mult, op1=ALU.mult,
                accum_out=ss[:, 0:1]), ['xt%d' % r], ['junk', 'ss'])
            sch.op('act', lambda E: E.activation(out=ss[:, 1:2], in_=ss[:, 0:1], func=AF.Ln, scale=1.0 / D,
                                                 bias=L['eps_c'][:]), ['ss'], ['ss1'])
            sch.op('act', lambda E: E.activation(out=ss[:, 1:2], in_=ss[:, 1:2], func=AF.Exp, scale=-0.5),
                   ['ss1'], ['ss1'])
            sch.op('dve', lambda E, r=r: E.tensor_scalar(out=xn[r][:], in0=xt[r][:], scalar1=ss[:, 1:2],
                                                         scalar2=None, op0=ALU.mult),
                   ['xt%d' % r, 'ss1'], ['xn%d' % r])

        def prep_tile_b(c, tl):
            h = hT[c % 2]
            hn = 'hT%d' % (c % 2)
            tt = c * 4 + tl
            r = tt % 2

            def tr(E, r=r):
                res = []
                for k in range(8):
                    res.append(E.transpose(out=tp[r][:, k, :], in_=xn[r][:, k * 128:(k + 1) * 128],
                                           identity=identb[:]))
                return res
            sch.op('pe', tr, ['xn%d' % r, 'identb'], ['tp%d' % r])

            def ev(E, r=r, tl=tl, h=h):
                res = []
                for k in range(8):
                    res.append(E.activation(out=h[:, k, tl * 128:(tl + 1) * 128], in_=tp[r][:, k, :],
                                            func=AF.Identity, scale=a_attn[:, k:k + 1], bias=modc[:, k:k + 1]))
                return res
            sch.op('act', ev, ['tp%d' % r, 'aattn', 'modc'], [hn])

        for tl in range(4):
            prep_tile(0, tl)
            prep_tile_b(0, tl)
        for c in range(NCH):
            h = hT[c % 2]
            hn = 'hT%d' % (c % 2)
            nfm_c = 0
            for (dst, t, c0, nr, gi) in fm:
                if c + 1 < NCH and nfm_c in (0, 4, 8, 12):
                    prep_tile(c + 1, nfm_c // 4)
                if c + 1 < NCH and nfm_c in (3, 7, 11, 15):
                    prep_tile_b(c + 1, (nfm_c - 3) // 4)
                nfm_c += 1
                pr = nfm % 3
                nfm += 1

                def mm(E, c0=c0, nr=nr, pr=pr, h=h):
                    res = []
                    for k in range(8):
                        res.append(E.matmul(pp[pr][0:nr, :], lhsT=win[:, k, c0:c0 + nr], rhs=h[:, k, :],
                                            start=(k == 0), stop=(k == 7)))
                    return res
                sch.op('pe', mm, ['win', hn], ['pp%d' % pr])
                st = stF[pr]
                sn = 'stF%d' % pr
                if t is None:
                    dap = dst[0:nr, c * 512:(c + 1) * 512]
                else:
                    dap = dst[t, 0:nr, c * 512:(c + 1) * 512]
                if gi is None:
                    sch.op('act', lambda E, nr=nr, pr=pr, st=st: E.activation(out=st[0:nr, :], in_=pp[pr][0:nr, :],
                                                                              func=AF.Copy),
                           ['pp%d' % pr], [sn])

                    def e2(dap=dap, st=st, sn=sn, nr=nr):
                        sch.dma('pool', dap, st[0:nr, :], reads=[sn], writes=[], chan='st_' + sn)
                else:
                    q = nfm % 3
                    sch.op('act', lambda E, pr=pr, q=q: E.activation(out=qsb[q][:], in_=pp[pr][:], func=AF.Copy),
                           ['pp%d' % pr], ['qsb%d' % q])
                    sch.op('dve', lambda E, q=q: E.tensor_tensor(out=sq[q][:], in0=qsb[q][:], in1=qsb[q][:],
                                                                  op=ALU.mult), ['qsb%d' % q], ['sq%d' % q])
                    ps_ = npss[0] % 2
                    npss[0] += 1
                    sch.op('pe', lambda E, q=q, ps_=ps_: E.matmul(pss[ps_][:], lhsT=blk1b[:], rhs=sq[q][:], start=True,
                                                                  stop=True),
                           ['sq%d' % q, 'blk1b'], ['pss%d' % ps_])

                    def e2(dap=dap, st=st, sn=sn, nr=nr, q=q, gi=gi, ps_=ps_):
                        sch.op('act', lambda E: E.activation(out=lr[q][:], in_=pss[ps_][:], func=AF.Ln, scale=1.0 / 64,
                                                             bias=L['eps_c'][:]), ['pss%d' % ps_], ['lr%d' % q])
                        sch.op('act', lambda E: E.activation(out=lr[q][:], in_=lr[q][:], func=AF.Exp, scale=-0.5),
                               ['lr%d' % q], ['lr%d' % q])
                        sch.op('dve', lambda E: E.scalar_tensor_tensor(
                            out=st[:], in0=qsb[q][:], scalar=qkg_sb[:, gi:gi + 1], in1=lr[q][:], op0=ALU.mult,
                            op1=ALU.mult), ['qsb%d' % q, 'lr%d' % q, 'qkg'], [sn])
                        sch.dma('pool', dap, st[0:nr, :], reads=[sn], writes=[], chan='st_' + sn)
                if pend[0] is not None:
                    pend[0]()
                pend[0] = e2
            if pend[0] is not None:
                pend[0]()
                pend[0] = None

            for tl in range(4):
                tt = c * 4 + tl
                pr = nfm % 3
                nfm += 1
                v = ntok % 2
                ntok += 1

                def mmv(E, pr=pr, tl=tl, h=h):
                    res = []
                    for k in range(8):
                        res.append(E.matmul(pp[pr][:, :], lhsT=h[:, k, tl * 128:(tl + 1) * 128],
                                            rhs=win[:, k, 1024:1536], start=(k == 0), stop=(k == 7)))
                    return res
                sch.op('pe', mmv, ['win', hn], ['pp%d' % pr])
                sch.op('act', lambda E, pr=pr, v=v: E.activation(out=stV[v][:], in_=pp[pr][:], func=AF.Copy),
                       ['pp%d' % pr], ['stV%d' % v])
                sch.dma('pool', L['Va'][tt * 128:(tt + 1) * 128, :], stV[v][:], reads=['stV%d' % v], writes=[],
                        chan='st_stV%d' % v)

                def mmb(E, tl=tl, h=h):
                    res = []
                    for k in range(8):
                        res.append(E.matmul(pB[:, 0:128], lhsT=h[:, k, tl * 128:(tl + 1) * 128],
                                            rhs=win[:, k, 2176:2304], start=(k == 0), stop=(k == 7)))
                    for k in range(8):
                        res.append(E.matmul(pB[:, 128:136], lhsT=h[:, k, tl * 128:(tl + 1) * 128],
                                            rhs=win[:, k, 2880:2888], start=(k == 0), stop=(k == 7)))
                    return res
                sch.op('pe', mmb, ['win', hn], ['pB'])
                sch.op('dve', lambda E, v=v: E.tensor_copy(out=stB[v][:], in_=pB[:, 0:128]), ['pB'], ['stB%d' % v])
                sch.op('dve', lambda E, v=v: E.tensor_scalar(out=stW[v][:], in0=pB[:, 128:136], scalar1=512.0 ** -0.5,
                                                             scalar2=None, op0=ALU.mult), ['pB'], ['stW%d' % v])
                sch.dma('pool', L['Vb'][tt * 128:(tt + 1) * 128, :], stB[v][:], reads=['stB%d' % v], writes=[],
                        chan='st_stB%d' % v)
                sch.dma('pool', L['iwS'][tt * 128:(tt + 1) * 128, :], stW[v][:], reads=['stW%d' % v], writes=[],
                        chan='st_stW%d' % v)
        sch.barrier()


def phase_attn_a(nc, sch, al, L):
    S, NT, NCH, b = L['S'], L['NT'], L['NCH'], L['b']
    identb, DThi, b31c = L['identb'], L['DThi'], L['b31c']
    neg_lam, subln_row, eps_c = L['neg_lam'], L['subln_row'], L['eps_c']
    QaT, KaT, Va, oS = L['QaT'], L['KaT'], L['Va'], L['oS']
    with ExitStack() as es:
        Ka = al.sb(es, [128, 4, S], BF16, "Ka")
        Vs = al.sb(es, [128, NT, 4, 130], BF16, "Vs")
        sch.op('pool', lambda E: E.memset(Vs[:], 1.0), [], ['Vs'])
        for t in range(4):
            sch.dma('sp', Ka[:, t, :], KaT[t, :, :], writes=['Ka%d' % t], chan='ldK')
        for j0 in range(NT):
            sch.dma('sp', Vs[:, j0, :, 0:128],
                    Va[j0 * 128:(j0 + 1) * 128, :].rearrange("p (h e) -> p h e", h=4),
                    reads=['Vs'], writes=['Vs%d' % j0], chan='ldV')
        sch.barrier()
        Qc = [al.sb(es, [128, 512], BF16, "Qc%d" % i) for i in range(2)]
        pt = [al.sb(es, [128, 2, 512], BF16, "pt%d" % i) for i in range(3)]
        ost = [al.sb(es, [128, 4, 128], BF16, "ost%d" % i) for i in range(2)]
        rr = [al.sb(es, [128, 4], F32, "rr%d" % i) for i in range(2)]
        t0 = [al.sb(es, [128, 128], F32, "t0%d" % i) for i in range(2)]
        dd = [al.sb(es, [128, 128], F32, "dd%d" % i) for i in range(2)]
        junk = al.sb(es, [128, 128], F32, "junkA")
        accs = [al.sb(es, [128, 2, 2, 2, 129], F32, "accs%d" % i) for i in range(2)]
        st = [al.ps(es, [128, 2, 512], F32, "st%d" % i) for i in range(2)]
        acc = [[al.ps(es, [128, 2, 256], F32, "acc%d%d" % (m, q)) for q in range(2)] for m in range(2)]
        nst = 0
        nq = 0
        nep = [0]
        blocks = []
        for h in range(4):
            for c in range(NCH):
                cq = nq % 2
                nq += 1
                nj = 4 * c + 4
                oq = (h * NCH + c) % 2

                def pre(cq=cq, h=h, c=c):
                    sch.dma('sp', Qc[cq][:], QaT[h, :, c * 512:(c + 1) * 512], writes=['Qc%d' % cq], chan='Qc%d' % cq)

                def epi(h=h, c=c, oq=oq):
                    aq = nep[0] % 2
                    nep[0] += 1
                    A_ = accs[aq]
                    an = 'accs%d' % aq
                    for m in range(2):
                        for q_ in range(2):
                            sch.op('dve', lambda E, m=m, q_=q_: E.tensor_copy(out=A_[:, m, q_, :, :],
                                                                              in_=acc[m][q_][:, :, 0:129]),
                                   [('acc', m, q_)], [an])

                    def rest(h=h, c=c, oq=oq, A_=A_, an=an):
                        epi_rest(h, c, oq, A_, an)
                    return rest

                def epi_rest(h, c, oq, A_, an):
                    for li in range(4):
                        e = li % 2
                        a0 = A_[:, 0, li // 2, li % 2, :]
                        a1 = A_[:, 1, li // 2, li % 2, :]
                        sch.op('dve', lambda E, e=e, a0=a0: E.reciprocal(out=rr[e][:, 0:1], in_=a0[:, 128:129]),
                               [an], ['rr%d' % e])
                        sch.op('dve', lambda E, e=e, a1=a1: E.reciprocal(out=rr[e][:, 1:2], in_=a1[:, 128:129]),
                               [an], ['rr%d' % e])
                        sch.op('dve', lambda E, e=e: E.tensor_tensor(out=rr[e][:, 1:2], in0=rr[e][:, 1:2], in1=neg_lam[:],
                                                                     op=ALU.mult), ['rr%d' % e], ['rr%d' % e])
                        sch.op('dve', lambda E, e=e, a0=a0: E.tensor_scalar(out=t0[e][:], in0=a0[:, 0:128],
                                                                            scalar1=rr[e][:, 0:1], scalar2=None,
                                                                            op0=ALU.mult),
                               [an, 'rr%d' % e], ['t0%d' % e])
                        sch.op('dve', lambda E, e=e, a1=a1: E.scalar_tensor_tensor(
                            out=dd[e][:], in0=a1[:, 0:128], scalar=rr[e][:, 1:2], in1=t0[e][:], op0=ALU.mult,
                            op1=ALU.add), [an, 'rr%d' % e, 't0%d' % e], ['dd%d' % e])
                        sch.op('dve', lambda E, e=e: E.scalar_tensor_tensor(
                            out=junk[:], in0=dd[e][:], scalar=1.0, in1=dd[e][:], op0=ALU.mult, op1=ALU.mult,
                            accum_out=rr[e][:, 2:3]), ['dd%d' % e], ['junkA', 'rs%d' % e])
                        sch.op('act', lambda E, e=e: E.activation(out=rr[e][:, 3:4], in_=rr[e][:, 2:3], func=AF.Ln,
                                                                  scale=1.0 / 128, bias=eps_c[:]), ['rs%d' % e], ['rt%d' % e])
                        sch.op('act', lambda E, e=e: E.activation(out=rr[e][:, 3:4], in_=rr[e][:, 3:4], func=AF.Exp,
                                                                  scale=-0.5), ['rt%d' % e], ['rt%d' % e])
                        sch.op('dve', lambda E, e=e, li=li, oq=oq: E.scalar_tensor_tensor(
                            out=ost[oq][:, li, :], in0=dd[e][:], scalar=rr[e][:, 3:4], in1=subln_row[:], op0=ALU.mult,
                            op1=ALU.mult), ['dd%d' % e, 'rt%d' % e], ['ost%d' % oq])
                    sch.dma('pool', oS[c * 512:(c + 1) * 512, h * 128:(h + 1) * 128].rearrange("(li p) e -> p li e", p=128),
                            ost[oq][:], reads=['ost%d' % oq], writes=[], chan='st_ost%d' % oq)

                for j in range(nj):
                    lo = max(0, j - 4 * c)
                    off = lo * 128
                    segs = [(j + dl - 4 * c, dl) for dl in (0, 1) if 0 <= j + dl - 4 * c <= 3]
                    r = nst % 2
                    p_ = nst % 3
                    nst += 1

                    def qk(E, r=r, j=j, off=off, segs=segs, cq=cq, h=h):
                        res = [E.matmul(st[r][:, m, off:512], lhsT=Ka[64 * m:64 * m + 64, h, j * 128:(j + 1) * 128],
                                        rhs=Qc[cq][64 * m:64 * m + 64, off:512], start=True, stop=(not segs))
                               for m in range(2)]
                        if segs:
                            c0 = segs[0][0] * 128
                            n = 128 * len(segs)
                            d0 = segs[0][1] * 128
                            for m in range(2):
                                for T in (DThi,):
                                    res.append(E.matmul(st[r][:, m, c0:c0 + n], lhsT=identb[:], rhs=T[:, h, d0:d0 + n],
                                                        start=False, stop=True))
                        return res

                    def pv(E, p_=p_, j=j, lo=lo, c=c, h=h):
                        res = []
                        for m in range(2):
                            for li in range(lo, 4):
                                res.append(E.matmul(acc[m][li // 2][:, li % 2, 0:129],
                                                    lhsT=pt[p_][:, m, li * 128:(li + 1) * 128],
                                                    rhs=Vs[:, j, h, 0:129], start=(j == 0 and li % 2 == 0),
                                                    stop=(j == 4 * c + li), skip_group_check=True))
                        return res
                    s0 = lambda qk=qk, cq=cq, r=r: sch.op('pe', qk, ['Qc%d' % cq], ['st%d' % r])
                    s1 = lambda r=r, p_=p_, off=off, h=h: sch.op('act', lambda E: E.activation(
                        out=pt[p_][:, :, off:512], in_=st[r][:, :, off:512], func=AF.Exp, bias=b31c[:, h:h + 1]),
                        ['st%d' % r], ['pt%d' % p_])
                    s2 = lambda pv=pv, p_=p_, lo=lo: sch.op(
                        'pe', pv, ['pt%d' % p_], sorted(set(('acc', m, li // 2) for m in range(2) for li in range(lo, 4))))
                    blk = Blk(s0, s1, s2, pre=(pre if j == 0 else None))
                    if j == nj - 1:
                        def post(blk=blk, epi=epi):
                            blk.post2 = epi()
                        blk.post = post
                    blocks.append(blk)
        run_pipe(blocks, 1)
        sch.barrier()


def phase_attn_b(nc, sch, al, L):
    S, NT, NCH, b = L['S'], L['NT'], L['NCH'], L['b']
    TK = min(TOPK, S // 4)
    identb, identf_sb, DThi, b31c = L['identb'], L['identf_sb'], L['DThi'], L['b31c']
    negtri, thr_const, fvec = L['negtri'], L['thr_const'], L['fvec']
    QbT, KbT, Vb, iqT, ikT, iwS, oS = L['QbT'], L['KbT'], L['Vb'], L['iqT'], L['ikT'], L['iwS'], L['oS']
    with ExitStack() as es:
        Kb = al.sb(es, [128, 2, S], BF16, "Kb")
        Vs = al.sb(es, [128, NT, 2, 66], BF16, "VsB")
        ik2 = al.sb(es, [128, S], BF16, "ik2")
        iw = al.sb(es, [128, NT, 8], F32, "iw")
        sch.op('pool', lambda E: E.memset(Vs[:], 1.0), [], ['VsB'])
        for half in range(2):
            for g in range(2):
                sch.dma('sp', Kb[64 * half:64 * half + 64, g, :], KbT[64 * g:64 * g + 64, :],
                        writes=['Kb%d%d' % (half, g)], chan='ldK')
            sch.dma('sp', ik2[64 * half:64 * half + 64, :], ikT[0:64, :], writes=['ik2%d' % half], chan='ldK')
        for j0 in range(NT):
            sch.dma('sp', iw[:, j0, :], iwS[j0 * 128:(j0 + 1) * 128, :], writes=['iw%d' % j0], chan='ldK')
        for j0 in range(NT):
            sch.dma('sp', Vs[:, j0, :, 0:64],
                    Vb[j0 * 128:(j0 + 1) * 128, :].rearrange("p (g e) -> p g e", g=2),
                    reads=['VsB'], writes=['VsB%d' % j0], chan='ldV')
        sch.barrier()
        NI = 4
        Ib = [al.sb(es, [128, S], F32, "I%d" % i) for i in range(NI)]
        ma = [al.sb(es, [128, S], BF16, "ma%d" % i) for i in range(NI)]
        maT = al.sb(es, [128, NT, 512], BF16, "maT")
        Rb = [al.sb(es, [128, 2, 512], BF16, "R%d" % i) for i in range(2)]
        dg = [al.sb(es, [128, 8, 128], BF16, "dg%d" % i) for i in range(2)]
        iqc = [al.sb(es, [128, 4, 128], BF16, "iqc%d" % i) for i in range(2)]
        Qc = [al.sb(es, [128, 4, 512], BF16, "QcB%d" % i) for i in range(2)]
        pt = [al.sb(es, [128, 2, 512], BF16, "ptB%d" % i) for i in range(3)]
        ost = [al.sb(es, [128, 4, 512], BF16, "ostB%d" % i) for i in range(1)]
        bs = [al.sb(es, [128, 8], F32, "bs%d" % i) for i in range(NI)]
        rcp = al.sb(es, [128, 2, 4], F32, "rcp")
        accbs = [al.sb(es, [128, 2, 4, 65], F32, "accbs%d" % i) for i in range(2)]
        bw = [al.sb(es, [128, NIT], F32, "bw%d" % i) for i in range(NI)]
        PP = [al.ps(es, [128, 2, 512], F32, "PP%d" % i) for i in range(2)]
        pacc = al.ps(es, [128, 512], F32, "pacc")
        tpb = al.ps(es, [128, 8, 128], BF16, "tpb")
        accb = al.ps(es, [128, 2, 4, 128], F32, "accb")
        cnt = {'x': 0, 'R': 0, 'pt': 0, 'sb': 0, 'acc': 0}

        def idx_chunk(c):
            blocks = []
            pending = []
            for li in range(4):
                i = 4 * c + li
                ib = i % NI
                q = i % 2
                Li = 128 * (i + 1)

                def pre(i=i, q=q):
                    sch.dma('sp', iqc[q][:], iqT[:, :, i * 128:(i + 1) * 128].rearrange("t p n -> p t n"),
                            writes=['iqc%d' % q], chan='iqc%d' % q)
                    for hh in range(8):
                        sch.op('pool', lambda E, hh=hh: E.tensor_scalar(
                            out=dg[q][:, hh, :], in0=identf_sb[:], scalar1=iw[:, i, hh:hh + 1], scalar2=None,
                            op0=ALU.mult), [], ['dg%d' % q])

                def gen_tile(i=i, ib=ib, Li=Li):
                    sch.op('dve', lambda E: E.tensor_tensor(
                        out=Ib[ib][:, i * 128:(i + 1) * 128], in0=Ib[ib][:, i * 128:(i + 1) * 128], in1=negtri[:],
                        op=ALU.add), ['I%d' % ib], ['I%d' % ib])
                    B = bs[ib]
                    bn = 'bs%d' % ib
                    if i >= TK // 128:
                        W = bw[ib]
                        sch.op('dve', lambda E: E.tensor_reduce(
                            out=B[:, 0:1], in_=Ib[ib][:, 0:i * 128], axis=mybir.AxisListType.X, op=ALU.min),
                            ['I%d' % ib], [bn])
                        sch.op('dve', lambda E: E.tensor_reduce(
                            out=B[:, 1:2], in_=Ib[ib][:, 0:Li], axis=mybir.AxisListType.X, op=ALU.max),
                            ['I%d' % ib], [bn])
                        yield
                        sch.op('dve', lambda E: E.tensor_tensor(out=B[:, 1:2], in0=B[:, 1:2], in1=B[:, 0:1],
                                                                op=ALU.subtract), [bn], [bn])
                        sch.op('dve', lambda E: E.tensor_scalar(out=W[:], in0=fvec[:], scalar1=B[:, 1:2], scalar2=None,
                                                                op0=ALU.mult), [bn], [bn + 'w'])
                        sch.op('dve', lambda E: E.tensor_tensor(out=B[:, 2:3], in0=B[:, 0:1], in1=W[:, 0:1], op=ALU.add),
                               [bn, bn + 'w'], [bn])
                        yield
                        for k in range(NIT):
                            sch.op('dve', lambda E: E.tensor_scalar(
                                out=ma[ib][:, 0:Li], in0=Ib[ib][:, 0:Li], scalar1=B[:, 2:3], scalar2=None, op0=ALU.is_ge,
                                op1=ALU.add, accum_out=B[:, 3:4]), [bn, 'I%d' % ib], [bn, 'ma%d' % ib])
                            yield
                            last = (k == NIT - 1)
                            sch.op('dve', lambda E, last=last: E.tensor_scalar(
                                out=B[:, 4:5], in0=B[:, 3:4], scalar1=float(TK), scalar2=(-1.0 if last else -0.5),
                                op0=ALU.is_ge, op1=ALU.add), [bn], [bn])
                            yield
                            sch.op('dve', lambda E, k=k, last=last: E.scalar_tensor_tensor(
                                out=(B[:, 5:6] if last else B[:, 2:3]), in0=B[:, 4:5], scalar=W[:, k:k + 1], in1=B[:, 2:3],
                                op0=ALU.mult, op1=ALU.add), [bn, bn + 'w'], [bn])
                            yield
                        thr = B[:, 5:6]
                    else:
                        thr = thr_const[:, 0:1]
                    sch.op('dve', lambda E: E.tensor_scalar(
                        out=ma[ib][:, 0:Li], in0=Ib[ib][:, 0:Li], scalar1=thr, scalar2=NEG, op0=ALU.is_lt, op1=ALU.mult),
                        [bn, 'I%d' % ib], ['ma%d' % ib])

                def post_tile(li=li, gen_tile=gen_tile):
                    pending.append(gen_tile())
                    if li % 2 == 1:
                        gens = list(pending)
                        del pending[:]
                        while gens:
                            for g_ in list(gens):
                                try:
                                    next(g_)
                                except StopIteration:
                                    gens.remove(g_)

                nsc = (Li + 511) // 512
                for si, sc0 in enumerate(range(0, Li, 512)):
                    w = min(512, Li - sc0)
                    for t in range(4):
                        xr = cnt['x'] % 2
                        cnt['x'] += 1
                        rq = cnt['R'] % 2
                        cnt['R'] += 1
                        s0 = lambda xr=xr, q=q, t=t, sc0=sc0, w=w: sch.op('pe', lambda E: [E.matmul(
                            PP[xr][:, e, 0:w], lhsT=iqc[q][64 * e:64 * e + 64, t, :], rhs=ik2[64 * e:64 * e + 64, sc0:sc0 + w],
                            start=True, stop=True) for e in range(2)], ['iqc%d' % q], ['PP%d' % xr])
                        s1 = lambda xr=xr, rq=rq, w=w: sch.op('act', lambda E: E.activation(
                            out=Rb[rq][:, :, 0:w], in_=PP[xr][:, :, 0:w], func=AF.Relu), ['PP%d' % xr], ['R%d' % rq])
                        s2 = lambda rq=rq, q=q, t=t, w=w: sch.op('pe', lambda E: [E.matmul(
                            pacc[:, 0:w], lhsT=dg[q][:, 2 * t + e, :], rhs=Rb[rq][:, e, 0:w], start=(t == 0 and e == 0),
                            stop=(t == 3 and e == 1)) for e in range(2)], ['R%d' % rq, 'dg%d' % q], ['pacc'])
                        post = None
                        if t == 3:
                            last = (si == nsc - 1)

                            def post(ib=ib, sc0=sc0, w=w, last=last, post_tile=post_tile):
                                sch.op('act', lambda E: E.activation(out=Ib[ib][:, sc0:sc0 + w], in_=pacc[:, 0:w],
                                                                     func=AF.Copy), ['pacc'], ['I%d' % ib])
                                if last:
                                    post_tile()
                        blocks.append(Blk(s0, s1, s2, pre=(pre if (si == 0 and t == 0) else None), post=post))
            run_pipe(blocks, 1)

        def tr_chunk(c):
            for li in range(4):
                i = 4 * c + li
                ib = i % NI
                for j0 in range(0, i + 1, 8):
                    n = min(8, i + 1 - j0)

                    def tr(E, ib=ib, j0=j0, n=n):
                        return [E.transpose(out=tpb[:, jj, :], in_=ma[ib][:, (j0 + jj) * 128:(j0 + jj + 1) * 128],
                                            identity=identb[:]) for jj in range(n)]
                    sch.op('pe', tr, ['ma%d' % ib], ['tpb'])
                    sch.op('act', lambda E, j0=j0, n=n, li=li: E.activation(
                        out=maT[:, j0:j0 + n, li * 128:(li + 1) * 128], in_=tpb[:, 0:n, :], func=AF.Copy), ['tpb'], ['maT'])

        def attn_chunk(c):
            cq = c % 2
            blocks = []
            nj = 4 * c + 4

            def pre():
                sch.dma('sp', Qc[cq][:], QbT[:, :, c * 512:(c + 1) * 512].rearrange("t p n -> p t n"),
                        writes=['QcB%d' % cq], chan='QcB%d' % cq)
            for t in range(4):
                g = t // 2

                def epi(t=t):
                    aq = cnt['acc'] % 2
                    cnt['acc'] += 1
                    A_ = accbs[aq]
                    an = 'accbs%d' % aq
                    sch.op('act', lambda E: E.activation(out=A_[:], in_=accb[:, :, :, 0:65], func=AF.Copy),
                           ['accb'], [an])

                    def rest(t=t, A_=A_, an=an):
                        sch.op('act', lambda E: E.activation(out=rcp[:], in_=A_[:, :, :, 64], func=AF.Ln),
                               [an], ['rcp'])
                        sch.op('act', lambda E: E.activation(out=rcp[:], in_=rcp[:], func=AF.Exp, scale=-1.0),
                               ['rcp'], ['rcp'])
                        sch.op('act', lambda E: [E.activation(
                            out=ost[0][:, li, (2 * t + e) * 64:(2 * t + e + 1) * 64], in_=A_[:, e, li, 0:64], func=AF.Copy,
                            scale=rcp[:, e, li:li + 1]) for e in range(2) for li in range(4)], [an, 'rcp'], ['ostB0'])
                        if t == 3:
                            sch.dma('pool', oS[c * 512:(c + 1) * 512, 512:1024].rearrange("(li p) e -> p li e", p=128),
                                    ost[0][:], reads=['ostB0'], writes=[], chan='st_ostB0')
                    return rest
                for j in range(nj):
                    lo = max(0, j - 4 * c)
                    off = lo * 128
                    segs = [(j + dl - 4 * c, dl) for dl in (0, 1) if 0 <= j + dl - 4 * c <= 3]
                    r = cnt['x'] % 2
                    cnt['x'] += 1
                    p_ = cnt['pt'] % 3
                    cnt['pt'] += 1

                    def qk(E, r=r, j=j, off=off, segs=segs, t=t, g=g):
                        res = [E.matmul(PP[r][:, e, off:512], lhsT=Kb[64 * e:64 * e + 64, g, j * 128:(j + 1) * 128],
                                        rhs=Qc[cq][64 * e:64 * e + 64, t, off:512], start=True, stop=False)
                               for e in range(2)]
                        for e in range(2):
                            res.append(E.matmul(PP[r][:, e, off:512], lhsT=identb[:], rhs=maT[:, j, off:512], start=False,
                                                stop=(not segs)))
                        if segs:
                            c0 = segs[0][0] * 128
                            n = 128 * len(segs)
                            d0 = segs[0][1] * 128
                            for e in range(2):
                                for T in (DThi,):
                                    res.append(E.matmul(PP[r][:, e, c0:c0 + n], lhsT=identb[:],
                                                        rhs=T[:, 4 + 2 * t + e, d0:d0 + n], start=False, stop=True))
                        return res

                    def pv(E, p_=p_, j=j, lo=lo, g=g):
                        return [E.matmul(accb[:, e, li, 0:65], lhsT=pt[p_][:, e, li * 128:(li + 1) * 128],
                                         rhs=Vs[:, j, g, 0:65], start=(j == 0 and li == 0), stop=(j == 4 * c + li),
                                         skip_group_check=True) for e in range(2) for li in range(lo, 4)]
                    s0 = lambda qk=qk, r=r: sch.op('pe', qk, ['QcB%d' % cq, 'maT'], ['PP%d' % r])
                    s1 = lambda r=r, p_=p_, off=off, t=t: sch.op('act', lambda E: [E.activation(
                        out=pt[p_][:, e, off:512], in_=PP[r][:, e, off:512], func=AF.Exp,
                        bias=b31c[:, 4 + 2 * t + e:5 + 2 * t + e]) for e in range(2)], ['PP%d' % r], ['ptB%d' % p_])
                    s2 = lambda pv=pv, p_=p_: sch.op('pe', pv, ['ptB%d' % p_], ['accb'])
                    blk = Blk(s0, s1, s2, pre=(pre if (t == 0 and j == 0) else None))
                    if j == nj - 1:
                        def post(blk=blk, epi=epi):
                            blk.post2 = epi()
                        blk.post = post
                    blocks.append(blk)
            run_pipe(blocks, 1)

        idx_chunk(0)
        tr_chunk(0)
        for c in range(NCH):
            if c + 1 < NCH:
                idx_chunk(c + 1)
            attn_chunk(c)
            if c + 1 < NCH:
                tr_chunk(c + 1)
        sch.barrier()


def phase_f(nc, sch, al, L):
    with ExitStack() as es:
        wu = al.sb(es, [128, 8, 2 * DFF], BF16, "wu")
        wd = al.sb(es, [128, 22, D], BF16, "wd")
        phase_f1(nc, sch, al, L, wu, wd)
        phase_f2(nc, sch, al, L, wu, wd)


def phase_f1(nc, sch, al, L, wu, wd):
    S, NT, NCH, b = L['S'], L['NT'], L['NCH'], L['b']
    w_up, w_down = L['w_up'], L['w_down']
    identb, eps_c, a_ffn, modc = L['identb'], L['eps_c'], L['a_ffn'], L['modc']
    x, out, oS, x1nT, w_out, modrow = L['x'], L['out'], L['oS'], L['x1nT'], L['w_out'], L['modrow']
    with ExitStack() as es:
        wo = al.sb(es, [128, 8, D], BF16, "wo")
        with ExitStack() as e2:
            grow = al.sb(e2, [128, D], F32, "grow")
            stg = [al.sb(e2, [128, D], F32, "wos%d" % i) for i in range(2)]
            sch.dma('sp', grow[:], modrow[b:b + 1, 2048:3072].partition_broadcast(128), writes=['grow'], chan='grow')
            for k in range(8):
                sch.dma('sp', stg[k % 2][:], w_out[k * 128:(k + 1) * 128, :], writes=['wos%d' % (k % 2)],
                        chan='wos%d' % (k % 2))
                sch.op('dve', lambda E, k=k: E.tensor_tensor(out=wo[:, k, :], in0=stg[k % 2][:], in1=grow[:], op=ALU.mult),
                       ['wos%d' % (k % 2), 'grow'], ['wo'])
            sch.barrier()
        ot = [al.sb(es, [128, D], BF16, "ot%d" % i) for i in range(2)]
        oT = [al.sb(es, [128, 8, 128], BF16, "oT%d" % i) for i in range(2)]
        xt = [al.sb(es, [128, D], F32, "xtf%d" % i) for i in range(2)]
        x1 = [al.sb(es, [128, D], F32, "x1%d" % i) for i in range(2)]
        x1n = [al.sb(es, [128, D], BF16, "x1n%d" % i) for i in range(2)]
        x1s = [al.sb(es, [128, 8, 128], BF16, "x1s%d" % i) for i in range(2)]
        junk = al.sb(es, [128, D], BF16, "junkF")
        ss = [al.sb(es, [128, 2], F32, "ssf%d" % i) for i in range(2)]
        tpo = [al.ps(es, [128, 8, 128], BF16, "tpo%d" % i) for i in range(2)]
        tpn = [al.ps(es, [128, 8, 128], BF16, "tpn%d" % i) for i in range(2)]
        pso = [[al.ps(es, [128, 512], F32, "pso%d%d" % (i, hf)) for hf in range(2)] for i in range(2)]
        growf = al.sb(es, [128, D], F32, "growf")
        pstg = [al.sb(es, [128, 1408], F32, "pstg%d" % i) for i in range(2)]
        sch.dma('sp', growf[:], modrow[b:b + 1, 5120:6144].partition_broadcast(128), writes=['growf'], chan='growf')
        steps = []
        for k in range(8):
            for q4 in range(4):
                steps.append(('cast', w_up[k * 128:(k + 1) * 128, q4 * 1408:(q4 + 1) * 1408],
                              wu[:, k, q4 * 1408:(q4 + 1) * 1408]))
        for ct in range(22):
            steps.append(('mul', w_down[ct * 128:(ct + 1) * 128, :], wd[:, ct, :]))
        nstep = [0]

        def prep_step():
            if nstep[0] >= len(steps):
                return
            kind, src, dst = steps[nstep[0]]
            q = nstep[0] % 2
            n_ = nstep[0]
            nstep[0] += 1
            if kind == 'cast':
                sch.dma('sp', pstg[q][:], src, writes=['pstg%d' % q], chan='pstg%d' % q)
                sch.op('act', lambda E: E.activation(out=dst, in_=pstg[q][:], func=AF.Copy), ['pstg%d' % q],
                       ['wprep%d' % n_])
            else:
                sch.dma('sp', pstg[q][:, 0:D], src, writes=['pstg%d' % q], chan='pstg%d' % q)
                sch.op('dve', lambda E: E.tensor_tensor(out=dst, in0=pstg[q][:, 0:D], in1=growf[:], op=ALU.mult),
                       ['pstg%d' % q, 'growf'], ['wprep%d' % n_])
        def part_a(tt):
            r = tt % 2
            rows = slice(tt * 128, (tt + 1) * 128)
            sch.dma('sp', ot[r][:], oS[rows, :], writes=['ot%d' % r], chan='ot%d' % r)
            sch.dma('sp', xt[r][:], x[b, rows, :], writes=['xtf%d' % r], chan='xtf%d' % r)
            sch.op('pe', lambda E, r=r: [E.transpose(out=tpo[r][:, k, :], in_=ot[r][:, k * 128:(k + 1) * 128],
                                                     identity=identb[:]) for k in range(8)],
                   ['ot%d' % r], ['tpo%d' % r])
            sch.op('act', lambda E, r=r: E.activation(out=oT[r][:], in_=tpo[r][:], func=AF.Copy),
                   ['tpo%d' % r], ['oT%d' % r])
            for hf in range(2):
                sch.op('pe', lambda E, r=r, hf=hf: [E.matmul(pso[r][hf][:], lhsT=oT[r][:, k, :],
                                                             rhs=wo[:, k, hf * 512:(hf + 1) * 512],
                                                             start=(k == 0), stop=(k == 7)) for k in range(8)],
                       ['oT%d' % r], ['pso%d%d' % (r, hf)])

        def part_a2(tt):
            r = tt % 2
            rows = slice(tt * 128, (tt + 1) * 128)
            for hf in range(2):
                sch.op('dve', lambda E, r=r, hf=hf: E.tensor_tensor(
                    out=x1[r][:, hf * 512:(hf + 1) * 512], in0=pso[r][hf][:], in1=xt[r][:, hf * 512:(hf + 1) * 512],
                    op=ALU.add), ['pso%d%d' % (r, hf), 'xtf%d' % r], ['x1%d' % r])
            sch.dma('pool', out[b, rows, :], x1[r][:], reads=['x1%d' % r], writes=[], chan='st_x1%d' % r)
            sch.op('dve', lambda E, r=r: E.scalar_tensor_tensor(
                out=junk[:], in0=x1[r][:], scalar=1.0, in1=x1[r][:], op0=ALU.mult, op1=ALU.mult,
                accum_out=ss[r][:, 0:1]), ['x1%d' % r], ['junkF', 'ssf%d' % r])
            sch.op('act', lambda E, r=r: E.activation(out=ss[r][:, 1:2], in_=ss[r][:, 0:1], func=AF.Ln, scale=1.0 / D,
                                                      bias=eps_c[:]), ['ssf%d' % r], ['ssg%d' % r])
            sch.op('act', lambda E, r=r: E.activation(out=ss[r][:, 1:2], in_=ss[r][:, 1:2], func=AF.Exp, scale=-0.5),
                   ['ssg%d' % r], ['ssg%d' % r])
            sch.op('dve', lambda E, r=r: E.tensor_scalar(out=x1n[r][:], in0=x1[r][:], scalar1=ss[r][:, 1:2],
                                                         scalar2=None, op0=ALU.mult),
                   ['x1%d' % r, 'ssg%d' % r], ['x1n%d' % r])

        def part_b(tt):
            r = tt % 2
            rows = slice(tt * 128, (tt + 1) * 128)
            sch.op('pe', lambda E, r=r: [E.transpose(out=tpn[r][:, k, :], in_=x1n[r][:, k * 128:(k + 1) * 128],
                                                     identity=identb[:]) for k in range(8)],
                   ['x1n%d' % r], ['tpn%d' % r])
            sch.op('act', lambda E, r=r: [E.activation(out=x1s[r][:, k, :], in_=tpn[r][:, k, :], func=AF.Identity,
                                                       scale=a_ffn[:, k:k + 1], bias=modc[:, 24 + k:25 + k])
                                          for k in range(8)], ['tpn%d' % r], ['x1s%d' % r])
            sch.dma('pool', x1nT[:, :, rows].rearrange("k p n -> p k n"), x1s[r][:], reads=['x1s%d' % r], writes=[],
                    chan='st_x1s%d' % r)

        for tt in range(NT + 2):
            if tt < NT:
                for _ in range((len(steps) + NT - 1) // NT):
                    prep_step()
                part_a(tt)
            if 1 <= tt <= NT:
                part_a2(tt - 1)
            if tt >= 2:
                part_b(tt - 2)
        sch.barrier()


def phase_f2(nc, sch, al, L, wu, wd):
    S, NT, NCH, b = L['S'], L['NT'], L['NCH'], L['b']
    cw, cb = L['cw_sb'], L['cb_sb']
    out, x1nT = L['out'], L['x1nT']
    with ExitStack() as es:
        gT = al.sb(es, [128, 22, 512], BF16, "gT")
        gpre = al.sb(es, [128, 2, 2, 512], BF16, "gpre")
        xc = [al.sb(es, [128, 8, 512], BF16, "xc%d" % i) for i in range(2)]
        u = [al.sb(es, [128, 514], F32, "u%d" % i) for i in range(2)]
        y = [al.sb(es, [128, 512], F32, "y%d" % i) for i in range(3)]
        sg = [al.sb(es, [128, 512], BF16, "sg%d" % i) for i in range(2)]
        x1t = [al.sb(es, [128, D], F32, "x1t%d" % i) for i in range(2)]
        hal = al.sb(es, [128, 44, 2], F32, "hal")
        pu = [al.ps(es, [128, 512], F32, "pu%d" % i) for i in range(4)]
        pd = [[al.ps(es, [128, 512], F32, "pd%d%d" % (i, hf)) for hf in range(2)] for i in range(2)]
        sch.op('pool', lambda E: E.memset(hal[:], 0.0), [], ['hal'])
        nu = 0
        ny = 0
        nt_ = 0
        def load_xc(c):
            sch.dma('sp', xc[c % 2][:], x1nT[:, :, c * 512:(c + 1) * 512].rearrange("k p n -> p k n"),
                    writes=['xc%d' % (c % 2)], chan='xc%d' % (c % 2))

        cntf = {'u': 0, 'y': 0}
        NPRE = 2

        def up_pair(c, ct):
            r = c % 2
            ys = []
            for hf in range(2):
                col = hf * 22 + ct
                p_ = cntf['u'] % 4
                uq = cntf['u'] % 2
                cntf['u'] += 1
                yq = cntf['y'] % 3
                cntf['y'] += 1
                ys.append(yq)
                sch.op('pe', lambda E, p_=p_, col=col, r=r: [E.matmul(
                    pu[p_][:], lhsT=wu[:, k, col * 128:(col + 1) * 128], rhs=xc[r][:, k, :], start=(k == 0),
                    stop=(k == 7)) for k in range(8)], ['xc%d' % r], ['pu%d' % p_])
                sch.op('act', lambda E, p_=p_, uq=uq: E.activation(out=u[uq][:, 2:514], in_=pu[p_][:], func=AF.Copy),
                       ['pu%d' % p_], ['u%d' % uq])
                sch.op('act', lambda E, p_=p_, yq=yq, col=col: E.activation(
                    out=y[yq][:], in_=pu[p_][:], func=AF.Identity, scale=cw[:, col, 2:3], bias=cb[:, col:col + 1]),
                    ['pu%d' % p_], ['y%d' % yq])
                sch.op('pool', lambda E, uq=uq, col=col: E.tensor_copy(out=u[uq][:, 0:2], in_=hal[:, col, :]),
                       ['hal'], ['u%d' % uq])
                sch.op('pool', lambda E, uq=uq, col=col: E.tensor_copy(out=hal[:, col, :], in_=u[uq][:, 512:514]),
                       ['u%d' % uq], ['hal'])
                for jj in (1, 0):
                    sch.op('dve', lambda E, uq=uq, yq=yq, col=col, jj=jj: E.scalar_tensor_tensor(
                        out=y[yq][:], in0=u[uq][:, jj:jj + 512], scalar=cw[:, col, jj:jj + 1], in1=y[yq][:],
                        op0=ALU.mult, op1=ALU.add), ['u%d' % uq, 'y%d' % yq], ['y%d' % yq])
            sq_ = ct % 2
            sch.op('act', lambda E, sq_=sq_, yg=ys[0]: E.activation(out=sg[sq_][:], in_=y[yg][:], func=AF.Silu),
                   ['y%d' % ys[0]], ['sg%d' % sq_])
            gdst = gpre[:, c % 2, ct, :] if ct < NPRE else gT[:, ct, :]
            gname = ('gpre', c % 2, ct) if ct < NPRE else ('gT', ct)
            sch.op('pool', lambda E, sq_=sq_, yv=ys[1], gdst=gdst: E.tensor_tensor(out=gdst, in0=sg[sq_][:],
                                                                                   in1=y[yv][:], op=ALU.mult),
                   ['sg%d' % sq_, 'y%d' % ys[1]], [gname])

        load_xc(0)
        for c in range(NCH):
            for ct in range(NPRE if c > 0 else 0, 22):
                up_pair(c, ct)
            if c + 1 < NCH:
                load_xc(c + 1)

            def load_x1t(li):
                rws = slice(c * 512 + li * 128, c * 512 + (li + 1) * 128)
                sch.dma('sp', x1t[li % 2][:], out[b, rws, :], writes=['x1t%d' % (li % 2)], chan='x1t%d' % (li % 2))
            load_x1t(0)
            load_x1t(1)
            if c + 1 < NCH:
                for ct in range(NPRE):
                    up_pair(c + 1, ct)
            for li in range(4):
                tq = li % 2
                rows = slice(c * 512 + li * 128, c * 512 + (li + 1) * 128)
                for hf in range(2):
                    sch.op('pe', lambda E, tq=tq, hf=hf, li=li, c=c: [E.matmul(
                        pd[tq][hf][:], lhsT=(gpre[:, c % 2, ct, li * 128:(li + 1) * 128] if ct < NPRE
                                             else gT[:, ct, li * 128:(li + 1) * 128]),
                        rhs=wd[:, ct, hf * 512:(hf + 1) * 512],
                        start=(ct == 0), stop=(ct == 21)) for ct in range(22)],
                        [(('gpre', c % 2, ct) if ct < NPRE else ('gT', ct)) for ct in range(22)],
                        ['pd%d%d' % (tq, hf)])
                    sch.op('dve', lambda E, tq=tq, hf=hf: E.tensor_tensor(
                        out=x1t[tq][:, hf * 512:(hf + 1) * 512], in0=pd[tq][hf][:], in1=x1t[tq][:, hf * 512:(hf + 1) * 512],
                        op=ALU.add), ['pd%d%d' % (tq, hf), 'x1t%d' % tq], ['x1t%d' % tq])
                sch.dma('sp', out[b, rows, :], x1t[tq][:], reads=['x1t%d' % tq], writes=[], chan='st_x1t%d' % tq)
                if li + 2 < 4:
                    load_x1t(li + 2)
        sch.barrier()


def t5_bucket_np(n):
    n = np.maximum(n, 0)
    nf = np.maximum(n, 1).astype(np.float32)
    large = 16 + (np.log(nf / np.float32(16)) / np.float32(math.log(128 / 16)) * np.float32(16)).astype(np.int32)
    large = np.minimum(large, 31)
    return np.where(n < 16, n, large)


def prep_shared(inp):
    f = lambda a: np.ascontiguousarray(a, dtype=np.float32)
    rb = np.asarray(inp['rel_bias'], np.float32)
    s_ = np.arange(128)[:, None]
    t_ = np.arange(256)[None, :]
    bk = t5_bucket_np(t_ - s_)
    biasT = rb[bk]
    cm = np.where(t_ >= s_, 0.0, NEG).astype(np.float32)
    pq = np.arange(128)
    sh = {
        'w_ada': f(inp['w_ada'][0]), 'b_ada': f(inp['b_ada']),
        'g_attn_c': f(np.asarray(inp['g_attn'][0]).reshape(8, 128).T),
        'g_ffn_c': f(np.asarray(inp['g_ffn'][0]).reshape(8, 128).T),
        'w_in': f(inp['w_in'][0]),
        'qkg': f(np.stack([np.tile(np.asarray(inp[k][0]), 2) for k in
                           ('q_norm_a', 'k_norm_a', 'q_norm_b', 'k_norm_b')], 1)),
        'lamv': f(np.asarray(inp['lam_vecs'][0]).reshape(1, 256)),
        'subln': f(np.asarray(inp['subln_a'])),
        'w_out': f(inp['w_out'][0]), 'w_up': f(inp['w_up'][0]),
        'conv_wc': f(np.asarray(inp['conv_w'][0]).T.reshape(44, 128, 3).transpose(1, 0, 2)),
        'conv_bc': f(np.asarray(inp['conv_b'][0]).reshape(44, 128).T),
        'w_down': f(inp['w_down'][0]),
        'biasT': f(biasT.transpose(0, 2, 1)),
        'b31': f(rb[31:32, :]),
        'cmask': cm,
        'identf': np.eye(128, dtype=np.float32),
        'blk1': f((pq[:, None] // 64) == (pq[None, :] // 64)),
    }
    return sh


def core_inputs(inp, sh, rows):
    m = dict(sh)
    xs = np.ascontiguousarray(np.asarray(inp['x'])[rows], dtype=np.float32)
    cs = np.asarray(inp['c'], np.float32)[rows]
    m['x'] = xs
    m['cT'] = np.ascontiguousarray(cs.reshape(len(rows), 8, 128).transpose(2, 1, 0))
    return m


_NC_CACHE = {}


def kernel(**inputs):
    B, S, _ = inputs['x'].shape
    ncores = 8
    NB = B // ncores
    key = (S, NB)
    if key not in _NC_CACHE:
        _NC_CACHE[key] = build(S, NB)
    nc = _NC_CACHE[key]
    sh = prep_shared(inputs)
    in_maps = [core_inputs(inputs, sh, list(range(i * NB, (i + 1) * NB))) for i in range(ncores)]
    res = run_bass_kernel_spmd(nc, in_maps, core_ids=list(range(ncores)))
    return np.concatenate([np.asarray(r['out']) for r in res.results], axis=0).astype(np.float32)
```

```python
from contextlib import ExitStack
import math
import numpy as np
import concourse.bass as bass
import concourse.mybir as mybir
from concourse.bass_utils import run_bass_kernel_spmd

F32 = mybir.dt.float32
BF16 = mybir.dt.bfloat16
AF = mybir.ActivationFunctionType
ALU = mybir.AluOpType

D = 1024
DFF = 2816
INC = 2888
NEG = -30000.0
EPS = 1e-6
LAM_INIT = 0.8 - 0.6
TOPK = 256
NIT = 13


class Sched:
    def __init__(self, nc):
        self.nc = nc
        self.engs = {'pe': nc.tensor, 'act': nc.scalar, 'dve': nc.vector, 'pool': nc.gpsimd, 'sp': nc.sync}
        self.sems, self.cnt, self.mult = {}, {}, {}
        self.seen = {e: {} for e in self.engs}
        self.lastw, self.readers = {}, {}
        for e in ['pe', 'act', 'dve', 'pool']:
            self._chan(e, 1)

    def _chan(self, name, mult):
        if name not in self.sems:
            self.sems[name] = self.nc.alloc_semaphore("s%d" % len(self.sems))
            self.cnt[name] = 0
            self.mult[name] = mult

    def op(self, eng, fn, reads=(), writes=(), chan=None):
        deps = {}

        def add(c, n):
            if deps.get(c, 0) < n:
                deps[c] = n
        for r in reads:
            for c, n in self.lastw.get(r, {}).items():
                add(c, n)
        for w in writes:
            for c, n in self.lastw.get(w, {}).items():
                add(c, n)
            for c, n in self.readers.get(w, {}).items():
                add(c, n)
        E = self.engs[eng]
        seen = self.seen[eng]
        need = []
        for c, n in deps.items():
            if eng == 'pe' and c == 'pe':
                continue
            if seen.get(c, 0) < n:
                need.append((c, n))
                seen[c] = n
        for c, n in need[1:]:
            E.wait_ge(self.sems[c], n * self.mult[c])
        r = fn(E)
        if isinstance(r, (tuple, list)):
            first, last = r[0], r[-1]
        else:
            first = last = r
        if need:
            c, n = need[0]
            first._wait_ge(self.sems[c], n * self.mult[c])
        ch = chan or eng
        if ch not in self.sems:
            self._chan(ch, 16)
        self.cnt[ch] += 1
        last.then_inc(self.sems[ch], self.mult[ch])
        me = (ch, self.cnt[ch])
        for w in writes:
            self.lastw[w] = {me[0]: me[1]}
            self.readers[w] = {}
        for r_ in reads:
            self.readers.setdefault(r_, {})[me[0]] = me[1]

    def dma(self, q, out, in_, reads=(), writes=(), chan=None, **kw):
        assert chan is not None
        self.op(q, lambda E: E.dma_start(out=out, in_=in_, **kw), reads, writes, chan=chan)

    def barrier(self, engines=None):
        for e in (engines or self.engs):
            E = self.engs[e]
            for c, n in self.cnt.items():
                if n > 0 and self.seen[e].get(c, 0) < n:
                    E.wait_ge(self.sems[c], n * self.mult[c])
                    self.seen[e][c] = n
        if engines is None:
            self.lastw, self.readers = {}, {}


class Alloc:
    def __init__(self, nc):
        self.nc = nc
        self.n = 0

    def sb(self, es, shape, dt, name="t"):
        self.n += 1
        return es.enter_context(self.nc.sbuf_tensor("%s_%d" % (name, self.n), list(shape), dt))

    def ps(self, es, shape, dt=F32, name="p"):
        self.n += 1
        return es.enter_context(self.nc.psum_tensor("%s_%d" % (name, self.n), list(shape), dt))


class Blk:
    __slots__ = ('pre', 's0', 's1', 's2', 'post', 'post2')

    def __init__(self, s0, s1, s2, pre=None, post=None, post2=None):
        self.pre, self.s0, self.s1, self.s2, self.post, self.post2 = pre, s0, s1, s2, post, post2


def run_pipe(blocks, depth=1):
    n = len(blocks)
    for i in range(min(depth, n)):
        if blocks[i].pre:
            blocks[i].pre()
        blocks[i].s0()
    for i in range(n):
        if i + depth < n:
            if blocks[i + depth].pre:
                blocks[i + depth].pre()
            blocks[i + depth].s0()
        blocks[i].s1()
        blocks[i].s2()
        if blocks[i].post:
            blocks[i].post()
        if i >= 3 and blocks[i - 3].post2:
            blocks[i - 3].post2()
    for i in range(max(0, n - 3), n):
        if blocks[i].post2:
            blocks[i].post2()


def build(S, NB, debug=False):
    NT = S // 128
    NCH = S // 512
    nc = bass.Bass("TRN2", target_bir_lowering=False)
    sch = Sched(nc)
    al = Alloc(nc)

    def din(name, shape, dt=F32):
        return nc.dram_tensor(name, list(shape), dt, kind="ExternalInput").ap()

    def dscr(name, shape, dt):
        return nc.dram_tensor(name, list(shape), dt, kind="ExternalOutput" if debug else "Internal").ap()

    x = din("x", [NB, S, D])
    cT = din("cT", [128, 8, NB])
    w_ada = din("w_ada", [D, 6 * D])
    b_ada = din("b_ada", [1, 6 * D])
    g_attn_c = din("g_attn_c", [128, 8])
    g_ffn_c = din("g_ffn_c", [128, 8])
    w_in = din("w_in", [D, INC])
    qkg = din("qkg", [128, 4])
    lamv = din("lamv", [1, 256])
    subln = din("subln", [1, 128])
    w_out = din("w_out", [D, D])
    w_up = din("w_up", [D, 2 * DFF])
    conv_wc = din("conv_wc", [128, 44, 3])
    conv_bc = din("conv_bc", [128, 44])
    w_down = din("w_down", [DFF, D])
    biasT = din("biasT", [128, 12, 256])
    b31 = din("b31", [1, 12])
    cmask = din("cmask", [128, 256])
    identf = din("identf", [128, 128])
    blk1 = din("blk1", [128, 128])
    out = nc.dram_tensor("out", [NB, S, D], F32, kind="ExternalOutput").ap()

    modrow = dscr("modrow", [NB, 6 * D], F32)
    QaT = dscr("QaT", [4, 128, S], BF16)
    KaT = dscr("KaT", [4, 128, S], BF16)
    Va = dscr("Va", [S, 512], BF16)
    QbT = dscr("QbT", [4, 128, S], BF16)
    KbT = dscr("KbT", [128, S], BF16)
    Vb = dscr("Vb", [S, 128], BF16)
    iqT = dscr("iqT", [4, 128, S], BF16)
    ikT = dscr("ikT", [128, S], BF16)
    iwS = dscr("iwS", [S, 8], F32)
    oS = dscr("oS", [S, D], BF16)
    x1nT = dscr("x1nT", [8, 128, S], BF16)

    with ExitStack() as g:
        identb = al.sb(g, [128, 128], BF16, "identb")
        identf_sb = al.sb(g, [128, 128], F32, "identf")
        blk1b = al.sb(g, [128, 128], BF16, "blk1b")
        cmask_sb = al.sb(g, [128, 256], F32, "cmask")
        negtri = al.sb(g, [128, 128], F32, "negtri")
        b31c = al.sb(g, [128, 12], F32, "b31c")
        DThi = al.sb(g, [128, 12, 256], BF16, "DThi")
        qkg_sb = al.sb(g, [128, 4], F32, "qkg")
        neg_lam = al.sb(g, [128, 1], F32, "neglam")
        subln_row = al.sb(g, [128, 128], F32, "sublnrow")
        gattn_sb = al.sb(g, [128, 8], F32, "gattn")
        gffn_sb = al.sb(g, [128, 8], F32, "gffn")
        cw_sb = al.sb(g, [128, 44, 3], F32, "cw")
        cb_sb = al.sb(g, [128, 44], F32, "cb")
        thr_const = al.sb(g, [128, 1], F32, "thrc")
        eps_c = al.sb(g, [128, 1], F32, "epsc")
        fvec = al.sb(g, [128, NIT], F32, "fvec")

        with ExitStack() as es:
            tmpf = al.sb(es, [128, 128], F32, "tmpf")
            bT = al.sb(es, [128, 12, 256], F32, "bT")
            lv = al.sb(es, [128, 256], F32, "lv")
            lsum = al.sb(es, [128, 2], F32, "lsum")
            junk = al.sb(es, [128, 64], F32, "junk")
            sch.dma('sp', identf_sb[:], identf[:, :], writes=['identf'], chan='c0')
            sch.dma('sp', tmpf[:], blk1[:, :], writes=['tmpf'], chan='c1')
            sch.dma('sp', cmask_sb[:], cmask[:, :], writes=['cmask'], chan='c2')
            sch.dma('sp', b31c[:], b31[0:1, :].partition_broadcast(128), writes=['b31c'], chan='c3')
            sch.dma('sp', bT[:], biasT[:, :, :], writes=['bT'], chan='c4')
            sch.dma('sp', qkg_sb[:], qkg[:, :], writes=['qkg'], chan='c5')
            sch.dma('sp', lv[:], lamv[0:1, :].partition_broadcast(128), writes=['lv'], chan='c6')
            sch.dma('sp', subln_row[:], subln[0:1, :].partition_broadcast(128), writes=['subln'], chan='c7')
            sch.dma('sp', gattn_sb[:], g_attn_c[:, :], writes=['gattn'], chan='c8')
            sch.dma('sp', gffn_sb[:], g_ffn_c[:, :], writes=['gffn'], chan='c9')
            sch.dma('sp', cw_sb[:], conv_wc[:, :, :], writes=['cw'], chan='c10')
            sch.dma('sp', cb_sb[:], conv_bc[:, :], writes=['cb'], chan='c11')
            sch.op('dve', lambda E: E.tensor_copy(out=identb[:], in_=identf_sb[:]), ['identf'], ['identb'])
            sch.op('dve', lambda E: E.tensor_copy(out=blk1b[:], in_=tmpf[:]), ['tmpf'], ['blk1b'])
            sch.op('dve', lambda E: E.tensor_scalar(out=qkg_sb[:, 0:1], in0=qkg_sb[:, 0:1], scalar1=0.125, scalar2=None,
                                                    op0=ALU.mult), ['qkg'], ['qkg'])
            sch.op('dve', lambda E: E.tensor_scalar(out=qkg_sb[:, 2:3], in0=qkg_sb[:, 2:3], scalar1=0.125, scalar2=None,
                                                    op0=ALU.mult), ['qkg'], ['qkg'])
            sch.op('dve', lambda E: E.memset(thr_const[:], -1e29), [], ['thrc'])
            sch.op('dve', lambda E: E.memset(eps_c[:], EPS), [], ['epsc'])
            for k in range(NIT):
                sch.op('dve', lambda E, k=k: E.memset(fvec[:, k:k + 1], 0.5 ** (k + 1)), [], ['fvec'])
            for h in range(12):
                sch.op('dve', lambda E, h=h: E.scalar_tensor_tensor(
                    out=bT[:, h, :], in0=bT[:, h, :], scalar=b31c[:, h:h + 1], in1=cmask_sb[:],
                    op0=ALU.subtract, op1=ALU.add), ['bT', 'b31c', 'cmask'], ['bT'])
            sch.op('dve', lambda E: E.tensor_copy(out=DThi[:], in_=bT[:]), ['bT'], ['DThi'])
            for i in range(2):
                sch.op('dve', lambda E, i=i: E.scalar_tensor_tensor(
                    out=junk[:], in0=lv[:, 128 * i:128 * i + 64], scalar=1.0, in1=lv[:, 128 * i + 64:128 * i + 128],
                    op0=ALU.mult, op1=ALU.mult, accum_out=lsum[:, i:i + 1]), ['lv'], ['junk', 'lsum'])
            sch.op('act', lambda E: E.activation(out=lsum[:], in_=lsum[:], func=AF.Exp), ['lsum'], ['lsum'])
            sch.op('dve', lambda E: E.tensor_tensor(out=neg_lam[:], in0=lsum[:, 1:2], in1=lsum[:, 0:1], op=ALU.subtract),
                   ['lsum'], ['neglam'])
            sch.op('dve', lambda E: E.tensor_scalar(out=neg_lam[:], in0=neg_lam[:], scalar1=-LAM_INIT, scalar2=None,
                                                    op0=ALU.add), ['neglam'], ['neglam'])
            sch.op('dve', lambda E: E.tensor_scalar(out=subln_row[:], in0=subln_row[:], scalar1=1.0 - LAM_INIT,
                                                    scalar2=None, op0=ALU.mult), ['subln'], ['subln'])
            with ExitStack() as e2:
                pt = al.ps(e2, [128, 128], F32, "ptri")
                sch.op('pe', lambda E: E.transpose(out=pt[:], in_=cmask_sb[:, 0:128], identity=identf_sb[:]),
                       ['cmask', 'identf'], ['ptri'])
                sch.op('dve', lambda E: E.tensor_scalar(out=negtri[:], in0=pt[:], scalar1=1e30 / 30000.0, scalar2=None,
                                                        op0=ALU.mult), ['ptri'], ['negtri'])
                sch.barrier()

        with ExitStack() as es:
            sc = al.sb(es, [128, 8, NB], F32, "sc")
            wb = [al.sb(es, [128, 8, 512], F32, "wada%d" % i) for i in range(2)]
            mrow = al.sb(es, [NB, 6 * D], F32, "mrow")
            brow = al.sb(es, [NB, 6 * D], F32, "brow")
            pm = [al.ps(es, [128, 512], F32, "pm%d" % i) for i in range(2)]
            sch.dma('sp', sc[:], cT[:, :, :], writes=['sc'], chan='c0')
            sch.dma('sp', brow[:], b_ada[0:1, :].partition_broadcast(NB), writes=['brow'], chan='c1')
            sch.op('act', lambda E: E.activation(out=sc[:], in_=sc[:], func=AF.Silu), ['sc'], ['sc'])
            for cc in range(12):
                wt = wb[cc % 2]
                sch.dma('sp', wt[:], w_ada[:, cc * 512:(cc + 1) * 512].rearrange("(k p) n -> p k n", p=128),
                        writes=['wada%d' % (cc % 2)], chan='wada%d' % (cc % 2))

                def mm(E, wt=wt, cc=cc):
                    r = []
                    for k in range(8):
                        r.append(E.matmul(pm[cc % 2][0:NB, :], lhsT=sc[:, k, :], rhs=wt[:, k, :],
                                          start=(k == 0), stop=(k == 7)))
                    return r
                sch.op('pe', mm, ['sc', 'wada%d' % (cc % 2)], ['pm%d' % (cc % 2)])
                sch.op('dve', lambda E, cc=cc: E.tensor_tensor(
                    out=mrow[:, cc * 512:(cc + 1) * 512], in0=pm[cc % 2][0:NB, :], in1=brow[:, cc * 512:(cc + 1) * 512],
                    op=ALU.add), ['pm%d' % (cc % 2), 'brow'], ['mrow'])
            sch.dma('sp', modrow[:, :], mrow[:], reads=['mrow'], writes=['modrow'], chan='c2')
            sch.barrier()

        for b in range(NB):
            with ExitStack() as eb:
                modc = al.sb(eb, [128, 48], F32, "modc")
                a_attn = al.sb(eb, [128, 8], F32, "aattn")
                a_ffn = al.sb(eb, [128, 8], F32, "affn")
                with ExitStack() as em:
                    mt = al.sb(em, [48, 128], F32, "mt")
                    pmt = al.ps(em, [128, 48], F32, "pmt")
                    sch.dma('sp', mt[:], modrow[b, :].rearrange("(t p) -> t p", p=128), writes=['mt'], chan='modc')
                    sch.op('pe', lambda E: E.transpose(out=pmt[:], in_=mt[:], identity=identf_sb[0:48, 0:48]),
                           ['mt'], ['pmt'])
                    sch.op('dve', lambda E: E.tensor_copy(out=modc[:], in_=pmt[:]), ['pmt'], ['modc'])
                    sch.barrier()
                sch.op('dve', lambda E: E.scalar_tensor_tensor(out=a_attn[:], in0=modc[:, 8:16], scalar=1.0,
                                                               in1=gattn_sb[:], op0=ALU.add, op1=ALU.mult),
                       ['modc'], ['aattn'])
                sch.op('dve', lambda E: E.scalar_tensor_tensor(out=a_ffn[:], in0=modc[:, 32:40], scalar=1.0,
                                                               in1=gffn_sb[:], op0=ALU.add, op1=ALU.mult),
                       ['modc'], ['affn'])
                sch.barrier()
                phase_proj(nc, sch, al, locals())
                phase_attn_a(nc, sch, al, locals())
                phase_attn_b(nc, sch, al, locals())
                phase_f(nc, sch, al, locals())
        sch.barrier()
    return nc


def phase_proj(nc, sch, al, L):
    S, NT, NCH, b = L['S'], L['NT'], L['NCH'], L['b']
    x, w_in = L['x'], L['w_in']
    identb, blk1b, qkg_sb = L['identb'], L['blk1b'], L['qkg_sb']
    a_attn, modc = L['a_attn'], L['modc']
    with ExitStack() as es:
        win = al.sb(es, [128, 8, INC], BF16, "win")
        with ExitStack() as e2:
            stg = [al.sb(e2, [128, INC], F32, "wstg%d" % i) for i in range(2)]
            for k in range(8):
                sch.dma('sp', stg[k % 2][:], w_in[k * 128:(k + 1) * 128, :], writes=['wstg%d' % (k % 2)],
                        chan='wstg%d' % (k % 2))
                if k % 2 == 0:
                    sch.op('dve', lambda E, k=k: E.tensor_copy(out=win[:, k, :], in_=stg[k % 2][:]),
                           ['wstg%d' % (k % 2)], ['win%d' % k])
                else:
                    sch.op('act', lambda E, k=k: E.activation(out=win[:, k, :], in_=stg[k % 2][:], func=AF.Copy),
                           ['wstg%d' % (k % 2)], ['win%d' % k])
            sch.barrier()
        xt = [al.sb(es, [128, D], F32, "xt%d" % i) for i in range(2)]
        xn = [al.sb(es, [128, D], BF16, "xn%d" % i) for i in range(2)]
        junk = al.sb(es, [128, D], BF16, "junk")
        ss = al.sb(es, [128, 2], F32, "ss")
        hT = [al.sb(es, [128, 8, 512], BF16, "hT%d" % i) for i in range(2)]
        qsb = [al.sb(es, [128, 512], F32, "qsb%d" % i) for i in range(3)]
        sq = [al.sb(es, [128, 512], BF16, "sq%d" % i) for i in range(3)]
        lr = [al.sb(es, [128, 512], F32, "lr%d" % i) for i in range(3)]
        stF = [al.sb(es, [128, 512], BF16, "stF%d" % i) for i in range(3)]
        stV = [al.sb(es, [128, 512], BF16, "stV%d" % i) for i in range(2)]
        stB = [al.sb(es, [128, 128], BF16, "stB%d" % i) for i in range(2)]
        stW = [al.sb(es, [128, 8], F32, "stW%d" % i) for i in range(2)]
        tp = [al.ps(es, [128, 8, 128], BF16, "tp%d" % i) for i in range(2)]
        pp = [al.ps(es, [128, 512], F32, "pp%d" % i) for i in range(3)]
        pss = [al.ps(es, [128, 512], F32, "pss%d" % i) for i in range(2)]
        pB = al.ps(es, [128, 136], F32, "pB")

        fm = []
        for t in range(4):
            fm.append((L['QaT'], t, 128 * t, 128, 0))
        for t in range(4):
            fm.append((L['KaT'], t, 512 + 128 * t, 128, 1))
        for t in range(4):
            fm.append((L['QbT'], t, 1536 + 128 * t, 128, 2))
        fm.append((L['KbT'], None, 2048, 128, 3))
        for t in range(4):
            fm.append((L['iqT'], t, 2304 + 128 * t, 128, None))
        fm.append((L['ikT'], None, 2816, 64, None))

        nfm = 0
        ntok = 0
        pend = [None]
        npss = [0]

        def prep_tile(c, tl):
            h = hT[c % 2]
            hn = 'hT%d' % (c % 2)
            tt = c * 4 + tl
            r = tt % 2
            sch.dma('sp', xt[r][:], x[b, tt * 128:(tt + 1) * 128, :], writes=['xt%d' % r], chan='xt%d' % r)
            sch.op('dve', lambda E, r=r: E.scalar_tensor_tensor(
                out=junk[:], in0=xt[r][:], scalar=1.0, in1=xt[r][:], op0=ALU.mult, op1=ALU.mult,
                accum_out=ss[:, 0:1]), ['xt%d' % r], ['junk', 'ss'])
            sch.op('act', lambda E: E.activation(out=ss[:, 1:2], in_=ss[:, 0:1], func=AF.Ln, scale=1.0 / D,
                                                 bias=L['eps_c'][:]), ['ss'], ['ss1'])
            sch.op('act', lambda E: E.activation(out=ss[:, 1:2], in_=ss[:, 1:2], func=AF.Exp, scale=-0.5),
                   ['ss1'], ['ss1'])
            sch.op('dve', lambda E, r=r: E.tensor_scalar(out=xn[r][:], in0=xt[r][:], scalar1=ss[:, 1:2],
                                                         scalar2=None, op0=ALU.mult),
                   ['xt%d' % r, 'ss1'], ['xn%d' % r])

        def prep_tile_b(c, tl):
            h = hT[c % 2]
            hn = 'hT%d' % (c % 2)
            tt = c * 4 + tl
            r = tt % 2

            def tr(E, r=r):
                res = []
                for k in range(8):
                    res.append(E.transpose(out=tp[r][:, k, :], in_=xn[r][:, k * 128:(k + 1) * 128],
                                           identity=identb[:]))
                return res
            sch.op('pe', tr, ['xn%d' % r, 'identb'], ['tp%d' % r])

            def ev(E, r=r, tl=tl, h=h):
                res = []
                for k in range(8):
                    res.append(E.activation(out=h[:, k, tl * 128:(tl + 1) * 128], in_=tp[r][:, k, :],
                                            func=AF.Identity, scale=a_attn[:, k:k + 1], bias=modc[:, k:k + 1]))
                return res
            sch.op('act', ev, ['tp%d' % r, 'aattn', 'modc'], [hn])

        for tl in range(4):
            prep_tile(0, tl)
            prep_tile_b(0, tl)
        for c in range(NCH):
            h = hT[c % 2]
            hn = 'hT%d' % (c % 2)
            nfm_c = 0
            for (dst, t, c0, nr, gi) in fm:
                if c + 1 < NCH and nfm_c in (0, 4, 8, 12):
                    prep_tile(c + 1, nfm_c // 4)
                if c + 1 < NCH and nfm_c in (3, 7, 11, 15):
                    prep_tile_b(c + 1, (nfm_c - 3) // 4)
                nfm_c += 1
                pr = nfm % 3
                nfm += 1

                def mm(E, c0=c0, nr=nr, pr=pr, h=h):
                    res = []
                    for k in range(8):
                        res.append(E.matmul(pp[pr][0:nr, :], lhsT=win[:, k, c0:c0 + nr], rhs=h[:, k, :],
                                            start=(k == 0), stop=(k == 7)))
                    return res
                sch.op('pe', mm, ['win', hn], ['pp%d' % pr])
                st = stF[pr]
                sn = 'stF%d' % pr
                if t is None:
                    dap = dst[0:nr, c * 512:(c + 1) * 512]
                else:
                    dap = dst[t, 0:nr, c * 512:(c + 1) * 512]
                if gi is None:
                    sch.op('act', lambda E, nr=nr, pr=pr, st=st: E.activation(out=st[0:nr, :], in_=pp[pr][0:nr, :],
                                                                              func=AF.Copy),
                           ['pp%d' % pr], [sn])

                    def e2(dap=dap, st=st, sn=sn, nr=nr):
                        sch.dma('pool', dap, st[0:nr, :], reads=[sn], writes=[], chan='st_' + sn)
                else:
                    q = nfm % 3
                    sch.op('act', lambda E, pr=pr, q=q: E.activation(out=qsb[q][:], in_=pp[pr][:], func=AF.Copy),
                           ['pp%d' % pr], ['qsb%d' % q])
                    sch.op('dve', lambda E, q=q: E.tensor_tensor(out=sq[q][:], in0=qsb[q][:], in1=qsb[q][:],
                                                                  op=ALU.mult), ['qsb%d' % q], ['sq%d' % q])
                    ps_ = npss[0] % 2
                    npss[0] += 1
                    sch.op('pe', lambda E, q=q, ps_=ps_: E.matmul(pss[ps_][:], lhsT=blk1b[:], rhs=sq[q][:], start=True,
                                                                  stop=True),
                           ['sq%d' % q, 'blk1b'], ['pss%d' % ps_])

                    def e2(dap=dap, st=st, sn=sn, nr=nr, q=q, gi=gi, ps_=ps_):
                        sch.op('act', lambda E: E.activation(out=lr[q][:], in_=pss[ps_][:], func=AF.Ln, scale=1.0 / 64,
                                                             bias=L['eps_c'][:]), ['pss%d' % ps_], ['lr%d' % q])
                        sch.op('act', lambda E: E.activation(out=lr[q][:], in_=lr[q][:], func=AF.Exp, scale=-0.5),
                               ['lr%d' % q], ['lr%d' % q])
                        sch.op('dve', lambda E: E.scalar_tensor_tensor(
                            out=st[:], in0=qsb[q][:], scalar=qkg_sb[:, gi:gi + 1], in1=lr[q][:], op0=ALU.mult,
                            op1=ALU.mult), ['qsb%d' % q, 'lr%d' % q, 'qkg'], [sn])
                        sch.dma('pool', dap, st[0:nr, :], reads=[sn], writes=[], chan='st_' + sn)
                if pend[0] is not None:
                    pend[0]()
                pend[0] = e2
            if pend[0] is not None:
                pend[0]()
                pend[0] = None

            for tl in range(4):
                tt = c * 4 + tl
                pr = nfm % 3
                nfm += 1
                v = ntok % 2
                ntok += 1

                def mmv(E, pr=pr, tl=tl, h=h):
                    res = []
                    for k in range(8):
                        res.append(E.matmul(pp[pr][:, :], lhsT=h[:, k, tl * 128:(tl + 1) * 128],
                                            rhs=win[:, k, 1024:1536], start=(k == 0), stop=(k == 7)))
                    return res
                sch.op('pe', mmv, ['win', hn], ['pp%d' % pr])
                sch.op('act', lambda E, pr=pr, v=v: E.activation(out=stV[v][:], in_=pp[pr][:], func=AF.Copy),
                       ['pp%d' % pr], ['stV%d' % v])
                sch.dma('pool', L['Va'][tt * 128:(tt + 1) * 128, :], stV[v][:], reads=['stV%d' % v], writes=[],
                        chan='st_stV%d' % v)

                def mmb(E, tl=tl, h=h):
                    res = []
                    for k in range(8):
                        res.append(E.matmul(pB[:, 0:128], lhsT=h[:, k, tl * 128:(tl + 1) * 128],
                                            rhs=win[:, k, 2176:2304], start=(k == 0), stop=(k == 7)))
                    for k in range(8):
                        res.append(E.matmul(pB[:, 128:136], lhsT=h[:, k, tl * 128:(tl + 1) * 128],
                                            rhs=win[:, k, 2880:2888], start=(k == 0), stop=(k == 7)))
                    return res
                sch.op('pe', mmb, ['win', hn], ['pB'])
                sch.op('dve', lambda E, v=v: E.tensor_copy(out=stB[v][:], in_=pB[:, 0:128]), ['pB'], ['stB%d' % v])
                sch.op('dve', lambda E, v=v: E.tensor_scalar(out=stW[v][:], in0=pB[:, 128:136], scalar1=512.0 ** -0.5,
                                                             scalar2=None, op0=ALU.mult), ['pB'], ['stW%d' % v])
                sch.dma('pool', L['Vb'][tt * 128:(tt + 1) * 128, :], stB[v][:], reads=['stB%d' % v], writes=[],
                        chan='st_stB%d' % v)
                sch.dma('pool', L['iwS'][tt * 128:(tt + 1) * 128, :], stW[v][:], reads=['stW%d' % v], writes=[],
                        chan='st_stW%d' % v)
        sch.barrier()


def phase_attn_a(nc, sch, al, L):
    S, NT, NCH, b = L['S'], L['NT'], L['NCH'], L['b']
    identb, DThi, b31c = L['identb'], L['DThi'], L['b31c']
    neg_lam, subln_row, eps_c = L['neg_lam'], L['subln_row'], L['eps_c']
    QaT, KaT, Va, oS = L['QaT'], L['KaT'], L['Va'], L['oS']
    with ExitStack() as es:
        Ka = al.sb(es, [128, 4, S], BF16, "Ka")
        Vs = al.sb(es, [128, NT, 4, 130], BF16, "Vs")
        sch.op('pool', lambda E: E.memset(Vs[:], 1.0), [], ['Vs'])
        for t in range(4):
            sch.dma('sp', Ka[:, t, :], KaT[t, :, :], writes=['Ka%d' % t], chan='ldK')
        for j0 in range(NT):
            sch.dma('sp' if j0 % 2 == 0 else 'pool', Vs[:, j0, :, 0:128],
                    Va[j0 * 128:(j0 + 1) * 128, :].rearrange("p (h e) -> p h e", h=4),
                    reads=['Vs'], writes=['Vs%d' % j0], chan='ldV%d' % (j0 % 2))
        sch.barrier()
        Qc = [al.sb(es, [128, 512], BF16, "Qc%d" % i) for i in range(2)]
        pt = [al.sb(es, [128, 2, 512], BF16, "pt%d" % i) for i in range(3)]
        ost = [al.sb(es, [128, 4, 128], BF16, "ost%d" % i) for i in range(2)]
        rr = [al.sb(es, [128, 4], F32, "rr%d" % i) for i in range(2)]
        t0 = [al.sb(es, [128, 128], F32, "t0%d" % i) for i in range(2)]
        dd = [al.sb(es, [128, 128], F32, "dd%d" % i) for i in range(2)]
        junk = al.sb(es, [128, 128], F32, "junkA")
        accs = [al.sb(es, [128, 2, 2, 2, 129], F32, "accs%d" % i) for i in range(2)]
        st = [al.ps(es, [128, 2, 512], F32, "st%d" % i) for i in range(2)]
        acc = [[al.ps(es, [128, 2, 256], F32, "acc%d%d" % (m, q)) for q in range(2)] for m in range(2)]
        nst = 0
        nq = 0
        nep = [0]
        blocks = []
        for h in range(4):
            for c in range(NCH):
                cq = nq % 2
                nq += 1
                nj = 4 * c + 4
                oq = (h * NCH + c) % 2

                def pre(cq=cq, h=h, c=c):
                    sch.dma('sp', Qc[cq][:], QaT[h, :, c * 512:(c + 1) * 512], writes=['Qc%d' % cq], chan='Qc%d' % cq)

                def epi(h=h, c=c, oq=oq):
                    aq = nep[0] % 2
                    nep[0] += 1
                    A_ = accs[aq]
                    an = 'accs%d' % aq
                    for m in range(2):
                        for q_ in range(2):
                            sch.op('dve', lambda E, m=m, q_=q_: E.tensor_copy(out=A_[:, m, q_, :, :],
                                                                              in_=acc[m][q_][:, :, 0:129]),
                                   [('acc', m, q_)], [an])

                    def rest(h=h, c=c, oq=oq, A_=A_, an=an):
                        epi_rest(h, c, oq, A_, an)
                    return rest

                def epi_rest(h, c, oq, A_, an):
                    for li in range(4):
                        e = li % 2
                        a0 = A_[:, 0, li // 2, li % 2, :]
                        a1 = A_[:, 1, li // 2, li % 2, :]
                        sch.op('dve', lambda E, e=e, a0=a0: E.reciprocal(out=rr[e][:, 0:1], in_=a0[:, 128:129]),
                               [an], ['rr%d' % e])
                        sch.op('dve', lambda E, e=e, a1=a1: E.reciprocal(out=rr[e][:, 1:2], in_=a1[:, 128:129]),
                               [an], ['rr%d' % e])
                        sch.op('dve', lambda E, e=e: E.tensor_tensor(out=rr[e][:, 1:2], in0=rr[e][:, 1:2], in1=neg_lam[:],
                                                                     op=ALU.mult), ['rr%d' % e], ['rr%d' % e])
                        sch.op('dve', lambda E, e=e, a0=a0: E.tensor_scalar(out=t0[e][:], in0=a0[:, 0:128],
                                                                            scalar1=rr[e][:, 0:1], scalar2=None,
                                                                            op0=ALU.mult),
                               [an, 'rr%d' % e], ['t0%d' % e])
                        sch.op('dve', lambda E, e=e, a1=a1: E.scalar_tensor_tensor(
                            out=dd[e][:], in0=a1[:, 0:128], scalar=rr[e][:, 1:2], in1=t0[e][:], op0=ALU.mult,
                            op1=ALU.add), [an, 'rr%d' % e, 't0%d' % e], ['dd%d' % e])
                        sch.op('dve', lambda E, e=e: E.scalar_tensor_tensor(
                            out=junk[:], in0=dd[e][:], scalar=1.0, in1=dd[e][:], op0=ALU.mult, op1=ALU.mult,
                            accum_out=rr[e][:, 2:3]), ['dd%d' % e], ['junkA', 'rs%d' % e])
                        sch.op('act', lambda E, e=e: E.activation(out=rr[e][:, 3:4], in_=rr[e][:, 2:3], func=AF.Ln,
                                                                  scale=1.0 / 128, bias=eps_c[:]), ['rs%d' % e], ['rt%d' % e])
                        sch.op('act', lambda E, e=e: E.activation(out=rr[e][:, 3:4], in_=rr[e][:, 3:4], func=AF.Exp,
                                                                  scale=-0.5), ['rt%d' % e], ['rt%d' % e])
                        sch.op('dve', lambda E, e=e, li=li, oq=oq: E.scalar_tensor_tensor(
                            out=ost[oq][:, li, :], in0=dd[e][:], scalar=rr[e][:, 3:4], in1=subln_row[:], op0=ALU.mult,
                            op1=ALU.mult), ['dd%d' % e, 'rt%d' % e], ['ost%d' % oq])
                    sch.dma('pool', oS[c * 512:(c + 1) * 512, h * 128:(h + 1) * 128].rearrange("(li p) e -> p li e", p=128),
                            ost[oq][:], reads=['ost%d' % oq], writes=[], chan='st_ost%d' % oq)

                for j in range(nj):
                    lo = max(0, j - 4 * c)
                    off = lo * 128
                    segs = [(j + dl - 4 * c, dl) for dl in (0, 1) if 0 <= j + dl - 4 * c <= 3]
                    r = nst % 2
                    p_ = nst % 3
                    nst += 1

                    def qk(E, r=r, j=j, off=off, segs=segs, cq=cq, h=h):
                        res = [E.matmul(st[r][:, m, off:512], lhsT=Ka[64 * m:64 * m + 64, h, j * 128:(j + 1) * 128],
                                        rhs=Qc[cq][64 * m:64 * m + 64, off:512], start=True, stop=(not segs))
                               for m in range(2)]
                        if segs:
                            c0 = segs[0][0] * 128
                            n = 128 * len(segs)
                            d0 = segs[0][1] * 128
                            for m in range(2):
                                for T in (DThi,):
                                    res.append(E.matmul(st[r][:, m, c0:c0 + n], lhsT=identb[:], rhs=T[:, h, d0:d0 + n],
                                                        start=False, stop=True))
                        return res

                    def pv(E, p_=p_, j=j, lo=lo, c=c, h=h):
                        res = []
                        for m in range(2):
                            for li in range(lo, 4):
                                res.append(E.matmul(acc[m][li // 2][:, li % 2, 0:129],
                                                    lhsT=pt[p_][:, m, li * 128:(li + 1) * 128],
                                                    rhs=Vs[:, j, h, 0:129], start=(j == 0 and li % 2 == 0),
                                                    stop=(j == 4 * c + li), skip_group_check=True))
                        return res
                    s0 = lambda qk=qk, cq=cq, r=r: sch.op('pe', qk, ['Qc%d' % cq], ['st%d' % r])
                    s1 = lambda r=r, p_=p_, off=off, h=h: sch.op('act', lambda E: E.activation(
                        out=pt[p_][:, :, off:512], in_=st[r][:, :, off:512], func=AF.Exp, bias=b31c[:, h:h + 1]),
                        ['st%d' % r], ['pt%d' % p_])
                    s2 = lambda pv=pv, p_=p_, lo=lo: sch.op(
                        'pe', pv, ['pt%d' % p_], sorted(set(('acc', m, li // 2) for m in range(2) for li in range(lo, 4))))
                    blk = Blk(s0, s1, s2, pre=(pre if j == 0 else None))
                    if j == nj - 1:
                        def post(blk=blk, epi=epi):
                            blk.post2 = epi()
                        blk.post = post
                    blocks.append(blk)
        run_pipe(blocks, 1)
        sch.barrier()


def phase_attn_b(nc, sch, al, L):
    S, NT, NCH, b = L['S'], L['NT'], L['NCH'], L['b']
    TK = min(TOPK, S // 4)
    identb, identf_sb, DThi, b31c = L['identb'], L['identf_sb'], L['DThi'], L['b31c']
    negtri, thr_const, fvec = L['negtri'], L['thr_const'], L['fvec']
    QbT, KbT, Vb, iqT, ikT, iwS, oS = L['QbT'], L['KbT'], L['Vb'], L['iqT'], L['ikT'], L['iwS'], L['oS']
    with ExitStack() as es:
        Kb = al.sb(es, [128, 2, S], BF16, "Kb")
        Vs = al.sb(es, [128, NT, 2, 66], BF16, "VsB")
        ik2 = al.sb(es, [128, S], BF16, "ik2")
        iw = al.sb(es, [128, NT, 8], F32, "iw")
        sch.op('pool', lambda E: E.memset(Vs[:], 1.0), [], ['VsB'])
        for half in range(2):
            for g in range(2):
                sch.dma('sp', Kb[64 * half:64 * half + 64, g, :], KbT[64 * g:64 * g + 64, :],
                        writes=['Kb%d%d' % (half, g)], chan='ldK')
            sch.dma('sp', ik2[64 * half:64 * half + 64, :], ikT[0:64, :], writes=['ik2%d' % half], chan='ldK')
        for j0 in range(NT):
            sch.dma('pool', iw[:, j0, :], iwS[j0 * 128:(j0 + 1) * 128, :], writes=['iw%d' % j0], chan='ldW')
        for j0 in range(NT):
            sch.dma('sp' if j0 % 2 == 0 else 'pool', Vs[:, j0, :, 0:64],
                    Vb[j0 * 128:(j0 + 1) * 128, :].rearrange("p (g e) -> p g e", g=2),
                    reads=['VsB'], writes=['VsB%d' % j0], chan='ldV%d' % (j0 % 2))
        sch.barrier()
        NI = 4
        Ib = [al.sb(es, [128, S], F32, "I%d" % i) for i in range(NI)]
        ma = [al.sb(es, [128, S], BF16, "ma%d" % i) for i in range(NI)]
        maT = al.sb(es, [128, NT, 512], BF16, "maT")
        Rb = [al.sb(es, [128, 2, 512], BF16, "R%d" % i) for i in range(2)]
        dg = [al.sb(es, [128, 8, 128], BF16, "dg%d" % i) for i in range(2)]
        iqc = [al.sb(es, [128, 4, 128], BF16, "iqc%d" % i) for i in range(2)]
        Qc = [al.sb(es, [128, 4, 512], BF16, "QcB%d" % i) for i in range(2)]
        pt = [al.sb(es, [128, 2, 512], BF16, "ptB%d" % i) for i in range(3)]
        ost = [al.sb(es, [128, 4, 512], BF16, "ostB%d" % i) for i in range(1)]
        bs = [al.sb(es, [128, 8], F32, "bs%d" % i) for i in range(NI)]
        rcp = al.sb(es, [128, 2, 4], F32, "rcp")
        accbs = [al.sb(es, [128, 2, 4, 65], F32, "accbs%d" % i) for i in range(2)]
        bw = [al.sb(es, [128, NIT], F32, "bw%d" % i) for i in range(NI)]
        PP = [al.ps(es, [128, 2, 512], F32, "PP%d" % i) for i in range(2)]
        pacc = al.ps(es, [128, 512], F32, "pacc")
        tpb = al.ps(es, [128, 8, 128], BF16, "tpb")
        accb = al.ps(es, [128, 2, 4, 128], F32, "accb")
        cnt = {'x': 0, 'R': 0, 'pt': 0, 'sb': 0, 'acc': 0}

        def idx_chunk(c):
            blocks = []
            pending = []
            for li in range(4):
                i = 4 * c + li
                ib = i % NI
                q = i % 2
                Li = 128 * (i + 1)

                def pre(i=i, q=q):
                    sch.dma('sp', iqc[q][:], iqT[:, :, i * 128:(i + 1) * 128].rearrange("t p n -> p t n"),
                            writes=['iqc%d' % q], chan='iqc%d' % q)
                    for hh in range(8):
                        sch.op('pool', lambda E, hh=hh: E.tensor_scalar(
                            out=dg[q][:, hh, :], in0=identf_sb[:], scalar1=iw[:, i, hh:hh + 1], scalar2=None,
                            op0=ALU.mult), [], ['dg%d' % q])

                def gen_tile(i=i, ib=ib, Li=Li):
                    sch.op('dve', lambda E: E.tensor_tensor(
                        out=Ib[ib][:, i * 128:(i + 1) * 128], in0=Ib[ib][:, i * 128:(i + 1) * 128], in1=negtri[:],
                        op=ALU.add), ['I%d' % ib], ['I%d' % ib])
                    B = bs[ib]
                    bn = 'bs%d' % ib
                    if i >= TK // 128:
                        W = bw[ib]
                        sch.op('dve', lambda E: E.tensor_reduce(
                            out=B[:, 0:1], in_=Ib[ib][:, 0:TK], axis=mybir.AxisListType.X, op=ALU.min),
                            ['I%d' % ib], [bn])
                        sch.op('dve', lambda E: E.tensor_reduce(
                            out=B[:, 1:2], in_=Ib[ib][:, 0:Li], axis=mybir.AxisListType.X, op=ALU.max),
                            ['I%d' % ib], [bn])
                        yield
                        sch.op('dve', lambda E: E.tensor_tensor(out=B[:, 1:2], in0=B[:, 1:2], in1=B[:, 0:1],
                                                                op=ALU.subtract), [bn], [bn])
                        sch.op('dve', lambda E: E.tensor_scalar(out=W[:], in0=fvec[:], scalar1=B[:, 1:2], scalar2=None,
                                                                op0=ALU.mult), [bn], [bn + 'w'])
                        sch.op('dve', lambda E: E.tensor_tensor(out=B[:, 2:3], in0=B[:, 0:1], in1=W[:, 0:1], op=ALU.add),
                               [bn, bn + 'w'], [bn])
                        yield
                        for k in range(NIT):
                            sch.op('dve', lambda E: E.tensor_scalar(
                                out=ma[ib][:, 0:Li], in0=Ib[ib][:, 0:Li], scalar1=B[:, 2:3], scalar2=None, op0=ALU.is_ge,
                                op1=ALU.add, accum_out=B[:, 3:4]), [bn, 'I%d' % ib], [bn, 'ma%d' % ib])
                            yield
                            last = (k == NIT - 1)
                            sch.op('dve', lambda E, last=last: E.tensor_scalar(
                                out=B[:, 4:5], in0=B[:, 3:4], scalar1=float(TK), scalar2=(-1.0 if last else -0.5),
                                op0=ALU.is_ge, op1=ALU.add), [bn], [bn])
                            yield
                            sch.op('dve', lambda E, k=k, last=last: E.scalar_tensor_tensor(
                                out=(B[:, 5:6] if last else B[:, 2:3]), in0=B[:, 4:5], scalar=W[:, k:k + 1], in1=B[:, 2:3],
                                op0=ALU.mult, op1=ALU.add), [bn, bn + 'w'], [bn])
                            yield
                        thr = B[:, 5:6]
                    else:
                        thr = thr_const[:, 0:1]
                    sch.op('dve', lambda E: E.tensor_scalar(
                        out=ma[ib][:, 0:Li], in0=Ib[ib][:, 0:Li], scalar1=thr, scalar2=NEG, op0=ALU.is_lt, op1=ALU.mult),
                        [bn, 'I%d' % ib], ['ma%d' % ib])

                def post_tile(li=li, gen_tile=gen_tile):
                    pending.append(gen_tile())
                    if li % 2 == 1:
                        gens = list(pending)
                        del pending[:]
                        while gens:
                            for g_ in list(gens):
                                try:
                                    next(g_)
                                except StopIteration:
                                    gens.remove(g_)

                nsc = (Li + 511) // 512
                for si, sc0 in enumerate(range(0, Li, 512)):
                    w = min(512, Li - sc0)
                    for t in range(4):
                        xr = cnt['x'] % 2
                        cnt['x'] += 1
                        rq = cnt['R'] % 2
                        cnt['R'] += 1
                        s0 = lambda xr=xr, q=q, t=t, sc0=sc0, w=w: sch.op('pe', lambda E: [E.matmul(
                            PP[xr][:, e, 0:w], lhsT=iqc[q][64 * e:64 * e + 64, t, :], rhs=ik2[64 * e:64 * e + 64, sc0:sc0 + w],
                            start=True, stop=True) for e in range(2)], ['iqc%d' % q], ['PP%d' % xr])
                        s1 = lambda xr=xr, rq=rq, w=w: sch.op('act', lambda E: E.activation(
                            out=Rb[rq][:, :, 0:w], in_=PP[xr][:, :, 0:w], func=AF.Relu), ['PP%d' % xr], ['R%d' % rq])
                        s2 = lambda rq=rq, q=q, t=t, w=w: sch.op('pe', lambda E: [E.matmul(
                            pacc[:, 0:w], lhsT=dg[q][:, 2 * t + e, :], rhs=Rb[rq][:, e, 0:w], start=(t == 0 and e == 0),
                            stop=(t == 3 and e == 1)) for e in range(2)], ['R%d' % rq, 'dg%d' % q], ['pacc'])
                        post = None
                        if t == 3:
                            last = (si == nsc - 1)

                            def post(ib=ib, sc0=sc0, w=w, last=last, post_tile=post_tile):
                                sch.op('act', lambda E: E.activation(out=Ib[ib][:, sc0:sc0 + w], in_=pacc[:, 0:w],
                                                                     func=AF.Copy), ['pacc'], ['I%d' % ib])
                                if last:
                                    post_tile()
                        blocks.append(Blk(s0, s1, s2, pre=(pre if (si == 0 and t == 0) else None), post=post))
            run_pipe(blocks, 1)

        def tr_chunk(c):
            for li in range(4):
                i = 4 * c + li
                ib = i % NI
                for j0 in range(0, i + 1, 8):
                    n = min(8, i + 1 - j0)

                    def tr(E, ib=ib, j0=j0, n=n):
                        return [E.transpose(out=tpb[:, jj, :], in_=ma[ib][:, (j0 + jj) * 128:(j0 + jj + 1) * 128],
                                            identity=identb[:]) for jj in range(n)]
                    sch.op('pe', tr, ['ma%d' % ib], ['tpb'])
                    sch.op('act', lambda E, j0=j0, n=n, li=li: E.activation(
                        out=maT[:, j0:j0 + n, li * 128:(li + 1) * 128], in_=tpb[:, 0:n, :], func=AF.Copy), ['tpb'], ['maT'])

        def attn_chunk(c):
            cq = c % 2
            blocks = []
            nj = 4 * c + 4

            def pre():
                sch.dma('sp', Qc[cq][:], QbT[:, :, c * 512:(c + 1) * 512].rearrange("t p n -> p t n"),
                        writes=['QcB%d' % cq], chan='QcB%d' % cq)
            for t in range(4):
                g = t // 2

                def epi(t=t):
                    aq = cnt['acc'] % 2
                    cnt['acc'] += 1
                    A_ = accbs[aq]
                    an = 'accbs%d' % aq
                    sch.op('act', lambda E: E.activation(out=A_[:], in_=accb[:, :, :, 0:65], func=AF.Copy),
                           ['accb'], [an])

                    def rest(t=t, A_=A_, an=an):
                        sch.op('act', lambda E: E.activation(out=rcp[:], in_=A_[:, :, :, 64], func=AF.Ln),
                               [an], ['rcp'])
                        sch.op('act', lambda E: E.activation(out=rcp[:], in_=rcp[:], func=AF.Exp, scale=-1.0),
                               ['rcp'], ['rcp'])
                        sch.op('act', lambda E: [E.activation(
                            out=ost[0][:, li, (2 * t + e) * 64:(2 * t + e + 1) * 64], in_=A_[:, e, li, 0:64], func=AF.Copy,
                            scale=rcp[:, e, li:li + 1]) for e in range(2) for li in range(4)], [an, 'rcp'], ['ostB0'])
                        if t == 3:
                            sch.dma('pool', oS[c * 512:(c + 1) * 512, 512:1024].rearrange("(li p) e -> p li e", p=128),
                                    ost[0][:], reads=['ostB0'], writes=[], chan='st_ostB0')
                    return rest
                for j in range(nj):
                    lo = max(0, j - 4 * c)
                    off = lo * 128
                    segs = [(j + dl - 4 * c, dl) for dl in (0, 1) if 0 <= j + dl - 4 * c <= 3]
                    r = cnt['x'] % 2
                    cnt['x'] += 1
                    p_ = cnt['pt'] % 3
                    cnt['pt'] += 1

                    def qk(E, r=r, j=j, off=off, segs=segs, t=t, g=g):
                        res = [E.matmul(PP[r][:, e, off:512], lhsT=Kb[64 * e:64 * e + 64, g, j * 128:(j + 1) * 128],
                                        rhs=Qc[cq][64 * e:64 * e + 64, t, off:512], start=True, stop=False)
                               for e in range(2)]
                        for e in range(2):
                            res.append(E.matmul(PP[r][:, e, off:512], lhsT=identb[:], rhs=maT[:, j, off:512], start=False,
                                                stop=(not segs)))
                        if segs:
                            c0 = segs[0][0] * 128
                            n = 128 * len(segs)
                            d0 = segs[0][1] * 128
                            for e in range(2):
                                for T in (DThi,):
                                    res.append(E.matmul(PP[r][:, e, c0:c0 + n], lhsT=identb[:],
                                                        rhs=T[:, 4 + 2 * t + e, d0:d0 + n], start=False, stop=True))
                        return res

                    def pv(E, p_=p_, j=j, lo=lo, g=g):
                        return [E.matmul(accb[:, e, li, 0:65], lhsT=pt[p_][:, e, li * 128:(li + 1) * 128],
                                         rhs=Vs[:, j, g, 0:65], start=(j == 0 and li == 0), stop=(j == 4 * c + li),
                                         skip_group_check=True) for e in range(2) for li in range(lo, 4)]
                    s0 = lambda qk=qk, r=r: sch.op('pe', qk, ['QcB%d' % cq, 'maT'], ['PP%d' % r])
                    s1 = lambda r=r, p_=p_, off=off, t=t: sch.op('act', lambda E: [E.activation(
                        out=pt[p_][:, e, off:512], in_=PP[r][:, e, off:512], func=AF.Exp,
                        bias=b31c[:, 4 + 2 * t + e:5 + 2 * t + e]) for e in range(2)], ['PP%d' % r], ['ptB%d' % p_])
                    s2 = lambda pv=pv, p_=p_: sch.op('pe', pv, ['ptB%d' % p_], ['accb'])
                    blk = Blk(s0, s1, s2, pre=(pre if (t == 0 and j == 0) else None))
                    if j == nj - 1:
                        def post(blk=blk, epi=epi):
                            blk.post2 = epi()
                        blk.post = post
                    blocks.append(blk)
            run_pipe(blocks, 1)

        idx_chunk(0)
        tr_chunk(0)
        for c in range(NCH):
            if c + 1 < NCH:
                idx_chunk(c + 1)
            attn_chunk(c)
            if c + 1 < NCH:
                tr_chunk(c + 1)
        sch.barrier()


def phase_f(nc, sch, al, L):
    with ExitStack() as es:
        wu = al.sb(es, [128, 8, 2 * DFF], BF16, "wu")
        wd = al.sb(es, [128, 22, D], BF16, "wd")
        phase_f1(nc, sch, al, L, wu, wd)
        phase_f2(nc, sch, al, L, wu, wd)


def phase_f1(nc, sch, al, L, wu, wd):
    S, NT, NCH, b = L['S'], L['NT'], L['NCH'], L['b']
    w_up, w_down = L['w_up'], L['w_down']
    identb, eps_c, a_ffn, modc = L['identb'], L['eps_c'], L['a_ffn'], L['modc']
    x, out, oS, x1nT, w_out, modrow = L['x'], L['out'], L['oS'], L['x1nT'], L['w_out'], L['modrow']
    with ExitStack() as es:
        wo = al.sb(es, [128, 8, D], BF16, "wo")
        with ExitStack() as e2:
            grow = al.sb(e2, [128, D], F32, "grow")
            stg = [al.sb(e2, [128, D], F32, "wos%d" % i) for i in range(2)]
            sch.dma('sp', grow[:], modrow[b:b + 1, 2048:3072].partition_broadcast(128), writes=['grow'], chan='grow')
            for k in range(8):
                sch.dma('sp', stg[k % 2][:], w_out[k * 128:(k + 1) * 128, :], writes=['wos%d' % (k % 2)],
                        chan='wos%d' % (k % 2))
                sch.op('dve', lambda E, k=k: E.tensor_tensor(out=wo[:, k, :], in0=stg[k % 2][:], in1=grow[:], op=ALU.mult),
                       ['wos%d' % (k % 2), 'grow'], ['wo'])
            sch.barrier()
        ot = [al.sb(es, [128, D], BF16, "ot%d" % i) for i in range(2)]
        oT = [al.sb(es, [128, 8, 128], BF16, "oT%d" % i) for i in range(2)]
        xt = [al.sb(es, [128, D], F32, "xtf%d" % i) for i in range(2)]
        x1 = [al.sb(es, [128, D], F32, "x1%d" % i) for i in range(2)]
        x1n = [al.sb(es, [128, D], BF16, "x1n%d" % i) for i in range(2)]
        x1s = [al.sb(es, [128, 8, 128], BF16, "x1s%d" % i) for i in range(2)]
        junk = al.sb(es, [128, D], BF16, "junkF")
        ss = [al.sb(es, [128, 2], F32, "ssf%d" % i) for i in range(2)]
        tpo = [al.ps(es, [128, 8, 128], BF16, "tpo%d" % i) for i in range(2)]
        tpn = [al.ps(es, [128, 8, 128], BF16, "tpn%d" % i) for i in range(2)]
        pso = [[al.ps(es, [128, 512], F32, "pso%d%d" % (i, hf)) for hf in range(2)] for i in range(2)]
        growf = al.sb(es, [128, D], F32, "growf")
        pstg = [al.sb(es, [128, 1408], F32, "pstg%d" % i) for i in range(2)]
        sch.dma('sp', growf[:], modrow[b:b + 1, 5120:6144].partition_broadcast(128), writes=['growf'], chan='growf')
        steps = []
        for k in range(8):
            for q4 in range(4):
                steps.append(('cast', w_up[k * 128:(k + 1) * 128, q4 * 1408:(q4 + 1) * 1408],
                              wu[:, k, q4 * 1408:(q4 + 1) * 1408]))
        for ct in range(22):
            steps.append(('mul', w_down[ct * 128:(ct + 1) * 128, :], wd[:, ct, :]))
        nstep = [0]

        def prep_step():
            if nstep[0] >= len(steps):
                return
            kind, src, dst = steps[nstep[0]]
            q = nstep[0] % 2
            n_ = nstep[0]
            nstep[0] += 1
            if kind == 'cast':
                sch.dma('sp', pstg[q][:], src, writes=['pstg%d' % q], chan='pstg%d' % q)
                sch.op('act', lambda E: E.activation(out=dst, in_=pstg[q][:], func=AF.Copy), ['pstg%d' % q],
                       ['wprep%d' % n_])
            else:
                sch.dma('sp', pstg[q][:, 0:D], src, writes=['pstg%d' % q], chan='pstg%d' % q)
                sch.op('dve', lambda E: E.tensor_tensor(out=dst, in0=pstg[q][:, 0:D], in1=growf[:], op=ALU.mult),
                       ['pstg%d' % q, 'growf'], ['wprep%d' % n_])
        def part_a(tt):
            r = tt % 2
            rows = slice(tt * 128, (tt + 1) * 128)
            sch.dma('sp', ot[r][:], oS[rows, :], writes=['ot%d' % r], chan='ot%d' % r)
            sch.dma('sp', xt[r][:], x[b, rows, :], writes=['xtf%d' % r], chan='xtf%d' % r)
            sch.op('pe', lambda E, r=r: [E.transpose(out=tpo[r][:, k, :], in_=ot[r][:, k * 128:(k + 1) * 128],
                                                     identity=identb[:]) for k in range(8)],
                   ['ot%d' % r], ['tpo%d' % r])
            sch.op('act', lambda E, r=r: E.activation(out=oT[r][:], in_=tpo[r][:], func=AF.Copy),
                   ['tpo%d' % r], ['oT%d' % r])
            for hf in range(2):
                sch.op('pe', lambda E, r=r, hf=hf: [E.matmul(pso[r][hf][:], lhsT=oT[r][:, k, :],
                                                             rhs=wo[:, k, hf * 512:(hf + 1) * 512],
                                                             start=(k == 0), stop=(k == 7)) for k in range(8)],
                       ['oT%d' % r], ['pso%d%d' % (r, hf)])

        def part_a2(tt):
            r = tt % 2
            rows = slice(tt * 128, (tt + 1) * 128)
            for hf in range(2):
                sch.op('dve', lambda E, r=r, hf=hf: E.tensor_tensor(
                    out=x1[r][:, hf * 512:(hf + 1) * 512], in0=pso[r][hf][:], in1=xt[r][:, hf * 512:(hf + 1) * 512],
                    op=ALU.add), ['pso%d%d' % (r, hf), 'xtf%d' % r], ['x1%d' % r])
            sch.dma('pool', out[b, rows, :], x1[r][:], reads=['x1%d' % r], writes=[], chan='st_x1%d' % r)
            sch.op('dve', lambda E, r=r: E.scalar_tensor_tensor(
                out=junk[:], in0=x1[r][:], scalar=1.0, in1=x1[r][:], op0=ALU.mult, op1=ALU.mult,
                accum_out=ss[r][:, 0:1]), ['x1%d' % r], ['junkF', 'ssf%d' % r])
            sch.op('act', lambda E, r=r: E.activation(out=ss[r][:, 1:2], in_=ss[r][:, 0:1], func=AF.Ln, scale=1.0 / D,
                                                      bias=eps_c[:]), ['ssf%d' % r], ['ssg%d' % r])
            sch.op('act', lambda E, r=r: E.activation(out=ss[r][:, 1:2], in_=ss[r][:, 1:2], func=AF.Exp, scale=-0.5),
                   ['ssg%d' % r], ['ssg%d' % r])
            sch.op('dve', lambda E, r=r: E.tensor_scalar(out=x1n[r][:], in0=x1[r][:], scalar1=ss[r][:, 1:2],
                                                         scalar2=None, op0=ALU.mult),
                   ['x1%d' % r, 'ssg%d' % r], ['x1n%d' % r])

        def part_b(tt):
            r = tt % 2
            rows = slice(tt * 128, (tt + 1) * 128)
            sch.op('pe', lambda E, r=r: [E.transpose(out=tpn[r][:, k, :], in_=x1n[r][:, k * 128:(k + 1) * 128],
                                                     identity=identb[:]) for k in range(8)],
                   ['x1n%d' % r], ['tpn%d' % r])
            sch.op('act', lambda E, r=r: [E.activation(out=x1s[r][:, k, :], in_=tpn[r][:, k, :], func=AF.Identity,
                                                       scale=a_ffn[:, k:k + 1], bias=modc[:, 24 + k:25 + k])
                                          for k in range(8)], ['tpn%d' % r], ['x1s%d' % r])
            sch.dma('pool', x1nT[:, :, rows].rearrange("k p n -> p k n"), x1s[r][:], reads=['x1s%d' % r], writes=[],
                    chan='st_x1s%d' % r)

        for tt in range(NT + 2):
            if tt < NT:
                for _ in range((len(steps) + NT - 1) // NT):
                    prep_step()
                part_a(tt)
            if 1 <= tt <= NT:
                part_a2(tt - 1)
            if tt >= 2:
                part_b(tt - 2)
        sch.barrier()


def phase_f2(nc, sch, al, L, wu, wd):
    S, NT, NCH, b = L['S'], L['NT'], L['NCH'], L['b']
    cw, cb = L['cw_sb'], L['cb_sb']
    out, x1nT = L['out'], L['x1nT']
    with ExitStack() as es:
        gT = al.sb(es, [128, 22, 512], BF16, "gT")
        gpre = al.sb(es, [128, 2, 2, 512], BF16, "gpre")
        xc = [al.sb(es, [128, 8, 512], BF16, "xc%d" % i) for i in range(2)]
        u = [al.sb(es, [128, 514], F32, "u%d" % i) for i in range(2)]
        y = [al.sb(es, [128, 512], F32, "y%d" % i) for i in range(3)]
        sg = [al.sb(es, [128, 512], BF16, "sg%d" % i) for i in range(2)]
        x1t = [al.sb(es, [128, D], F32, "x1t%d" % i) for i in range(2)]
        hal = al.sb(es, [128, 44, 2], F32, "hal")
        pu = [al.ps(es, [128, 512], F32, "pu%d" % i) for i in range(4)]
        pd = [[al.ps(es, [128, 512], F32, "pd%d%d" % (i, hf)) for hf in range(2)] for i in range(2)]
        sch.op('pool', lambda E: E.memset(hal[:], 0.0), [], ['hal'])
        nu = 0
        ny = 0
        nt_ = 0
        def load_xc(c):
            sch.dma('sp', xc[c % 2][:], x1nT[:, :, c * 512:(c + 1) * 512].rearrange("k p n -> p k n"),
                    writes=['xc%d' % (c % 2)], chan='xc%d' % (c % 2))

        cntf = {'u': 0, 'y': 0}
        NPRE = 2

        def up_pair(c, ct):
            r = c % 2
            ys = []
            for hf in range(2):
                col = hf * 22 + ct
                p_ = cntf['u'] % 4
                uq = cntf['u'] % 2
                cntf['u'] += 1
                yq = cntf['y'] % 3
                cntf['y'] += 1
                ys.append(yq)
                sch.op('pe', lambda E, p_=p_, col=col, r=r: [E.matmul(
                    pu[p_][:], lhsT=wu[:, k, col * 128:(col + 1) * 128], rhs=xc[r][:, k, :], start=(k == 0),
                    stop=(k == 7)) for k in range(8)], ['xc%d' % r], ['pu%d' % p_])
                sch.op('act', lambda E, p_=p_, uq=uq: E.activation(out=u[uq][:, 2:514], in_=pu[p_][:], func=AF.Copy),
                       ['pu%d' % p_], ['u%d' % uq])
                sch.op('act', lambda E, p_=p_, yq=yq, col=col: E.activation(
                    out=y[yq][:], in_=pu[p_][:], func=AF.Identity, scale=cw[:, col, 2:3], bias=cb[:, col:col + 1]),
                    ['pu%d' % p_], ['y%d' % yq])
                sch.op('pool', lambda E, uq=uq, col=col: E.tensor_copy(out=u[uq][:, 0:2], in_=hal[:, col, :]),
                       ['hal'], ['u%d' % uq])
                sch.op('pool', lambda E, uq=uq, col=col: E.tensor_copy(out=hal[:, col, :], in_=u[uq][:, 512:514]),
                       ['u%d' % uq], ['hal'])
                for jj in (1, 0):
                    sch.op('dve', lambda E, uq=uq, yq=yq, col=col, jj=jj: E.scalar_tensor_tensor(
                        out=y[yq][:], in0=u[uq][:, jj:jj + 512], scalar=cw[:, col, jj:jj + 1], in1=y[yq][:],
                        op0=ALU.mult, op1=ALU.add), ['u%d' % uq, 'y%d' % yq], ['y%d' % yq])
            sq_ = ct % 2
            sch.op('act', lambda E, sq_=sq_, yg=ys[0]: E.activation(out=sg[sq_][:], in_=y[yg][:], func=AF.Silu),
                   ['y%d' % ys[0]], ['sg%d' % sq_])
            gdst = gpre[:, c % 2, ct, :] if ct < NPRE else gT[:, ct, :]
            gname = ('gpre', c % 2, ct) if ct < NPRE else ('gT', ct)
            sch.op('pool', lambda E, sq_=sq_, yv=ys[1], gdst=gdst: E.tensor_tensor(out=gdst, in0=sg[sq_][:],
                                                                                   in1=y[yv][:], op=ALU.mult),
                   ['sg%d' % sq_, 'y%d' % ys[1]], [gname])

        load_xc(0)
        for c in range(NCH):
            for ct in range(NPRE if c > 0 else 0, 22):
                up_pair(c, ct)
            if c + 1 < NCH:
                load_xc(c + 1)

            def load_x1t(li):
                rws = slice(c * 512 + li * 128, c * 512 + (li + 1) * 128)
                sch.dma('sp', x1t[li % 2][:], out[b, rws, :], writes=['x1t%d' % (li % 2)], chan='x1t%d' % (li % 2))
            load_x1t(0)
            load_x1t(1)
            if c + 1 < NCH:
                for ct in range(NPRE):
                    up_pair(c + 1, ct)
            for li in range(4):
                tq = li % 2
                rows = slice(c * 512 + li * 128, c * 512 + (li + 1) * 128)
                for hf in range(2):
                    sch.op('pe', lambda E, tq=tq, hf=hf, li=li, c=c: [E.matmul(
                        pd[tq][hf][:], lhsT=(gpre[:, c % 2, ct, li * 128:(li + 1) * 128] if ct < NPRE
                                             else gT[:, ct, li * 128:(li + 1) * 128]),
                        rhs=wd[:, ct, hf * 512:(hf + 1) * 512],
                        start=(ct == 0), stop=(ct == 21)) for ct in range(22)],
                        [(('gpre', c % 2, ct) if ct < NPRE else ('gT', ct)) for ct in range(22)],
                        ['pd%d%d' % (tq, hf)])
                    sch.op('dve', lambda E, tq=tq, hf=hf: E.tensor_tensor(
                        out=x1t[tq][:, hf * 512:(hf + 1) * 512], in0=pd[tq][hf][:], in1=x1t[tq][:, hf * 512:(hf + 1) * 512],
                        op=ALU.add), ['pd%d%d' % (tq, hf), 'x1t%d' % tq], ['x1t%d' % tq])
                sch.dma('sp', out[b, rows, :], x1t[tq][:], reads=['x1t%d' % tq], writes=[], chan='st_x1t%d' % tq)
                if li + 2 < 4:
                    load_x1t(li + 2)
        sch.barrier()


def t5_bucket_np(n):
    n = np.maximum(n, 0)
    nf = np.maximum(n, 1).astype(np.float32)
    large = 16 + (np.log(nf / np.float32(16)) / np.float32(math.log(128 / 16)) * np.float32(16)).astype(np.int32)
    large = np.minimum(large, 31)
    return np.where(n < 16, n, large)


def prep_shared(inp):
    f = lambda a: np.ascontiguousarray(a, dtype=np.float32)
    rb = np.asarray(inp['rel_bias'], np.float32)
    s_ = np.arange(128)[:, None]
    t_ = np.arange(256)[None, :]
    bk = t5_bucket_np(t_ - s_)
    biasT = rb[bk]
    cm = np.where(t_ >= s_, 0.0, NEG).astype(np.float32)
    pq = np.arange(128)
    sh = {
        'w_ada': f(inp['w_ada'][0]), 'b_ada': f(inp['b_ada']),
        'g_attn_c': f(np.asarray(inp['g_attn'][0]).reshape(8, 128).T),
        'g_ffn_c': f(np.asarray(inp['g_ffn'][0]).reshape(8, 128).T),
        'w_in': f(inp['w_in'][0]),
        'qkg': f(np.stack([np.tile(np.asarray(inp[k][0]), 2) for k in
                           ('q_norm_a', 'k_norm_a', 'q_norm_b', 'k_norm_b')], 1)),
        'lamv': f(np.asarray(inp['lam_vecs'][0]).reshape(1, 256)),
        'subln': f(np.asarray(inp['subln_a'])),
        'w_out': f(inp['w_out'][0]), 'w_up': f(inp['w_up'][0]),
        'conv_wc': f(np.asarray(inp['conv_w'][0]).T.reshape(44, 128, 3).transpose(1, 0, 2)),
        'conv_bc': f(np.asarray(inp['conv_b'][0]).reshape(44, 128).T),
        'w_down': f(inp['w_down'][0]),
        'biasT': f(biasT.transpose(0, 2, 1)),
        'b31': f(rb[31:32, :]),
        'cmask': cm,
        'identf': np.eye(128, dtype=np.float32),
        'blk1': f((pq[:, None] // 64) == (pq[None, :] // 64)),
    }
    return sh


def core_inputs(inp, sh, rows):
    m = dict(sh)
    xs = np.ascontiguousarray(np.asarray(inp['x'])[rows], dtype=np.float32)
    cs = np.asarray(inp['c'], np.float32)[rows]
    m['x'] = xs
    m['cT'] = np.ascontiguousarray(cs.reshape(len(rows), 8, 128).transpose(2, 1, 0))
    return m


_NC_CACHE = {}


def kernel(**inputs):
    B, S, _ = inputs['x'].shape
    ncores = 8
    NB = B // ncores
    key = (S, NB)
    if key not in _NC_CACHE:
        _NC_CACHE[key] = build(S, NB)
    nc = _NC_CACHE[key]
    sh = prep_shared(inputs)
    in_maps = [core_inputs(inputs, sh, list(range(i * NB, (i + 1) * NB))) for i in range(ncores)]
    res = run_bass_kernel_spmd(nc, in_maps, core_ids=list(range(ncores)))
    return np.concatenate([np.asarray(r['out']) for r in res.results], axis=0).astype(np.float32)
```
